# Optimizing a Trainium2 kernel written in Bass

```python
import numpy as np
import jax
import jax.numpy as jnp
from jax import lax

D_MODEL = 1024
BATCH = 8
SEQ = 8192
DEPTH = 2
DEC_BATCH = 8
DEC_SEQ = 16
PAST_LEN = 2048

CHUNK = 64
RET_HEADS = 4
RET_DK = 128
RET_DV = 256
RET_QK = RET_HEADS * RET_DK
RET_V = RET_HEADS * RET_DV
ROPE_BASE = 10000.0
HG_HEADS = 8
HG_DK = 128
HG_DV = D_MODEL // HG_HEADS
HG_K = HG_HEADS * HG_DK
HG_V = HG_HEADS * HG_DV
HG_MIN_F = 1e-6
RG_BLOCKS = 5
RG_BLOCK = 256
RG_WIDTH = RG_BLOCKS * RG_BLOCK
RG_CONV = 4
RG_C = 8.0
D_FF = 2816
FFN_CONV = 3
MIX_WIDTH = RET_V + HG_V + RG_WIDTH
IN_SPLITS = (RET_QK, RET_QK, RET_V, RET_V, HG_K, HG_K, HG_V, HG_V, RG_WIDTH, RG_WIDTH, D_MODEL, D_MODEL, D_MODEL)
IN_WIDTH = sum(IN_SPLITS)
EPS = 1e-6

kernel_name = 'hybrid_retention_hgrn2_rglru_streaming_step'


def _rmsnorm(x, w):
    xf = x.astype(jnp.float32)
    y = xf * lax.rsqrt(jnp.mean(xf * xf, axis=-1, keepdims=True) + EPS)
    return (y * w.astype(jnp.float32)).astype(x.dtype)


def _group_rms(o):
    return o * lax.rsqrt(jnp.mean(o * o, axis=-1, keepdims=True) + EPS)


def _split(a, sizes):
    idx = [int(i) for i in np.cumsum(sizes)[:-1]]
    return jnp.split(a, idx, axis=-1)


def _heads(a, h):
    b, t, _ = a.shape
    return a.reshape(b, t, h, -1).transpose(0, 2, 1, 3)


def _merge_heads(a):
    b, h, t, d = a.shape
    return a.transpose(0, 2, 1, 3).reshape(b, t, h * d)


def _rotary(x, pos0):
    t, d = x.shape[2], x.shape[3]
    half = d // 2
    inv = jnp.power(ROPE_BASE, -jnp.arange(half, dtype=jnp.float32) / half)
    ang = (jnp.arange(t, dtype=jnp.float32) + pos0)[:, None] * inv[None, :]
    cos, sin = jnp.cos(ang), jnp.sin(ang)
    x1, x2 = x[..., :half], x[..., half:]
    return jnp.concatenate([x1 * cos - x2 * sin, x1 * sin + x2 * cos], axis=-1)


def _causal_dwconv(u, buf, w, b):
    k = w.shape[0]
    t = u.shape[1]
    up = jnp.concatenate([buf.astype(u.dtype), u], axis=1)
    y = b
    for j in range(k):
        y = y + w[j] * up[:, j:j + t]
    return y, up[:, t:]


def _chunked(step, s0, seqs):
    b, h, t = seqs[0].shape[:3]
    if t <= CHUNK:
        return step(s0, seqs)
    nc = t // CHUNK
    xs = tuple(jnp.moveaxis(a.reshape(b, h, nc, CHUNK, a.shape[-1]), 2, 0) for a in seqs)
    s, o = lax.scan(step, s0, xs)
    return s, jnp.moveaxis(o, 0, 2).reshape(b, h, t, o.shape[-1])


def _retention(q, k, v, g, r0, pos0):
    f32 = jnp.float32
    q = _rotary(_heads(q, RET_HEADS).astype(f32), pos0)
    k = _rotary(_heads(k, RET_HEADS).astype(f32), pos0) * (RET_DK ** -0.5)
    v = _heads(v, RET_HEADS).astype(f32)
    log_gamma = jnp.log1p(-jnp.power(2.0, -5.0 - jnp.arange(RET_HEADS, dtype=f32)))
    lg = log_gamma[:, None]

    def step(r, xs):
        qc, kc, vc = xs
        n_len = qc.shape[2]
        n = jnp.arange(n_len, dtype=f32)
        intra = jnp.exp(jnp.abs(n[:, None] - n[None, :])[None] * lg[:, :, None])
        scores = jnp.einsum('bhnk,bhmk->bhnm', qc, kc) * intra
        q_dec = jnp.exp(lg * (n + 1.0))[:, :, None]
        k_dec = jnp.exp(lg * (n_len - 1.0 - n))[:, :, None]
        o = jnp.einsum('bhnm,bhmv->bhnv', scores, vc) + jnp.einsum('bhnk,bhkv->bhnv', qc * q_dec, r)
        r_new = jnp.exp(lg * n_len)[:, :, None] * r + jnp.einsum('bhmk,bhmv->bhkv', kc * k_dec, vc)
        return r_new, o

    r, o = _chunked(step, r0.astype(f32), (q, k, v))
    o = _merge_heads(_group_rms(o))
    return o * jax.nn.silu(g.astype(f32)), r


def _hgrn2(q, f, i, g, s0, lb, norm_w):
    f32 = jnp.float32
    q = jax.nn.silu(_heads(q, HG_HEADS).astype(f32))
    z = _heads(f, HG_HEADS).astype(f32)
    lbh = lb.astype(f32).reshape(HG_HEADS, 1, HG_DK)
    fgate = lbh + (1.0 - lbh) * jax.nn.sigmoid(z)
    logf = jnp.log(jnp.maximum(fgate, HG_MIN_F))
    k = (1.0 - lbh) * jax.nn.sigmoid(-z)
    v = _heads(i, HG_HEADS).astype(f32)

    def step(s, xs):
        qc, kc, vc, gc = xs
        n_len = qc.shape[2]
        bcum = jnp.cumsum(gc, axis=2)
        causal = jnp.tril(jnp.ones((n_len, n_len), bool))[:, :, None]
        diff = bcum[:, :, :, None, :] - bcum[:, :, None, :, :]
        decay = jnp.where(causal, jnp.exp(jnp.where(causal, diff, 0.0)), 0.0)
        attn = jnp.einsum('bhtk,bhsk,bhtsk->bhts', qc, kc, decay)
        o = jnp.einsum('bhts,bhsv->bhtv', attn, vc) + jnp.einsum('bhtk,bhkv->bhtv', qc * jnp.exp(bcum), s)
        b_last = bcum[:, :, -1:, :]
        s_new = jnp.exp(b_last[:, :, 0, :, None]) * s + jnp.einsum('bhsk,bhsv->bhkv', kc * jnp.exp(b_last - bcum), vc)
        return s_new, o

    s, o = _chunked(step, s0.astype(f32), (q, k, v, logf))
    o = _merge_heads(_group_rms(o)) * norm_w.astype(f32)
    return o * jax.nn.silu(g.astype(f32)), s


def _lin_combine(c1, c2):
    a1, b1 = c1
    a2, b2 = c2
    return a1 * a2, a2 * b1 + b2


def _rglru(u, y, h0, buf, conv_w, conv_b, w_r, b_r, w_i, b_i, lam, pos0):
    f32 = jnp.float32
    b, t, _ = u.shape
    xc, new_buf = _causal_dwconv(u, buf, conv_w, conv_b)
    xc = xc.astype(f32)
    xb = xc.reshape(b, t, RG_BLOCKS, RG_BLOCK)
    r = jax.nn.sigmoid(jnp.einsum('btnd,nde->btne', xb, w_r.astype(f32)).reshape(b, t, RG_WIDTH) + b_r.astype(f32))
    ig = jax.nn.sigmoid(jnp.einsum('btnd,nde->btne', xb, w_i.astype(f32)).reshape(b, t, RG_WIDTH) + b_i.astype(f32))
    log_a = -RG_C * r * jax.nn.softplus(-lam.astype(f32))
    a = jnp.exp(log_a)
    mult = jnp.sqrt(jnp.maximum(-jnp.expm1(2.0 * log_a), 0.0))
    pos = jnp.arange(t) + pos0
    mult = jnp.where((pos == 0)[None, :, None], 1.0, mult)
    bt = mult * (ig * xc)
    bt = bt.at[:, 0].add(a[:, 0] * h0.astype(f32))
    _, h = lax.associative_scan(_lin_combine, (a, bt), axis=1)
    return h * jax.nn.gelu(y.astype(f32)), h[:, -1], new_buf


def _layer(x, pos0, r0, s0, h0, rgb0, ffb0, lb, norm1_w, w_in, w_branch, w_out, rg_conv_w, rg_conv_b,
           rg_w_r, rg_b_r, rg_w_i, rg_b_i, rg_lambda, hg_norm_w, norm2_w, w_up, ffn_conv_w, ffn_conv_b, w_down):
    f32 = jnp.float32
    hn = _rmsnorm(x, norm1_w)
    proj = hn @ w_in
    rq, rk, rv, rgate, hq, hf, hi, hgate, ru, ry, g_ret, g_hg, g_rg = _split(proj, IN_SPLITS)
    o_ret, r_new = _retention(rq, rk, rv, rgate, r0, pos0)
    o_hg, s_new = _hgrn2(hq, hf, hi, hgate, s0, lb, hg_norm_w)
    o_rg, h_new, rgb_new = _rglru(ru, ry, h0, rgb0, rg_conv_w, rg_conv_b, rg_w_r, rg_b_r, rg_w_i, rg_b_i, rg_lambda, pos0)
    wb_ret, wb_hg, wb_rg = jnp.split(w_branch, [RET_V, RET_V + HG_V], axis=0)
    mixed = (jax.nn.sigmoid(g_ret.astype(f32)) * (o_ret @ wb_ret)
             + jax.nn.sigmoid(g_hg.astype(f32)) * (o_hg @ wb_hg)
             + jax.nn.sigmoid(g_rg.astype(f32)) * (o_rg @ wb_rg))
    x = x + mixed.astype(x.dtype) @ w_out
    hn = _rmsnorm(x, norm2_w)
    a, gate = jnp.split(hn @ w_up, [D_FF], axis=-1)
    a, ffb_new = _causal_dwconv(a, ffb0, ffn_conv_w, ffn_conv_b)
    x = x + (jax.nn.gelu(a) * gate) @ w_down
    return x, r_new, s_new, h_new, rgb_new, ffb_new


def _trunk(x, pos0, r0, s0, h0, rgb0, ffb0, lbs, weights, final_norm_w):
    per_layer = []
    for l in range(DEPTH):
        x, r, s, h, rgb, ffb = _layer(x, pos0, r0[l], s0[l], h0[l], rgb0[l], ffb0[l], lbs[l],
                                      *[w[l] for w in weights])
        per_layer.append((r, s, h, rgb, ffb))
    new_states = tuple(jnp.stack(st, axis=0) for st in zip(*per_layer))
    return _rmsnorm(x, final_norm_w), new_states


def setup_inputs(seed: int = 0) -> dict:
    key = jax.random.key(seed)
    ks = jax.random.split(key, 32)
    f32 = jnp.float32

    def nrm(k, shape, scale):
        return jax.random.normal(k, shape, f32) * scale

    a8 = jax.random.uniform(ks[0], (DEPTH, RG_WIDTH), f32, 0.9, 0.999)
    a1 = a8 ** (1.0 / RG_C)
    rg_lambda = jnp.log(a1) - jnp.log1p(-a1)
    return {
        'x_prompt': nrm(ks[1], (BATCH, SEQ, D_MODEL), 1.0),
        'x_sample': nrm(ks[2], (DEC_BATCH, DEC_SEQ, D_MODEL), 1.0),
        'state_ret': nrm(ks[3], (DEPTH, DEC_BATCH, RET_HEADS, RET_DK, RET_DV), 0.5),
        'state_hgrn': nrm(ks[4], (DEPTH, DEC_BATCH, HG_HEADS, HG_DK, HG_DV), 0.5),
        'state_rglru': nrm(ks[5], (DEPTH, DEC_BATCH, RG_WIDTH), 0.5),
        'cache_rg_conv': nrm(ks[6], (DEPTH, DEC_BATCH, RG_CONV - 1, RG_WIDTH), 1.0),
        'cache_ffn_conv': nrm(ks[7], (DEPTH, DEC_BATCH, FFN_CONV - 1, D_FF), 1.0),
        'norm1_w': 1.0 + nrm(ks[8], (DEPTH, D_MODEL), 0.01),
        'w_in': nrm(ks[9], (DEPTH, D_MODEL, IN_WIDTH), D_MODEL ** -0.5),
        'w_branch': nrm(ks[10], (DEPTH, MIX_WIDTH, D_MODEL), D_MODEL ** -0.5),
        'w_out': nrm(ks[11], (DEPTH, D_MODEL, D_MODEL), D_MODEL ** -0.5),
        'rg_conv_w': nrm(ks[12], (DEPTH, RG_CONV, RG_WIDTH), RG_CONV ** -0.5),
        'rg_conv_b': nrm(ks[13], (DEPTH, RG_WIDTH), 0.01),
        'rg_w_r': nrm(ks[14], (DEPTH, RG_BLOCKS, RG_BLOCK, RG_BLOCK), RG_BLOCK ** -0.5),
        'rg_b_r': nrm(ks[15], (DEPTH, RG_WIDTH), 0.01),
        'rg_w_i': nrm(ks[16], (DEPTH, RG_BLOCKS, RG_BLOCK, RG_BLOCK), RG_BLOCK ** -0.5),
        'rg_b_i': nrm(ks[17], (DEPTH, RG_WIDTH), 0.01),
        'rg_lambda': rg_lambda,
        'hg_lb': nrm(ks[18], (DEPTH, HG_K), 1.0),
        'hg_norm_w': 1.0 + nrm(ks[19], (DEPTH, HG_V), 0.01),
        'norm2_w': 1.0 + nrm(ks[20], (DEPTH, D_MODEL), 0.01),
        'w_up': nrm(ks[21], (DEPTH, D_MODEL, 2 * D_FF), D_MODEL ** -0.5),
        'ffn_conv_w': nrm(ks[22], (DEPTH, FFN_CONV, D_FF), FFN_CONV ** -0.5),
        'ffn_conv_b': nrm(ks[23], (DEPTH, D_FF), 0.01),
        'w_down': nrm(ks[24], (DEPTH, D_FF, D_MODEL), D_FF ** -0.5),
        'final_norm_w': 1.0 + nrm(ks[25], (D_MODEL,), 0.01),
    }


def reference(x_prompt, x_sample, state_ret, state_hgrn, state_rglru, cache_rg_conv, cache_ffn_conv,
              norm1_w, w_in, w_branch, w_out, rg_conv_w, rg_conv_b, rg_w_r, rg_b_r, rg_w_i, rg_b_i,
              rg_lambda, hg_lb, hg_norm_w, norm2_w, w_up, ffn_conv_w, ffn_conv_b, w_down, final_norm_w):
    f32 = jnp.float32
    sm = jax.nn.softmax(hg_lb.astype(f32), axis=0)
    lbs = jnp.cumsum(sm, axis=0) - sm[:1]
    weights = (norm1_w, w_in, w_branch, w_out, rg_conv_w, rg_conv_b, rg_w_r, rg_b_r, rg_w_i, rg_b_i,
               rg_lambda, hg_norm_w, norm2_w, w_up, ffn_conv_w, ffn_conv_b, w_down)

    y_prompt, (ret_p, hg_p, rgh_p, rgc_p, ffc_p) = _trunk(
        x_prompt, 0,
        jnp.zeros((DEPTH, BATCH, RET_HEADS, RET_DK, RET_DV), f32),
        jnp.zeros((DEPTH, BATCH, HG_HEADS, HG_DK, HG_DV), f32),
        jnp.zeros((DEPTH, BATCH, RG_WIDTH), f32),
        jnp.zeros((DEPTH, BATCH, RG_CONV - 1, RG_WIDTH), f32),
        jnp.zeros((DEPTH, BATCH, FFN_CONV - 1, D_FF), f32),
        lbs, weights, final_norm_w)

    y_sample, (ret_s, hg_s, rgh_s, rgc_s, ffc_s) = _trunk(
        x_sample, PAST_LEN, state_ret, state_hgrn, state_rglru, cache_rg_conv, cache_ffn_conv,
        lbs, weights, final_norm_w)

    return (y_prompt, y_sample, ret_p, ret_s, hg_p, hg_s, rgh_p, rgh_s, rgc_p, rgc_s, ffc_p, ffc_s)
```

```python
import numpy as np
from contextlib import ExitStack
import concourse.bass as bass
import concourse.mybir as mybir
from concourse.bass_utils import run_bass_kernel_spmd

F32 = mybir.dt.float32
BF16 = mybir.dt.bfloat16
AF = mybir.ActivationFunctionType
ALU = mybir.AluOpType

D = 1024
DEPTH = 2
SEQ_FULL = 8192
DSEQ = 16
PAST = 2048
TT = 512
INW = 12800
DFF = 2816
RGW = 1280
EPS = 1e-6
SLOT = 5120
NSLOT = 4


class Ins:
    __slots__ = ("q", "fn", "deps", "needed", "sem", "semname", "val", "dma", "inc", "tag", "cost", "odeps", "idx", "tset", "nbytes", "evac")


class Sched:
    def __init__(self):
        self.queues = {k: [] for k in ("pe", "act", "dve", "pool", "sp")}
        self.lw = {}
        self.rd = {}
        self.n = 0
        self.tag = "setup"
        self.sub = ""
        self.rawids = {}

    def op(self, q, fn, reads=(), writes=(), dma_sem=None, inc=16, cost=0.5, tset=None, nbytes=0, evac=False):
        ins = Ins()
        ins.evac = evac
        ins.cost = cost
        ins.tset = tset
        ins.nbytes = nbytes
        ins.odeps = []
        ins.idx = self.n
        ins.tag = self.tag + ((":" + self.sub) if self.sub else "")
        ins.q = q
        ins.fn = fn
        ins.needed = False
        ins.dma = dma_sem is not None
        ins.sem = dma_sem[1] if dma_sem else None
        ins.semname = dma_sem[0] if dma_sem else q
        ins.val = None
        ins.inc = inc
        deps = []
        for k in reads:
            w = self.lw.get(k)
            if w is not None:
                deps.append(w)
        nraw = len(deps)
        self._nraw = nraw
        for k in writes:
            w = self.lw.get(k)
            if w is not None:
                deps.append(w)
            r = self.rd.get(k)
            if r:
                deps.extend(r)
        dd = []
        seen = set()
        rawids = set(id(d) for d in deps[:nraw])
        self.rawids[id(ins)] = rawids
        for d in deps:
            if id(d) in seen or d is ins:
                continue
            seen.add(id(d))
            if q == "pe" and d.q == "pe" and not d.dma:
                ins.odeps.append(d)
                continue
            d.needed = True
            dd.append(d)
        ins.deps = dd
        for k in writes:
            self.lw[k] = ins
            self.rd[k] = []
        self.n += 1
        for k in reads:
            self.rd.setdefault(k, []).append(ins)
        self.queues[q].append(ins)
        return ins

    def reorder(self, window=None):
        import os as _os2
        _wp = int(_os2.environ.get("K_WPE", "256"))
        _we = int(_os2.environ.get("K_WEL", "64"))
        window = window or {"pe": _wp, "act": _we, "dve": _we, "pool": _we, "sp": 1}
        seg = {}
        pre = {}
        for q, lst in self.queues.items():
            cut = 0
            for i, ins in enumerate(lst):
                if ins.fn is None:
                    cut = i + 1
            pre[q] = lst[:cut]
            seg[q] = lst[cut:]
        inseg = set()
        for q in seg:
            for ins in seg[q]:
                inseg.add(id(ins))
        nun = {}
        users = {}
        for q in seg:
            for ins in seg[q]:
                c = 0
                for d in ins.deps + ins.odeps:
                    if id(d) in inseg:
                        c += 1
                        users.setdefault(id(d), []).append(ins)
                nun[id(ins)] = c
        done = {}
        rt = {}
        for q in seg:
            for ins in seg[q]:
                if nun[id(ins)] == 0:
                    rt[id(ins)] = 0.0
        pending = {q: list(seg[q]) for q in seg}
        tfree = {q: 0.0 for q in seg}
        out = {q: [] for q in seg}
        cur_t = [None]
        bus = [0.0]
        import os as _os
        LAT_X = float(_os.environ.get('K_LATX', '1.0'))
        LAT_S = 0.2
        EVAC_BONUS = float(_os.environ.get('K_EVAC', '1.5'))
        cand = {}
        self.idle_causes = {}
        dirty = set(seg.keys())
        total = sum(len(v) for v in seg.values())
        nsched = 0
        while nsched < total:
            for q in list(dirty):
                best = None
                lst = pending[q]
                W = window[q]
                tf = tfree[q]
                for j in range(min(W, len(lst))):
                    ins = lst[j]
                    r = rt.get(id(ins))
                    if r is None:
                        continue
                    st = r if r > tf else tf
                    key = st
                    if q == "act" and ins.tset is not None and ins.tset != cur_t[0]:
                        key = st + 1.3
                    if ins.evac:
                        key -= EVAC_BONUS
                    if best is None or key < best[0] - 1e-9:
                        best = (key, st, j, ins)
                    if key <= tf - EVAC_BONUS + 1e-9:
                        break
                cand[q] = best
            dirty.clear()
            bq = None
            for q, b in cand.items():
                if b is not None and (bq is None or b[0] < cand[bq][0]):
                    bq = q
            key, st, j, ins = cand[bq]
            tf0 = tfree[bq]
            lst = pending[bq]
            if lst[j] is not ins:
                j = lst.index(ins)
            del lst[j]
            cost = ins.cost
            if bq == "act" and ins.tset is not None and ins.tset != cur_t[0]:
                cost += 1.3
                cur_t[0] = ins.tset
            if ins.dma:
                tfree[bq] = st + 0.08
                s0 = max(bus[0], st)
                bus[0] = s0 + ins.nbytes / 180e3
                fin = bus[0] + 2.0
            else:
                fin = st + cost
                tfree[bq] = fin
            done[id(ins)] = fin
            if bq == "pe" and st - tf0 > 0.3:
                bd = None
                for d in ins.deps + ins.odeps:
                    if id(d) in done and (bd is None or done[id(d)] > done[id(bd)]):
                        bd = d
                kind = "RAW" if (bd is not None and id(bd) in self.rawids.get(id(ins), ())) else "WAR"
                k = (ins.tag, ((bd.q + "|" + bd.tag + "|" + kind) if bd is not None else "-"))
                self.idle_causes[k] = self.idle_causes.get(k, 0.0) + (st - tf0)
            out[bq].append(ins)
            nsched += 1
            dirty.add(bq)
            for u in users.get(id(ins), ()):
                nun[id(u)] -= 1
                lat = LAT_S if (u.q == ins.q and not ins.dma) else LAT_X
                v = fin + lat
                if rt.get(("p", id(u)), 0.0) < v:
                    rt[("p", id(u))] = v
                if nun[id(u)] == 0:
                    rt[id(u)] = rt.get(("p", id(u)), 0.0)
                    dirty.add(u.q)
        for q in seg:
            self.queues[q] = pre[q] + out[q]
        self.sim_makespan = max(tfree.values())

    def barrier(self):
        lasts = []
        for q in ("pe", "act", "dve", "pool"):
            lst = [i for i in self.queues[q] if i.fn is not None]
            if lst:
                lst[-1].needed = True
                lasts.append(lst[-1])
        for q in self.queues:
            ins = Ins()
            ins.evac = False
            ins.tag = "barrier"
            ins.q = q
            ins.fn = None
            ins.needed = False
            ins.dma = False
            ins.sem = None
            ins.semname = q
            ins.val = None
            ins.inc = 0
            ins.deps = list(lasts)
            self.queues[q].append(ins)

    def finalize(self, nc, engsems, block):
        dcnt = {}
        allsem = {}
        snap = None
        order = ["sp"] + [q for q in self.queues if q != "sp"]
        for q in order:
            lst = self.queues[q]
            c = 0
            for ins in lst:
                if ins.fn is None:
                    if q == "sp":
                        snap = dict(dcnt)
                        self._snaps = getattr(self, "_snaps", []) + [snap]
                    continue
                if ins.dma:
                    dcnt[ins.semname] = dcnt.get(ins.semname, 0) + ins.inc
                    ins.val = dcnt[ins.semname]
                    allsem[ins.semname] = ins.sem
                elif ins.needed:
                    c += 1
                    ins.val = c
                    ins.sem = engsems[q]

        snaps = getattr(self, "_snaps", [])

        def run(e, lst, final=False):
            waited = {}
            bi = 0
            for ins in lst:
                for d in ins.deps:
                    if waited.get(d.semname, 0) < d.val:
                        e.wait_ge(d.sem, d.val)
                        waited[d.semname] = d.val
                if ins.fn is None:
                    for name, tot in snaps[bi].items():
                        if waited.get(name, 0) < tot:
                            e.wait_ge(allsem[name], tot)
                            waited[name] = tot
                    bi += 1
                    continue
                r = ins.fn(e)
                if ins.dma:
                    r.then_inc(ins.sem, ins.inc)
                elif ins.needed:
                    r.then_inc(ins.sem, 1)
            if final:
                for name, tot in dcnt.items():
                    if waited.get(name, 0) < tot:
                        e.wait_ge(allsem[name], tot)

        qs = self.queues

        @block.sync
        def _(e):
            run(e, qs["sp"], final=True)

        @block.tensor
        def _(e):
            run(e, qs["pe"])

        @block.scalar
        def _(e):
            run(e, qs["act"])

        @block.vector
        def _(e):
            run(e, qs["dve"])

        @block.gpsimd
        def _(e):
            run(e, qs["pool"])


class BufPool:
    def __init__(self, items):
        self.free_list = list(items)
        self.total = len(items)

    def alloc(self):
        if not self.free_list:
            raise RuntimeError("pool exhausted")
        return self.free_list.pop(0)

    def free(self, b):
        self.free_list.append(b)


def weight_groups():
    g = []

    def win(name, c0, n):
        g.append((name, 8, n, [("w_in", 0, c0, n, 0)], "norm1"))

    win("rgate0", 2048, 512)
    win("rgate1", 2560, 512)
    win("rq", 0, 512)
    win("rk", 512, 512)
    win("rv0", 1024, 512)
    win("rv1", 1536, 512)
    win("hgate0", 6144, 512)
    win("hgate1", 6656, 512)
    win("hi0", 5120, 512)
    win("hi1", 5632, 512)
    win("hq0", 3072, 512)
    win("hf0", 4096, 512)
    win("hq1", 3584, 512)
    win("hf1", 4608, 512)
    win("ry0", 8448, 512)
    win("ry1", 8960, 512)
    win("ry2", 9472, 256)
    g.append(("rgw", 2, 2560, None, None))
    win("ru0", 7168, 512)
    win("ru1", 7680, 512)
    win("ru2", 8192, 256)
    brow = [0, 1024, 2048]
    bkc = [8, 8, 10]
    for b in range(3):
        for jh in range(2):
            win(f"mg{b}{jh}", 9728 + b * 1024 + jh * 512, 512)
            g.append((f"wb{b}{jh}", bkc[b], 512, [("w_branch", brow[b], jh * 512, 512, 0)], "hgn" if b == 1 else None))
    g.append(("wo0", 8, 512, [("w_out", 0, 0, 512, 0)], None))
    g.append(("wo1", 8, 512, [("w_out", 0, 512, 512, 0)], None))
    for i in range(11):
        g.append((f"wu{i}", 8, 512, [("w_up", 0, 256 * i, 256, 0), ("w_up", 0, DFF + 256 * i, 256, 256)], "norm2"))
    for half in range(2):
        for kg, (k0, kn) in enumerate([(0, 8), (8, 8), (16, 6)]):
            g.append((f"wd{half}{kg}", kn, 512, [("w_down", k0 * 128, half * 512, 512, 0)], None))
    return g


def smalls_layout():
    off = {}
    c = 0

    def add(name, n):
        nonlocal c
        off[name] = (c, n)
        c += n

    add("lb0", 8)
    add("lb1", 8)
    for l in range(DEPTH):
        add(f"rcw{l}", 40)
        add(f"rcb{l}", 10)
        add(f"rbr{l}", 10)
        add(f"rbi{l}", 10)
        add(f"rlam{l}", 10)
        add(f"fcw{l}", 66)
        add(f"fcb{l}", 22)
        add(f"n1{l}", 8)
        add(f"n2{l}", 8)
        add(f"hgn{l}", 8)
        add(f"h0{l}", 10)
        add(f"u0{l}", 30)
        add(f"a0{l}", 44)
    return off, c


SM_OFF, SM_N = smalls_layout()
OUT_W = 84


def build_program(SEQ):
    NTILE = SEQ // TT
    nc = bass.Bass("TRN2", target_bir_lowering=False)
    S = Sched()
    es = ExitStack()

    def din(name, shape, dt=F32):
        return nc.dram_tensor(name, list(shape), dt, kind="ExternalInput").ap()

    def dout(name, shape, dt=F32):
        return nc.dram_tensor(name, list(shape), dt, kind="ExternalOutput").ap()

    x_p = din("x_p", [SEQ, D])
    x_s = din("x_s", [DSEQ, D])
    st_ret = din("st_ret", [DEPTH, 4, 128, 256])
    st_hg = din("st_hg", [DEPTH, 8, 128, 128])
    smalls_d = din("smalls", [128, SM_N])
    wfin_d = din("wfin", [128, D])
    W = {
        "w_in": din("w_in", [DEPTH, D, INW]),
        "w_branch": din("w_branch", [DEPTH, 3328, D]),
        "w_out": din("w_out", [DEPTH, D, D]),
        "w_up": din("w_up", [DEPTH, D, 2 * DFF]),
        "w_down": din("w_down", [DEPTH, DFF, D]),
    }
    rg_w_r = din("rg_w_r", [DEPTH, 5, 256, 256])
    rg_w_i = din("rg_w_i", [DEPTH, 5, 256, 256])
    ident_d = din("ident", [128, 128])
    perm_d = din("permT", [128, 128])
    rotc_p = din("rotc_p", [128, SEQ])
    rots_p = din("rots_p", [128, SEQ])
    rotc_s = din("rotc_s", [128, DSEQ])
    rots_s = din("rots_s", [128, DSEQ])
    retmask_d = din("retmask", [2, 128, 4, 128])
    qd_d = din("qd", [2, 128, 4, 128])
    kd_d = din("kd", [2, 128, 4, 128])
    hgmask_d = din("hgmask", [2, 128, 128])
    scanmask_d = din("scanmask", [2, 128, TT])

    y_p = dout("y_p", [SEQ, D])
    y_s = dout("y_s", [DSEQ, D])
    ret_o = dout("ret_o", [2, DEPTH, 4, 128, 256])
    hg_o = dout("hg_o", [2, DEPTH, 8, 128, 128])
    small_o = dout("small_o", [128, 2 * DEPTH * OUT_W])

    groups = weight_groups()
    NG = len(groups)
    wscr = nc.dram_tensor("wscr", [DEPTH * NG, 128, SLOT], BF16, kind="Internal").ap()

    def sem(name):
        return (name, es.enter_context(nc.semaphore(name)))

    engsems = {q: sem("e_" + q)[1] for q in ("pe", "act", "dve", "pool")}
    nsem = {"i": 0}

    def newsem():
        nsem["i"] += 1
        return sem(f"d_m{nsem['i']}")

    sem_x = [sem(f"d_x{b}") for b in range(4)]
    sem_y = [sem(f"d_y{b}") for b in range(4)]
    sem_slot = [sem(f"d_w{i}") for i in range(NSLOT)]
    sem_stg = [sem(f"d_stg{i}") for i in range(2)]
    sem_scr = [sem(f"d_scr{i}") for i in range(2)]
    sem_rot = sem("d_rot")
    sem_out = sem("d_out")
    sem_gc = [sem(f"d_gc{i}") for i in range(5)]
    sem_sti = [sem(f"d_sti{i}") for i in range(4)]
    sem_sto = [sem(f"d_sto{i}") for i in range(4)]

    def sb(name, shape, dt=F32):
        return es.enter_context(nc.sbuf_tensor(name, list(shape), dt))

    smalls = sb("smalls_sb", [128, SM_N])
    K_SM = ("smalls",)

    def smc(name, a=0, n=None):
        o, w = SM_OFF[name]
        if n is None:
            n = w - a
        return smalls[:, o + a:o + a + n]

    def isps(*aps):
        for a_ in aps:
            try:
                if a_.space == mybir.MemoryType.PSUM:
                    return True
            except Exception:
                pass
        return False

    def fsz(ap):
        n = 1
        for d in ap.shape[1:]:
            n *= int(d)
        return n

    TSET = {AF.Silu: "silu", AF.Sigmoid: "sig", AF.Exp: "exp", AF.Ln: "ln", AF.Sqrt: "sqrt", AF.Gelu_apprx_tanh: "gelu"}

    def ACT(out, in_, func, reads, writes, bias=None, scale=None, accum=None):
        kw = {}
        if bias is not None:
            kw["bias"] = bias
        if scale is not None:
            kw["scale"] = scale
        if accum is not None:
            kw["accum_out"] = accum
        c = 0.22 + fsz(out) / 1400.0 + (0.15 if accum is not None else 0.0)
        return S.op("act", lambda e: e.activation(out=out, in_=in_, func=func, **kw), reads, writes, cost=c, tset=TSET.get(func), evac=isps(in_))

    def ecost(q, n, mul=1.0):
        if q == "dve":
            return 0.12 + mul * n / 960.0
        return 0.25 + mul * n / 700.0

    def TT_(q, out, in0, in1, op, reads, writes):
        return S.op(q, lambda e: e.tensor_tensor(out=out, in0=in0, in1=in1, op=op), reads, writes, cost=ecost(q, fsz(out)), evac=isps(in0, in1))

    def TS(q, out, in0, s1, s2, op0, op1, reads, writes):
        if s2 is None:
            return S.op(q, lambda e: e.tensor_scalar(out=out, in0=in0, scalar1=s1, scalar2=None, op0=op0), reads, writes,
                        cost=ecost(q, fsz(out)), evac=isps(in0))
        return S.op(q, lambda e: e.tensor_scalar(out=out, in0=in0, scalar1=s1, scalar2=s2, op0=op0, op1=op1), reads, writes,
                    cost=ecost(q, fsz(out)), evac=isps(in0))

    def STT(out, in0, scalar, in1, op0, op1, reads, writes):
        return S.op("dve", lambda e: e.scalar_tensor_tensor(out=out, in0=in0, scalar=scalar, in1=in1, op0=op0, op1=op1), reads, writes,
                    cost=ecost("dve", fsz(out)), evac=isps(in0, in1))

    def CP(q, out, in_, reads, writes):
        return S.op(q, lambda e: e.tensor_copy(out=out, in_=in_), reads, writes, cost=ecost(q, fsz(out)), evac=isps(in_))

    def MM(out, lhsT, rhs, start, stop, reads, writes):
        n = max(64, fsz(rhs))
        c = 0.03 + n / 1950.0
        if rhs.dtype == F32:
            c *= 4
        return S.op("pe", lambda e: e.matmul(out, lhsT=lhsT, rhs=rhs, start=start, stop=stop), reads, writes, cost=c)

    def TR(out, in_, idn, reads, writes):
        return S.op("pe", lambda e: e.transpose(out=out, in_=in_, identity=idn), reads, writes, cost=0.1)

    def DMA(out, in_, reads, writes, semt, nc_ok=False):
        nb = int(out.shape[0]) * fsz(out) * (2 if out.dtype == BF16 else 4)
        if nc_ok:
            return S.op("sp", lambda e: e.dma_start(out=out, in_=in_, allow_slow_non_contiguous=True), reads, writes, dma_sem=semt, nbytes=nb)
        return S.op("sp", lambda e: e.dma_start(out=out, in_=in_), reads, writes, dma_sem=semt, nbytes=nb)

    def MEMSET(q, ap, val, writes):
        return S.op(q, lambda e: e.memset(ap, val), (), writes, cost=ecost(q, fsz(ap)))

    rr = {"i": 0}

    def anyq():
        rr["i"] += 1
        return ("dve", "pool")[rr["i"] % 2]

    DMA(smalls[:], smalls_d[:, :], (), [K_SM], newsem())

    S.tag = "phase0"
    es0 = ExitStack()
    stg = [es0.enter_context(nc.sbuf_tensor(f"stg{i}", [128, SLOT], F32)) for i in range(2)]
    cvt = [es0.enter_context(nc.sbuf_tensor(f"cvt{i}", [128, SLOT], BF16)) for i in range(2)]
    allg = [(l, gidx) for l in range(DEPTH) for gidx in range(NG)]

    def p0_load(gi):
        l, gidx = allg[gi]
        name, kcn, ncols, pieces, scale = groups[gidx]
        i = gi % 2
        n = kcn * ncols
        sv = stg[i][:, 0:n].rearrange("p (k c) -> p k c", c=ncols)
        if name == "rgw":
            for gt, wsrc in enumerate((rg_w_r, rg_w_i)):
                dst = stg[i][:, gt * 2560:(gt + 1) * 2560].rearrange("p (n k e) -> p n k e", n=5, k=2)
                for nb in range(5):
                    DMA(dst[:, nb, :, :], wsrc[l, nb].rearrange("(k p) e -> p k e", p=128), (), [("stg", i)], sem_stg[i])
        else:
            for (tn, r0, c0, npc, dc) in pieces:
                src = W[tn][l, r0:r0 + kcn * 128, c0:c0 + npc].rearrange("(k p) c -> p k c", p=128)
                DMA(sv[:, :, dc:dc + npc], src, (), [("stg", i)], sem_stg[i])

    def p0_cvt(gi):
        l, gidx = allg[gi]
        name, kcn, ncols, pieces, scale = groups[gidx]
        i = gi % 2
        n = kcn * ncols
        sv = stg[i][:, 0:n].rearrange("p (k c) -> p k c", c=ncols)
        cvv = cvt[i][:, 0:n].rearrange("p (k c) -> p k c", c=ncols)
        if name == "rgw":
            CP("pool", cvt[i][:, 0:5120], stg[i][:, 0:5120], [("stg", i)], [("cvt", i)])
            n = 5120
        elif scale is None:
            ACT(cvv, sv, AF.Copy, [("stg", i)], [("cvt", i)])
        else:
            scn = {"norm1": f"n1{l}", "norm2": f"n2{l}", "hgn": f"hgn{l}"}[scale]
            sc = smc(scn).unsqueeze(2).broadcast_to([128, kcn, ncols])
            TT_(("dve", "pool")[(gi // 2) % 2], cvv, sv, sc, ALU.mult, [("stg", i), K_SM], [("cvt", i)])
        DMA(wscr[l * NG + gidx, :, 0:n], cvt[i][:, 0:n], [("cvt", i)], [("wscr", l, gidx)], sem_scr[i])

    p0_load(0)
    for gi in range(len(allg)):
        if gi + 1 < len(allg):
            p0_load(gi + 1)
        p0_cvt(gi)

    S.barrier()
    es0.close()

    xt = sb("xt", [128, 4, D])
    xnb = [sb(f"xnb{i}", [128, D], BF16) for i in range(2)]
    junk = sb("junk", [128, D], BF16)
    hnT = sb("hnT", [128, 8, TT], BF16)
    Vt = sb("Vt", [128, 4, D], BF16)
    big = sb("big", [128, 26, TT], BF16)
    NF = 13
    NBF = 16
    fbufs = [sb(f"f{i}", [128, TT]) for i in range(NF)]
    bbufs = [sb(f"b{i}", [128, TT], BF16) for i in range(NBF)]
    sbb = [sb(f"sbb{i}", [128, 1024], BF16) for i in range(2)]
    sall = [sb(f"sall{i}", [128, 1024]) for i in range(2)]
    ubuf = [sb(f"ubuf{i}", [128, 3 + TT]) for i in range(2)]
    abuf = [sb(f"abuf{i}", [128, 2 + TT]) for i in range(2)]
    wslot = [sb(f"wslot{i}", [128, SLOT], BF16) for i in range(NSLOT)]
    rotc = sb("rotc", [128, TT])
    rots = sb("rots", [128, TT])
    wfin = sb("wfin_sb", [128, D])
    ident = sb("ident_sb", [128, 128], BF16)
    identf = sb("identf_sb", [128, 128])
    ones = sb("ones_sb", [128, 128], BF16)
    permT = sb("perm_sb", [128, 128])
    retmask = sb("retmask_sb", [128, 4, 128])
    qd = sb("qd_sb", [128, 4, 128])
    kd = sb("kd_sb", [128, 4, 128])
    hgmask = sb("hgmask_sb", [128, 128])
    scanmask = sb("scanmask_sb", [128, TT])
    S_ret = [sb(f"S_ret{l}", [128, 4, 256]) for l in range(DEPTH)]
    S_hg = [sb(f"S_hg{l}", [128, 8, 128]) for l in range(DEPTH)]
    hstate = [sb(f"hstate{l}", [128, 10]) for l in range(DEPTH)]
    uhalo = [sb(f"uhalo{l}", [128, 10, 3]) for l in range(DEPTH)]
    ahalo = [sb(f"ahalo{l}", [128, 22, 2]) for l in range(DEPTH)]
    outst = sb("outst", [128, 2 * DEPTH * OUT_W])
    stat = sb("stat", [128, 16])
    lbt = sb("lbt", [128, 2, 8])
    omlt = sb("omlt", [128, 2, 8])
    rgc = sb("rgc", [128, 2, 10])
    rgc2 = sb("rgc2", [128, 2, 10])
    ebc = [sb(f"ebc{i}", [128, 8]) for i in range(4)]
    epsb = sb("epsb", [128, 1])

    psum = [es.enter_context(nc.psum_tensor(f"ps{i}", [128, TT], F32)) for i in range(8)]

    FP = BufPool([(fbufs[i], ("f", i)) for i in range(NF)])
    BP = BufPool([(bbufs[i], ("b", i)) for i in range(NBF)])
    PP = BufPool([(psum[i], ("ps", i)) for i in range(8)])
    WP = BufPool(list(range(NSLOT)))

    DMA(wfin[:], wfin_d[:, :], (), [("wfin",)], newsem())
    DMA(identf[:], ident_d[:, :], (), [("identf",)], newsem())
    DMA(permT[:], perm_d[:, :], (), [("permT",)], newsem())
    CP("pool", ident[:], identf[:], [("identf",)], [("ident",)])
    MEMSET("pool", ones[:], 1.0, [("ones",)])
    MEMSET("pool", epsb[:], EPS, [("epsb",)])
    MEMSET("pool", lbt[:, 0, :], 0.0, [("lbt",)])
    TT_("dve", lbt[:, 1, :], smc("lb1"), smc("lb0"), ALU.subtract, [K_SM], [("lbt",)])
    ACT(lbt[:, 1, :], lbt[:, 1, :], AF.Sigmoid, [("lbt",)], [("lbt",)])
    TS("dve", omlt[:], lbt[:], -1.0, 1.0, ALU.mult, ALU.add, [("lbt",)], [("omlt",)])
    for l in range(DEPTH):
        ACT(rgc[:, l, :], smc(f"rlam{l}"), AF.Exp, [K_SM], [("rgc",)], scale=-1.0)
        TS("dve", rgc[:, l, :], rgc[:, l, :], 1.0, None, ALU.add, None, [("rgc",)], [("rgc",)])
        ACT(rgc[:, l, :], rgc[:, l, :], AF.Ln, [("rgc",)], [("rgc",)])
    TS("dve", rgc[:], rgc[:], -8.0, None, ALU.mult, None, [("rgc",)], [("rgc",)])
    TS("dve", rgc2[:], rgc[:], 2.0, None, ALU.mult, None, [("rgc",)], [("rgc2",)])

    wseq = []
    wstate = {"next": 0, "cur": -1}
    wloaded = {}

    def wpump():
        while wstate["next"] < len(wseq) and wstate["next"] <= wstate["cur"] + NSLOT and WP.free_list:
            l, gidx = wseq[wstate["next"]]
            name, kcn, ncols, pieces, scale = groups[gidx]
            n = 5120 if name == "rgw" else kcn * ncols
            s = WP.alloc()
            DMA(wslot[s][:, 0:n], wscr[l * NG + gidx, :, 0:n], [("wscr", l, gidx)], [("wslot", s)], sem_slot[s])
            wloaded[wstate["next"]] = s
            wstate["next"] += 1

    def wget(l, name):
        wstate["cur"] += 1
        k = wstate["cur"]
        ll, gidx = wseq[k]
        assert ll == l and groups[gidx][0] == name, (ll, l, groups[gidx][0], name)
        wpump()
        assert k in wloaded
        s = wloaded.pop(k)
        _, kcn, ncols, _, _ = groups[gidx]
        view = wslot[s][:, 0:kcn * ncols].rearrange("p (k c) -> p k c", c=ncols)
        return s, view

    def wfree(s):
        WP.free(s)
        wpump()

    def emit_tile(grp, ti, T, xsrc, ydst):
        tb = min(128, T)
        NB = T // tb
        first = (ti == 0)

        for b in range(NB):
            DMA(xt[:tb, b, :], xsrc[b * tb:(b + 1) * tb, :], (), [("x", b)], sem_x[b])

        def norm_to_hnT(l):
            for b in range(NB):
                ACT(junk[:tb, :], xt[:tb, b, :], AF.Square, [("x", b)], [("junk",), ("stat", b)], accum=stat[:tb, b:b + 1])
                ACT(stat[:tb, 4 + b:5 + b], stat[:tb, b:b + 1], AF.Sqrt, [("stat", b), ("epsb",)], [("stat2", b)],
                    bias=epsb[:tb, 0:1], scale=1.0 / D)
                S.op("dve", lambda e, b=b: e.reciprocal(out=stat[:tb, 8 + b:9 + b], in_=stat[:tb, 4 + b:5 + b]),
                     [("stat2", b)], [("stat3", b)], cost=0.15)
                xb_ = xnb[b % 2]
                TS("dve", xb_[:tb, :], xt[:tb, b, :], stat[:tb, 8 + b:9 + b], None, ALU.mult, None,
                   [("x", b), ("stat3", b)], [("xnb", b % 2)])
                ps, pk = PP.alloc()
                pv = ps[:].bitcast(BF16)
                for kc in range(8):
                    TR(pv[:, kc * 128:kc * 128 + tb], xb_[:tb, kc * 128:(kc + 1) * 128], ident[:tb, :tb],
                       [("xnb", b % 2), ("ident",)], [pk])
                src = pv.rearrange("p (k t) -> p k t", t=128)[:, :, 0:tb]
                if b % 2 == 0:
                    ACT(hnT[:, :, b * tb:(b + 1) * tb], src, AF.Copy, [pk], [("hnT", b)])
                else:
                    CP("dve", hnT[:, :, b * tb:(b + 1) * tb], src, [pk], [("hnT", b)])
                PP.free((ps, pk))

        HN_ALL = [("hnT", b) for b in range(4)]

        def proj_fm(wv, cc, reads_extra=()):
            ps, pk = PP.alloc()
            for kc in range(8):
                MM(ps[:, 0:T], wv[:, kc, cc * 128:(cc + 1) * 128], hnT[:, kc, 0:T], kc == 0, kc == 7,
                   HN_ALL[:NB] + list(reads_extra), [pk])
            return ps, pk

        def proj_tm_to_V(wv, ws, coff):
            for b in range(NB):
                ps, pk = PP.alloc()
                for kc in range(8):
                    MM(ps[:tb, 0:512], hnT[:, kc, b * tb:(b + 1) * tb], wv[:, kc, 0:512], kc == 0, kc == 7,
                       [("hnT", b), ("wslot", ws)], [pk])
                ACT(Vt[:tb, b, coff:coff + 512], ps[:tb, 0:512], AF.Copy, [pk], [("V", b)])
                PP.free((ps, pk))

        def la_phase1(QT, qk, KT, kk, KE, kek, ke_scale, vcol, dv, maskap, maskkey, seg, S32, skey, decs, dec_reads, sbi):
            nseg_b = tb // seg
            nseg = NB * nseg_b
            S.sub = "la_tr"
            ps, pk = PP.alloc()
            pv = ps[:].bitcast(BF16)
            for b in range(NB):
                TR(pv[:tb, b * 128:(b + 1) * 128], KE[:, b * tb:(b + 1) * tb], ident[:, :], [kek, ("ident",)], [pk])
            ketm, ketk = BP.alloc()
            ACT(ketm[:tb, 0:NB * 128], pv[:tb, 0:NB * 128], AF.Copy, [pk], [ketk], scale=ke_scale)
            PP.free((ps, pk))
            S.sub = "la_U"
            per_bank = 512 // dv
            nbank = max(nseg_b, (nseg + per_bank - 1) // per_bank)
            ubanks = [PP.alloc() for _ in range(nbank)]

            def uloc(si):
                if nseg_b > 1:
                    return si % nseg_b, (si // nseg_b) * dv
                return si // per_bank, (si % per_bank) * dv

            for si in range(nseg):
                bi_, o0 = uloc(si)
                ps, pk = ubanks[bi_]
                b = si // nseg_b
                p0 = (si % nseg_b) * seg
                MM(ps[:, o0:o0 + dv], ketm[p0:p0 + seg, b * 128:(b + 1) * 128], Vt[p0:p0 + seg, b, vcol:vcol + dv], True, True,
                   [ketk, ("V", b)], [pk])
            BP.free((ketm, ketk))
            S.sub = "la_sc"
            sps, spk = PP.alloc()
            for b in range(NB):
                MM(sps[:tb, b * 128:b * 128 + tb], KT[:, b * tb:(b + 1) * tb], QT[:, b * tb:(b + 1) * tb], True, True,
                   [kk, qk], [spk])
            S.sub = "la_chain"
            sbt = sbb[sbi]
            sbk = ("sbb", sbi)
            sbv = sbt[:, 0:nseg * dv].rearrange("p (s v) -> p s v", v=dv)
            sal = sall[sbi]
            salk = ("sall", sbi)
            sav = sal[:, 0:nseg * dv].rearrange("p (s v) -> p s v", v=dv)
            CP("pool", sbv[:, 0, :], S32, [skey], [sbk])
            prev, prevk = S32, skey
            for si in range(nseg):
                bi_, o0 = uloc(si)
                ps, pk = ubanks[bi_]
                if si == nseg - 1:
                    out, outk = S32, skey
                else:
                    out, outk = sav[:, si, :], salk
                STT(out, prev, decs(si), ps[:, o0:o0 + dv], ALU.mult, ALU.add, [prevk, pk] + list(dec_reads), [outk])
                prev, prevk = out, outk
            for u in ubanks:
                PP.free(u)
            if nseg > 1:
                CP("pool", sbv[:, 1:nseg, :], sav[:, 0:nseg - 1, :], [salk], [sbk])
            S.sub = "la_mask"
            scm, sck = BP.alloc()
            psv = sps[:tb, 0:NB * 128].rearrange("p (b t) -> p b t", t=128)[:, :, 0:tb]
            scv = scm[:tb, 0:NB * 128].rearrange("p (b t) -> p b t", t=128)[:, :, 0:tb]
            TT_("dve", scv, psv, maskap.unsqueeze(1).broadcast_to([tb, NB, tb]), ALU.mult, [spk, maskkey], [sck])
            PP.free((sps, spk))
            S.sub = ""
            return dict(QT=QT, qk=qk, scm=scm, sck=sck, sbv=sbv, sbk=sbk, vcol=vcol, dv=dv, seg=seg)

        def la_phase2(c):
            QT, qk, scm, sck, sbv, sbk, vcol, dv, seg = (c[k] for k in ("QT", "qk", "scm", "sck", "sbv", "sbk", "vcol", "dv", "seg"))
            nd = dv // 128
            nseg_b = tb // seg
            outs = []
            S.sub = "la_o"
            for d in range(nd):
                ps, pk = PP.alloc()
                for b in range(NB):
                    MM(ps[:, b * tb:(b + 1) * tb], Vt[:tb, b, vcol + d * 128:vcol + (d + 1) * 128],
                       scm[:tb, b * 128:b * 128 + tb], True, False, [("V", b), sck], [pk])
                    for sj in range(nseg_b):
                        si = b * nseg_b + sj
                        t0 = b * tb + sj * seg
                        MM(ps[:, t0:t0 + seg], sbv[:, si, d * 128:(d + 1) * 128], QT[:, t0:t0 + seg], False, sj == nseg_b - 1,
                           [sbk, qk], [pk])
                outs.append((ps, pk))
            BP.free((scm, sck))
            S.sub = "la_sq"
            sqs = []
            for (ps, pk) in outs:
                sq, sqk = BP.alloc()
                ACT(sq[:, 0:T], ps[:, 0:T], AF.Square, [pk], [sqk])
                sqs.append((sq, sqk))
            S.sub = ""
            return outs, sqs

        def postproc(outs, sqs, kc0, dvtot):
            S.sub = "post"
            ss, ssk = PP.alloc()
            for i, (sq, sqk) in enumerate(sqs):
                MM(ss[:, 0:T], ones[:, :], sq[:, 0:T], i == 0, i == len(sqs) - 1, [("ones",), sqk], [ssk])
            for b_ in sqs:
                BP.free(b_)
            rs, rsk = FP.alloc()
            ACT(rs[:, 0:T], ss[:, 0:T], AF.Sqrt, [ssk, ("epsb",)], [rsk], bias=epsb[:, 0:1], scale=1.0 / dvtot)
            PP.free((ss, ssk))
            S.op("dve", lambda e: e.reciprocal(out=rs[:, 0:T], in_=rs[:, 0:T]), [rsk], [rsk], cost=0.12 + T / 960.0)
            for d, (ps, pk) in enumerate(outs):
                tmp, tk = FP.alloc()
                TT_("dve", tmp[:, 0:T], ps[:, 0:T], rs[:, 0:T], ALU.mult, [pk, rsk], [tk])
                PP.free((ps, pk))
                TT_("pool", big[:, kc0 + d, 0:T], tmp[:, 0:T], big[:, kc0 + d, 0:T], ALU.mult, [tk, ("big", kc0 + d)], [("big", kc0 + d)])
                FP.free((tmp, tk))
            FP.free((rs, rsk))
            S.sub = ""

        def pipeline(n, stages):
            for step in range(n + len(stages) - 1):
                for k, f in enumerate(stages):
                    i = step - k
                    if 0 <= i < n:
                        f(i)

        if grp == 0:
            DMA(rotc[:, 0:T], rotc_p[:, ti * TT:ti * TT + T], (), [("rot",)], sem_rot)
            DMA(rots[:, 0:T], rots_p[:, ti * TT:ti * TT + T], (), [("rot",)], sem_rot)
        else:
            DMA(rotc[:, 0:T], rotc_s[:, 0:T], (), [("rot",)], sem_rot)
            DMA(rots[:, 0:T], rots_s[:, 0:T], (), [("rot",)], sem_rot)

        lg = [float(np.log1p(-2.0 ** (-5.0 - h))) for h in range(4)]
        Lblk = tb

        for l in range(DEPTH):
            S.tag = "norm1"
            norm_to_hnT(l)

            S.tag = "ret"
            for gname, kcb in (("rgate0", 0), ("rgate1", 4)):
                ws, wv = wget(l, gname)
                for cc in range(4):
                    ps, pk = proj_fm(wv, cc, [("wslot", ws)])
                    ACT(big[:, kcb + cc, 0:T], ps[:, 0:T], AF.Silu, [pk], [("big", kcb + cc)])
                    PP.free((ps, pk))
                wfree(ws)
            QK = {}
            rws = {}
            rctx = {}

            def rot_a(i, l=l):
                gname = ("rq", "rk")[i // 4]
                h = i % 4
                if h == 0:
                    rws[gname] = wget(l, gname)
                ws, wv = rws[gname]
                ps, pk = proj_fm(wv, h, [("wslot", ws)])
                q32, q32k = FP.alloc()
                ACT(q32[:, 0:T], ps[:, 0:T], AF.Copy, [pk], [q32k])
                PP.free((ps, pk))
                rctx[i] = (q32, q32k)
                if h == 3:
                    wfree(ws)

            def rot_b(i):
                gname = ("rq", "rk")[i // 4]
                tab = (qd, kd)[i // 4]
                h = i % 4
                q32, q32k = rctx.pop(i)
                pq, pqk = PP.alloc()
                S.tag = "ret_perm"
                MM(pq[:, 0:T], permT[:, :], q32[:, 0:T], True, True, [("permT",), q32k], [pqk])
                S.tag = "ret"
                t1, t1k = FP.alloc()
                TT_("pool", t1[:, 0:T], q32[:, 0:T], rotc[:, 0:T], ALU.mult, [q32k, ("rot",)], [t1k])
                t2, t2k = FP.alloc()
                TT_("dve", t2[:, 0:T], pq[:, 0:T], rots[:, 0:T], ALU.mult, [pqk, ("rot",)], [t2k])
                PP.free((pq, pqk))
                FP.free((q32, q32k))
                TT_("pool", t1[:, 0:T], t1[:, 0:T], t2[:, 0:T], ALU.add, [t1k, t2k], [t1k])
                FP.free((t2, t2k))
                o, ok = BP.alloc()
                ov = o[:, 0:T].rearrange("p (b t) -> p b t", t=tb)
                tv = t1[:, 0:T].rearrange("p (b t) -> p b t", t=tb)
                TT_("pool", ov, tv, tab[:, h, 0:tb].unsqueeze(1).broadcast_to([128, NB, tb]), ALU.mult,
                    [t1k, ("rtabq",), ("rtabk",)], [ok])
                FP.free((t1, t1k))
                QK[(gname, h)] = (o, ok)

            pipeline(8, [rot_a, rot_b])
            for gname, coff in (("rv0", 0), ("rv1", 512)):
                ws, wv = wget(l, gname)
                proj_tm_to_V(wv, ws, coff)
                wfree(ws)
            lctx = {}

            def ret_b(h, l=l):
                QT, qk_ = QK[("rq", h)]
                KT, kk_ = QK[("rk", h)]
                gL = float(np.exp(lg[h] * Lblk))
                lctx[h] = la_phase1(QT, qk_, KT, kk_, KT, kk_, gL, h * 256, 256,
                                    retmask[:tb, h, 0:tb], ("rmask",), Lblk, S_ret[l][:, h, :], ("S_ret", l, h),
                                    lambda si, gL=gL: gL, (), h % 2)
                BP.free((KT, kk_))

            def ret_c(h):
                c = lctx[h]
                c["outs"], c["sqs"] = la_phase2(c)
                BP.free((c["QT"], c["qk"]))

            def ret_d(h):
                c = lctx.pop(h)
                postproc(c["outs"], c["sqs"], 2 * h, 256)

            pipeline(4, [ret_b, ret_c, ret_d])

            S.tag = "hgrn"
            for gname, kcb in (("hgate0", 8), ("hgate1", 12)):
                ws, wv = wget(l, gname)
                for cc in range(4):
                    ps, pk = proj_fm(wv, cc, [("wslot", ws)])
                    ACT(big[:, kcb + cc, 0:T], ps[:, 0:T], AF.Silu, [pk], [("big", kcb + cc)])
                    PP.free((ps, pk))
                wfree(ws)
            for gname, coff in (("hi0", 0), ("hi1", 512)):
                ws, wv = wget(l, gname)
                proj_tm_to_V(wv, ws, coff)
                wfree(ws)
            hq = {}
            hws = {}
            hctx = {}
            seg = min(64, T)
            nsg = T // seg

            def hg_a(hp, l=l):
                H2 = []
                for h in (2 * hp, 2 * hp + 1):
                    hh, cc = h // 4, h % 4
                    if cc == 0:
                        ws, wv = wget(l, f"hq{hh}")
                        for c2 in range(4):
                            ps, pk = proj_fm(wv, c2, [("wslot", ws)])
                            q, qk_ = FP.alloc()
                            ACT(q[:, 0:T], ps[:, 0:T], AF.Silu, [pk], [qk_])
                            PP.free((ps, pk))
                            hq[hh * 4 + c2] = (q, qk_)
                        wfree(ws)
                        hws[hh] = wget(l, f"hf{hh}")
                    ws, wv = hws[hh]
                    ps, pk = proj_fm(wv, cc, [("wslot", ws)])
                    if cc == 3:
                        wfree(ws)
                    f, fk = FP.alloc()
                    k1, k1k = FP.alloc()
                    B, Bk = FP.alloc()
                    H2.append(dict(h=h, ps=ps, pk=pk, f=f, fk=fk, k1=k1, k1k=k1k, B=B, Bk=Bk, eb=ebc[h % 4]))
                for c in H2:
                    ACT(c["f"][:, 0:T], c["ps"][:, 0:T], AF.Sigmoid, [c["pk"]], [c["fk"]])
                    PP.free((c["ps"], c["pk"]))
                for c in H2:
                    h = c["h"]
                    TS("dve", c["f"][:, 0:T], c["f"][:, 0:T], omlt[:, l, h:h + 1], lbt[:, l, h:h + 1], ALU.mult, ALU.add,
                       [c["fk"], ("omlt",), ("lbt",)], [c["fk"]])
                for c in H2:
                    TS("pool", c["k1"][:, 0:T], c["f"][:, 0:T], -1.0, 1.0, ALU.mult, ALU.add, [c["fk"]], [c["k1k"]])
                for c in H2:
                    TS("pool", c["f"][:, 0:T], c["f"][:, 0:T], 1e-6, None, ALU.max, None, [c["fk"]], [c["fk"]])
                for c in H2:
                    ACT(c["f"][:, 0:T], c["f"][:, 0:T], AF.Ln, [c["fk"]], [c["fk"]])
                for c in H2:
                    B, f = c["B"], c["f"]
                    S.op("dve", lambda e, B=B, f=f: e.tensor_tensor_scan(out=B[:, 0:T], data0=scanmask[:, 0:T], data1=f[:, 0:T],
                                                                      initial=0.0, op0=ALU.mult, op1=ALU.add),
                         [c["fk"], ("scanmask",)], [c["Bk"]], cost=0.12 + 2 * T / 960.0)
                for c in H2:
                    ACT(c["f"][:, 0:T], c["B"][:, 0:T], AF.Exp, [c["Bk"]], [c["fk"]])
                for c in H2:
                    ACT(c["B"][:, 0:T], c["B"][:, 0:T], AF.Exp, [c["Bk"]], [c["Bk"]], scale=-1.0)
                for c in H2:
                    h = c["h"]
                    Ev = c["f"][:, 0:T].rearrange("p (c j) -> p c j", j=seg)
                    CP("pool", c["eb"][:, 0:nsg], Ev[:, :, seg - 1], [c["fk"]], [("ebc", h % 4)])
                    q, qk_ = hq.pop(h)
                    QT, QTk = BP.alloc()
                    TT_("pool", QT[:, 0:T], q[:, 0:T], c["f"][:, 0:T], ALU.mult, [qk_, c["fk"]], [QTk])
                    FP.free((q, qk_))
                    c["QT"], c["QTk"] = QT, QTk
                for c in H2:
                    KT, KTk = BP.alloc()
                    TT_("dve", KT[:, 0:T], c["k1"][:, 0:T], c["B"][:, 0:T], ALU.mult, [c["k1k"], c["Bk"]], [KTk])
                    FP.free((c["k1"], c["k1k"]))
                    FP.free((c["B"], c["Bk"]))
                    c["KT"], c["KTk"] = KT, KTk
                for c in H2:
                    Ev = c["f"][:, 0:T].rearrange("p (c j) -> p c j", j=seg)
                    KE, KEk = BP.alloc()
                    TT_("pool", KE[:, 0:T].rearrange("p (c j) -> p c j", j=seg), c["KT"][:, 0:T].rearrange("p (c j) -> p c j", j=seg),
                        Ev[:, :, seg - 1:seg].broadcast_to([128, nsg, seg]), ALU.mult, [c["KTk"], c["fk"]], [KEk])
                    FP.free((c["f"], c["fk"]))
                    hctx[c["h"]] = (c["QT"], c["QTk"], c["KT"], c["KTk"], KE, KEk, c["eb"])

            def hg_b(h, l=l):
                QT, QTk, KT, KTk, KE, KEk, eb = hctx[h]
                hctx[h] = la_phase1(QT, QTk, KT, KTk, KE, KEk, 1.0, h * 128, 128,
                                    hgmask[:tb, 0:tb], ("hgmask",), seg, S_hg[l][:, h, :], ("S_hg", l, h),
                                    lambda si, eb=eb: eb[:, si:si + 1], [("ebc", h % 4)], h % 2)
                BP.free((KT, KTk))
                BP.free((KE, KEk))

            def hg_c(h):
                c = hctx[h]
                c["outs"], c["sqs"] = la_phase2(c)
                BP.free((c["QT"], c["qk"]))

            def hg_d(h):
                c = hctx.pop(h)
                postproc(c["outs"], c["sqs"], 8 + h, 128)

            def hg_a1(h):
                if h % 2 == 0:
                    hg_a(h // 2)

            pipeline(8, [hg_a1, hg_b, hg_c, hg_d])

            S.tag = "rglru"
            cidx = 0
            for gname, ncc in (("ry0", 4), ("ry1", 4), ("ry2", 2)):
                ws, wv = wget(l, gname)
                for cc in range(ncc):
                    ps, pk = proj_fm(wv, cc, [("wslot", ws)])
                    ACT(big[:, 16 + cidx, 0:T], ps[:, 0:T], AF.Gelu_apprx_tanh, [pk], [("big", 16 + cidx)])
                    PP.free((ps, pk))
                    cidx += 1
                wfree(ws)
            gws, _ = wget(l, "rgw")
            gwv = wslot[gws][:, 0:5120].rearrange("p (g n k e) -> p g n k e", g=2, n=5, k=2)
            ruw = {}
            rgx = {}

            def rg_a(n, l=l):
                for c in (2 * n, 2 * n + 1):
                    gi_, cc = c // 4, c % 4
                    if cc == 0:
                        ruw[gi_] = wget(l, f"ru{gi_}")
                    ws, wv = ruw[gi_]
                    ps, pk = proj_fm(wv, cc, [("wslot", ws)])
                    if c == 9 or cc == 3:
                        wfree(ws)
                    ub = ubuf[c % 2]
                    ubk = ("ubuf", c % 2)
                    CP("pool", ub[:, 0:3], uhalo[l][:, c, :], [("uhalo", l, c)], [ubk])
                    ACT(ub[:, 3:3 + T], ps[:, 0:T], AF.Copy, [pk], [ubk])
                    PP.free((ps, pk))
                    CP("pool", uhalo[l][:, c, :], ub[:, T:T + 3], [ubk], [("uhalo", l, c)])
                    cw = smc(f"rcw{l}", 4 * c, 4)
                    t, tk = FP.alloc()
                    ACT(t[:, 0:T], ub[:, 0:T], AF.Identity, [ubk, K_SM], [tk], scale=cw[:, 0:1], bias=smc(f"rcb{l}", c, 1))
                    STT(t[:, 0:T], ub[:, 1:1 + T], cw[:, 1:2], t[:, 0:T], ALU.mult, ALU.add, [ubk, tk, K_SM], [tk])
                    STT(t[:, 0:T], ub[:, 2:2 + T], cw[:, 2:3], t[:, 0:T], ALU.mult, ALU.add, [ubk, tk, K_SM], [tk])
                    STT(t[:, 0:T], ub[:, 3:3 + T], cw[:, 3:4], t[:, 0:T], ALU.mult, ALU.add, [ubk, tk, K_SM], [tk])
                    xb_, xbk = BP.alloc()
                    CP("pool", xb_[:, 0:T], t[:, 0:T], [tk], [xbk])
                    rgx[c] = (t, tk, xb_, xbk)

            def rg_b(n, l=l):
                pend = [rgx.pop(2 * n), rgx.pop(2 * n + 1)]
                gps = []
                for ei in range(2):
                    rps, rpk = PP.alloc()
                    ips, ipk = PP.alloc()
                    for kc in range(2):
                        MM(rps[:, 0:T], gwv[:, 0, n, kc, ei * 128:(ei + 1) * 128], pend[kc][2][:, 0:T], kc == 0, kc == 1,
                           [("wslot", gws), pend[kc][3]], [rpk])
                    for kc in range(2):
                        MM(ips[:, 0:T], gwv[:, 1, n, kc, ei * 128:(ei + 1) * 128], pend[kc][2][:, 0:T], kc == 0, kc == 1,
                           [("wslot", gws), pend[kc][3]], [ipk])
                    gps.append((rps, rpk, ips, ipk))
                E2 = []
                for ei in range(2):
                    e_ = 2 * n + ei
                    xc, xck = pend[ei][0], pend[ei][1]
                    rps, rpk, ips, ipk = gps[ei]
                    r, rk_ = FP.alloc()
                    ig, igk = FP.alloc()
                    a, ak = FP.alloc()
                    E2.append(dict(e_=e_, xc=xc, xck=xck, rps=rps, rpk=rpk, ips=ips, ipk=ipk, r=r, rk=rk_, ig=ig, igk=igk, a=a, ak=ak))
                for c in E2:
                    ACT(c["r"][:, 0:T], c["rps"][:, 0:T], AF.Sigmoid, [c["rpk"], K_SM], [c["rk"]], bias=smc(f"rbr{l}", c["e_"], 1))
                    PP.free((c["rps"], c["rpk"]))
                for c in E2:
                    ACT(c["ig"][:, 0:T], c["ips"][:, 0:T], AF.Sigmoid, [c["ipk"], K_SM], [c["igk"]], bias=smc(f"rbi{l}", c["e_"], 1))
                    PP.free((c["ips"], c["ipk"]))
                for c in E2:
                    e_ = c["e_"]
                    ACT(c["a"][:, 0:T], c["r"][:, 0:T], AF.Exp, [c["rk"], ("rgc",)], [c["ak"]], scale=rgc[:, l, e_:e_ + 1])
                for c in E2:
                    STT(c["r"][:, 0:T], c["a"][:, 0:T], -1.0, c["a"][:, 0:T], ALU.mult, ALU.mult, [c["ak"]], [c["rk"]])
                for c in E2:
                    TT_("pool", c["ig"][:, 0:T], c["ig"][:, 0:T], c["xc"][:, 0:T], ALU.mult, [c["igk"], c["xck"]], [c["igk"]])
                for c in E2:
                    TS("pool", c["r"][:, 0:T], c["r"][:, 0:T], -1.0, None, ALU.max, None, [c["rk"]], [c["rk"]])
                for c in E2:
                    ACT(c["r"][:, 0:T], c["r"][:, 0:T], AF.Sqrt, [c["rk"]], [c["rk"]], bias=1.0, scale=1.0)
                    if grp == 0 and first:
                        MEMSET("pool", c["r"][:, 0:1], 1.0, [c["rk"]])
                for c in E2:
                    TT_("dve", c["ig"][:, 0:T], c["ig"][:, 0:T], c["r"][:, 0:T], ALU.mult, [c["igk"], c["rk"]], [c["igk"]])
                    FP.free((c["r"], c["rk"]))
                for c in E2:
                    e_ = c["e_"]
                    hk = ("hstate", l, e_)
                    xc, a, ig = c["xc"], c["a"], c["ig"]
                    S.op("dve", lambda e, xc=xc, a=a, ig=ig, e_=e_, l=l: e.tensor_tensor_scan(
                        out=xc[:, 0:T], data0=a[:, 0:T], data1=ig[:, 0:T], initial=hstate[l][:, e_:e_ + 1],
                        op0=ALU.mult, op1=ALU.add), [c["ak"], c["igk"], hk], [c["xck"]], cost=0.12 + 2 * T / 960.0)
                    FP.free((c["a"], c["ak"]))
                    FP.free((c["ig"], c["igk"]))
                for c in E2:
                    e_ = c["e_"]
                    hk = ("hstate", l, e_)
                    CP("pool", hstate[l][:, e_:e_ + 1], c["xc"][:, T - 1:T], [c["xck"]], [hk])
                    TT_("pool", big[:, 16 + e_, 0:T], c["xc"][:, 0:T], big[:, 16 + e_, 0:T], ALU.mult,
                        [c["xck"], ("big", 16 + e_)], [("big", 16 + e_)])
                for (xc, xck, xb2, xbk2) in pend:
                    FP.free((xc, xck))
                    BP.free((xb2, xbk2))

            pipeline(5, [rg_a, rg_b])
            wfree(gws)

            S.tag = "merge"
            V_ALL = [("V", b) for b in range(4)]
            bkc = [8, 8, 10]
            bk0 = [0, 8, 16]
            acc = [FP.alloc() for _ in range(8)]
            for b in range(3):
                for jh in range(2):
                    gs, gv = wget(l, f"mg{b}{jh}")
                    bs, bv = wget(l, f"wb{b}{jh}")
                    for cc in range(4):
                        j = jh * 4 + cc
                        gps, gpk = proj_fm(gv, cc, [("wslot", gs)])
                        sg, sgk = FP.alloc()
                        ACT(sg[:, 0:T], gps[:, 0:T], AF.Sigmoid, [gpk], [sgk])
                        PP.free((gps, gpk))
                        pps, ppk = PP.alloc()
                        for kc in range(bkc[b]):
                            MM(pps[:, 0:T], bv[:, kc, cc * 128:(cc + 1) * 128], big[:, bk0[b] + kc, 0:T], kc == 0, kc == bkc[b] - 1,
                               [("wslot", bs), ("big", bk0[b] + kc)], [ppk])
                        am, amk = acc[j]
                        if b == 0:
                            TT_("dve", am[:, 0:T], pps[:, 0:T], sg[:, 0:T], ALU.mult, [ppk, sgk], [amk])
                        else:
                            TT_("dve", sg[:, 0:T], pps[:, 0:T], sg[:, 0:T], ALU.mult, [ppk, sgk], [sgk])
                            if b == 1:
                                TT_("pool", am[:, 0:T], am[:, 0:T], sg[:, 0:T], ALU.add, [amk, sgk], [amk])
                            else:
                                mxv = Vt[:].rearrange("p b c -> p (b c)").rearrange("p (k t) -> p k t", t=TT)
                                TT_("pool", mxv[:, j, 0:T], am[:, 0:T], sg[:, 0:T], ALU.add, [amk, sgk] + V_ALL, V_ALL + [("mx", j)])
                        PP.free((pps, ppk))
                        FP.free((sg, sgk))
                    wfree(gs)
                    wfree(bs)
            for a_ in acc:
                FP.free(a_)
            mxv = Vt[:].rearrange("p b c -> p (b c)").rearrange("p (k t) -> p k t", t=TT)

            S.tag = "wout"
            for half in range(2):
                ws, wv = wget(l, f"wo{half}")
                for b in range(NB):
                    ps, pk = PP.alloc()
                    for kc in range(8):
                        MM(ps[:tb, 0:512], mxv[:, kc, b * tb:(b + 1) * tb], wv[:, kc, 0:512], kc == 0, kc == 7,
                           V_ALL + [("wslot", ws)], [pk])
                    TT_("dve", xt[:tb, b, half * 512:(half + 1) * 512], ps[:tb, 0:512], xt[:tb, b, half * 512:(half + 1) * 512],
                        ALU.add, [pk, ("x", b)], [("x", b)])
                    PP.free((ps, pk))
                wfree(ws)

            S.tag = "norm2"
            norm_to_hnT(l)
            S.tag = "ffn_up"
            for i in range(11):
                ws, wv = wget(l, f"wu{i}")
                for cc in range(2):
                    c = 2 * i + cc
                    aps, apk = proj_fm(wv, cc, [("wslot", ws)])
                    gps, gpk = proj_fm(wv, 2 + cc, [("wslot", ws)])
                    ab = abuf[c % 2]
                    abk = ("abuf", c % 2)
                    CP("pool", ab[:, 0:2], ahalo[l][:, c, :], [("ahalo", l, c)], [abk])
                    ACT(ab[:, 2:2 + T], aps[:, 0:T], AF.Copy, [apk], [abk])
                    CP("pool", ahalo[l][:, c, :], ab[:, T:T + 2], [abk], [("ahalo", l, c)])
                    cw = smc(f"fcw{l}", 3 * c, 3)
                    t, tk = FP.alloc()
                    ACT(t[:, 0:T], ab[:, 0:T], AF.Identity, [abk, K_SM], [tk], scale=cw[:, 0:1])
                    STT(t[:, 0:T], ab[:, 1:1 + T], cw[:, 1:2], t[:, 0:T], ALU.mult, ALU.add, [abk, tk, K_SM], [tk])
                    STT(t[:, 0:T], aps[:, 0:T], cw[:, 2:3], t[:, 0:T], ALU.mult, ALU.add, [apk, tk, K_SM], [tk])
                    PP.free((aps, apk))
                    ACT(t[:, 0:T], t[:, 0:T], AF.Gelu_apprx_tanh, [tk, K_SM], [tk], bias=smc(f"fcb{l}", c, 1))
                    TT_("dve", big[:, c, 0:T], gps[:, 0:T], t[:, 0:T], ALU.mult, [gpk, tk], [("big", c)])
                    PP.free((gps, gpk))
                    FP.free((t, tk))
                wfree(ws)
            S.tag = "ffn_down"
            kgs = [(0, 8), (8, 8), (16, 6)]
            accs = [PP.alloc() for _ in range(NB)]
            for kg, (k0, kn) in enumerate(kgs):
                ws, wv = wget(l, f"wd0{kg}")
                for b in range(NB):
                    ps, pk = accs[b]
                    for kc in range(kn):
                        MM(ps[:tb, 0:512], big[:, k0 + kc, b * tb:(b + 1) * tb], wv[:, kc, 0:512],
                           (kg == 0 and kc == 0), (kg == 2 and kc == kn - 1), [("big", k0 + kc), ("wslot", ws)], [pk])
                wfree(ws)
            for b in range(NB):
                ps, pk = accs[b]
                TT_("dve", xt[:tb, b, 0:512], ps[:tb, 0:512], xt[:tb, b, 0:512], ALU.add, [pk, ("x", b)], [("x", b)])
                PP.free((ps, pk))
            wds = [wget(l, f"wd1{kg}") for kg in range(3)]
            for b in range(NB):
                ps, pk = PP.alloc()
                for kg, (k0, kn) in enumerate(kgs):
                    ws, wv = wds[kg]
                    for kc in range(kn):
                        MM(ps[:tb, 0:512], big[:, k0 + kc, b * tb:(b + 1) * tb], wv[:, kc, 0:512],
                           (kg == 0 and kc == 0), (kg == 2 and kc == kn - 1), [("big", k0 + kc), ("wslot", ws)], [pk])
                TT_("dve", xt[:tb, b, 512:1024], ps[:tb, 0:512], xt[:tb, b, 512:1024], ALU.add, [pk, ("x", b)], [("x", b)])
                PP.free((ps, pk))
            for ws, wv in wds:
                wfree(ws)

        S.tag = "final"
        for b in range(NB):
            ACT(junk[:tb, :], xt[:tb, b, :], AF.Square, [("x", b)], [("junk",), ("stat", b)], accum=stat[:tb, b:b + 1])
            ACT(stat[:tb, 4 + b:5 + b], stat[:tb, b:b + 1], AF.Sqrt, [("stat", b), ("epsb",)], [("stat2", b)],
                bias=epsb[:tb, 0:1], scale=1.0 / D)
            S.op("dve", lambda e, b=b: e.reciprocal(out=stat[:tb, 8 + b:9 + b], in_=stat[:tb, 4 + b:5 + b]),
                 [("stat2", b)], [("stat3", b)], cost=0.15)
            STT(xt[:tb, b, :], xt[:tb, b, :], stat[:tb, 8 + b:9 + b], wfin[:tb, :], ALU.mult, ALU.mult,
                [("x", b), ("stat3", b), ("wfin",)], [("x", b)])
            DMA(ydst[b * tb:(b + 1) * tb, :], xt[:tb, b, :], [("x", b)], [], sem_y[b])

    ntiles_total = NTILE + 1
    for _ in range(ntiles_total):
        for l in range(DEPTH):
            for gidx in range(NG):
                wseq.append((l, gidx))

    def load_group_consts(grp):
        DMA(retmask[:], retmask_d[grp], (), [("rmask",)], sem_gc[0])
        DMA(qd[:], qd_d[grp], (), [("rtabq",)], sem_gc[1])
        DMA(kd[:], kd_d[grp], (), [("rtabk",)], sem_gc[2])
        DMA(hgmask[:], hgmask_d[grp], (), [("hgmask",)], sem_gc[3])
        DMA(scanmask[:], scanmask_d[grp], (), [("scanmask",)], sem_gc[4])

    def state_keys(l):
        return [("S_ret", l, h) for h in range(4)], [("S_hg", l, h) for h in range(8)], \
               [("hstate", l, e) for e in range(10)], [("uhalo", l, c) for c in range(10)], [("ahalo", l, c) for c in range(22)]

    def store_states(grp):
        for l in range(DEPTH):
            kr, kh, ks, ku, ka = state_keys(l)
            DMA(ret_o[grp, l].rearrange("h k v -> k h v"), S_ret[l][:], kr, [], sem_sto[2 * l])
            DMA(hg_o[grp, l].rearrange("h k v -> k h v"), S_hg[l][:], kh, [], sem_sto[2 * l + 1])
            o0 = (grp * DEPTH + l) * OUT_W
            CP("pool", outst[:, o0:o0 + 10], hstate[l][:], ks, [("outst",)])
            CP("pool", outst[:, o0 + 10:o0 + 40], uhalo[l][:].rearrange("p c j -> p (c j)"), ku, [("outst",)])
            CP("pool", outst[:, o0 + 40:o0 + 84], ahalo[l][:].rearrange("p c j -> p (c j)"), ka, [("outst",)])

    load_group_consts(0)
    for l in range(DEPTH):
        kr, kh, ks, ku, ka = state_keys(l)
        MEMSET("pool", S_ret[l][:], 0.0, kr)
        MEMSET("pool", S_hg[l][:], 0.0, kh)
        MEMSET("pool", hstate[l][:], 0.0, ks)
        MEMSET("pool", uhalo[l][:], 0.0, ku)
        MEMSET("pool", ahalo[l][:], 0.0, ka)
    for ti in range(NTILE):
        emit_tile(0, ti, TT, x_p[ti * TT:(ti + 1) * TT, :], y_p[ti * TT:(ti + 1) * TT, :])
    store_states(0)
    load_group_consts(1)
    for l in range(DEPTH):
        kr, kh, ks, ku, ka = state_keys(l)
        DMA(S_ret[l][:], st_ret[l].rearrange("h k v -> k h v"), (), kr, sem_sti[2 * l])
        DMA(S_hg[l][:], st_hg[l].rearrange("h k v -> k h v"), (), kh, sem_sti[2 * l + 1])
        CP("pool", hstate[l][:], smc(f"h0{l}"), [K_SM], ks)
        CP("pool", uhalo[l][:].rearrange("p c j -> p (c j)"), smc(f"u0{l}"), [K_SM], ku)
        CP("pool", ahalo[l][:].rearrange("p c j -> p (c j)"), smc(f"a0{l}"), [K_SM], ka)
    emit_tile(1, 0, DSEQ, x_s, y_s)
    store_states(1)
    DMA(small_o[:, :], outst[:], [("outst",)], [], sem_out)

    build_program.last_sched = S
    S.reorder()
    block = es.enter_context(nc.Block())
    S.finalize(nc, engsems, block)
    es.close()
    return nc


def _fm(v, nch):
    return np.ascontiguousarray(np.asarray(v, np.float32).reshape(nch, 128).T)


def _consts(SEQ):
    c = {}
    c["ident"] = np.eye(128, dtype=np.float32)
    P = np.zeros((128, 128), np.float32)
    for p in range(64):
        P[p + 64, p] = -1.0
    for p in range(64, 128):
        P[p - 64, p] = 1.0
    c["permT"] = P
    half = 64
    inv = np.power(np.float32(10000.0), -np.arange(half, dtype=np.float32) / np.float32(half)).astype(np.float32)
    inv2 = np.concatenate([inv, inv])

    def rot(pos0, n):
        pos = (np.arange(n, dtype=np.float32) + np.float32(pos0)).astype(np.float32)
        ang = (inv2[:, None] * pos[None, :]).astype(np.float32)
        return np.cos(ang).astype(np.float32), np.sin(ang).astype(np.float32)

    c["rotc_p"], c["rots_p"] = rot(0, SEQ)
    c["rotc_s"], c["rots_s"] = rot(PAST, DSEQ)
    lg = np.log1p(-np.power(2.0, -5.0 - np.arange(4, dtype=np.float64)))
    retmask = np.zeros((2, 128, 4, 128), np.float32)
    qd = np.zeros((2, 128, 4, 128), np.float32)
    kd = np.zeros((2, 128, 4, 128), np.float32)
    for grp, L in ((0, 128), (1, DSEQ)):
        n = np.arange(L)
        for h in range(4):
            qd[grp, :, h, :L] = np.exp(lg[h] * (n + 1.0))[None, :]
            kd[grp, :, h, :L] = (np.exp(-lg[h] * (n + 1.0)) * (128 ** -0.5))[None, :]
            m = n[:, None]
            t = n[None, :]
            samechunk = (m // 64) == (t // 64)
            Dm = np.where(samechunk, np.where(t >= m, 1.0, np.exp(lg[h] * 2.0 * (m - t))), np.where(t > m, 1.0, 0.0))
            retmask[grp, :L, h, :L] = Dm
    c["retmask"], c["qd"], c["kd"] = retmask, qd, kd
    hgmask = np.zeros((2, 128, 128), np.float32)
    n = np.arange(128)
    hgmask[0] = (((n[:, None] // 64) == (n[None, :] // 64)) & (n[:, None] <= n[None, :])).astype(np.float32)
    hgmask[1, :DSEQ, :DSEQ] = (n[:DSEQ, None] <= n[None, :DSEQ]).astype(np.float32)
    c["hgmask"] = hgmask
    sm = np.ones((2, 128, TT), np.float32)
    sm[0, :, 0::64] = 0.0
    sm[1, :, 0] = 0.0
    c["scanmask"] = sm
    return c


_CACHE = {}


def kernel(x_prompt, x_sample, state_ret, state_hgrn, state_rglru, cache_rg_conv, cache_ffn_conv,
           norm1_w, w_in, w_branch, w_out, rg_conv_w, rg_conv_b, rg_w_r, rg_b_r, rg_w_i, rg_b_i,
           rg_lambda, hg_lb, hg_norm_w, norm2_w, w_up, ffn_conv_w, ffn_conv_b, w_down, final_norm_w):
    f32 = np.float32
    x_prompt = np.asarray(x_prompt, f32)
    B, SEQ, _ = x_prompt.shape
    ncore = 8
    assert B == ncore
    if SEQ not in _CACHE:
        _CACHE[SEQ] = (build_program(SEQ), _consts(SEQ))
    nc, consts = _CACHE[SEQ]

    shared = {
        "w_in": np.ascontiguousarray(w_in, f32), "w_branch": np.ascontiguousarray(w_branch, f32),
        "w_out": np.ascontiguousarray(w_out, f32), "w_up": np.ascontiguousarray(w_up, f32),
        "w_down": np.ascontiguousarray(w_down, f32), "rg_w_r": np.ascontiguousarray(rg_w_r, f32),
        "rg_w_i": np.ascontiguousarray(rg_w_i, f32),
        "wfin": np.ascontiguousarray(np.broadcast_to(np.asarray(final_norm_w, f32)[None, :], (128, D))),
    }
    shared.update(consts)
    in_maps = []
    for b in range(ncore):
        sm = np.zeros((128, SM_N), f32)

        def put(name, arr):
            o, w = SM_OFF[name]
            sm[:, o:o + w] = np.asarray(arr, f32).reshape(128, w)

        put("lb0", _fm(hg_lb[0], 8))
        put("lb1", _fm(hg_lb[1], 8))
        for l in range(DEPTH):
            put(f"rcw{l}", np.asarray(rg_conv_w[l], f32).reshape(4, 10, 128).transpose(2, 1, 0))
            put(f"rcb{l}", _fm(rg_conv_b[l], 10))
            put(f"rbr{l}", _fm(rg_b_r[l], 10))
            put(f"rbi{l}", _fm(rg_b_i[l], 10))
            put(f"rlam{l}", _fm(rg_lambda[l], 10))
            put(f"fcw{l}", np.asarray(ffn_conv_w[l], f32).reshape(3, 22, 128).transpose(2, 1, 0))
            put(f"fcb{l}", _fm(ffn_conv_b[l], 22))
            put(f"n1{l}", _fm(norm1_w[l], 8))
            put(f"n2{l}", _fm(norm2_w[l], 8))
            put(f"hgn{l}", _fm(hg_norm_w[l], 8))
            put(f"h0{l}", _fm(state_rglru[l, b], 10))
            put(f"u0{l}", np.asarray(cache_rg_conv[l, b], f32).reshape(3, 10, 128).transpose(2, 1, 0))
            put(f"a0{l}", np.asarray(cache_ffn_conv[l, b], f32).reshape(2, 22, 128).transpose(2, 1, 0))
        m = dict(shared)
        m["x_p"] = np.ascontiguousarray(x_prompt[b])
        m["x_s"] = np.ascontiguousarray(x_sample[b], f32)
        m["st_ret"] = np.ascontiguousarray(state_ret[:, b], f32)
        m["st_hg"] = np.ascontiguousarray(state_hgrn[:, b], f32)
        m["smalls"] = sm
        in_maps.append(m)

    res = run_bass_kernel_spmd(nc, in_maps, core_ids=list(range(ncore)))
    R = res.results
    y_p = np.stack([R[b]["y_p"] for b in range(ncore)], 0)
    y_s = np.stack([R[b]["y_s"] for b in range(ncore)], 0)
    ret = np.stack([R[b]["ret_o"] for b in range(ncore)], 0)
    hg = np.stack([R[b]["hg_o"] for b in range(ncore)], 0)
    so = np.stack([R[b]["small_o"] for b in range(ncore)], 0)
    so = so.reshape(ncore, 128, 2, DEPTH, OUT_W)
    outs = [y_p.astype(f32), y_s.astype(f32)]
    outs.append(np.ascontiguousarray(ret[:, 0].transpose(1, 0, 2, 3, 4)))
    outs.append(np.ascontiguousarray(ret[:, 1].transpose(1, 0, 2, 3, 4)))
    outs.append(np.ascontiguousarray(hg[:, 0].transpose(1, 0, 2, 3, 4)))
    outs.append(np.ascontiguousarray(hg[:, 1].transpose(1, 0, 2, 3, 4)))
    for grp in range(2):
        pass
    hs = so[..., 0:10]
    uh = so[..., 10:40].reshape(ncore, 128, 2, DEPTH, 10, 3)
    ah = so[..., 40:84].reshape(ncore, 128, 2, DEPTH, 22, 2)
    for grp in range(2):
        pass
    rgl = [np.ascontiguousarray(hs[:, :, g].transpose(2, 0, 3, 1).reshape(DEPTH, ncore, RGW)) for g in range(2)]
    rgc_ = [np.ascontiguousarray(uh[:, :, g].transpose(2, 0, 4, 3, 1).reshape(DEPTH, ncore, 3, RGW)) for g in range(2)]
    ffc = [np.ascontiguousarray(ah[:, :, g].transpose(2, 0, 4, 3, 1).reshape(DEPTH, ncore, 2, DFF)) for g in range(2)]
    outs += [rgl[0], rgl[1], rgc_[0], rgc_[1], ffc[0], ffc[1]]
    return tuple(o.astype(f32) for o in outs)
```

```python
import numpy as np
from contextlib import ExitStack
import concourse.bass as bass
import concourse.mybir as mybir
from concourse.bass_utils import run_bass_kernel_spmd

F32 = mybir.dt.float32
BF16 = mybir.dt.bfloat16
AF = mybir.ActivationFunctionType
ALU = mybir.AluOpType

D = 1024
DEPTH = 2
SEQ_FULL = 8192
DSEQ = 16
PAST = 2048
TT = 512
INW = 12800
DFF = 2816
RGW = 1280
EPS = 1e-6
SLOT = 5120
NSLOT = 4


class Ins:
    __slots__ = ("q", "fn", "deps", "needed", "sem", "semname", "val", "dma", "inc", "tag", "cost", "odeps", "idx", "tset", "nbytes", "evac")


class Sched:
    def __init__(self):
        self.queues = {k: [] for k in ("pe", "act", "dve", "pool", "sp")}
        self.lw = {}
        self.rd = {}
        self.n = 0
        self.tag = "setup"
        self.sub = ""
        self.rawids = {}

    def op(self, q, fn, reads=(), writes=(), dma_sem=None, inc=16, cost=0.5, tset=None, nbytes=0, evac=False):
        ins = Ins()
        ins.evac = evac
        ins.cost = cost
        ins.tset = tset
        ins.nbytes = nbytes
        ins.odeps = []
        ins.idx = self.n
        ins.tag = self.tag + ((":" + self.sub) if self.sub else "")
        ins.q = q
        ins.fn = fn
        ins.needed = False
        ins.dma = dma_sem is not None
        ins.sem = dma_sem[1] if dma_sem else None
        ins.semname = dma_sem[0] if dma_sem else q
        ins.val = None
        ins.inc = inc
        deps = []
        for k in reads:
            w = self.lw.get(k)
            if w is not None:
                deps.append(w)
        nraw = len(deps)
        self._nraw = nraw
        for k in writes:
            w = self.lw.get(k)
            if w is not None:
                deps.append(w)
            r = self.rd.get(k)
            if r:
                deps.extend(r)
        dd = []
        seen = set()
        rawids = set(id(d) for d in deps[:nraw])
        self.rawids[id(ins)] = rawids
        for d in deps:
            if id(d) in seen or d is ins:
                continue
            seen.add(id(d))
            if q == "pe" and d.q == "pe" and not d.dma:
                ins.odeps.append(d)
                continue
            d.needed = True
            dd.append(d)
        ins.deps = dd
        for k in writes:
            self.lw[k] = ins
            self.rd[k] = []
        self.n += 1
        for k in reads:
            self.rd.setdefault(k, []).append(ins)
        self.queues[q].append(ins)
        return ins

    def reorder(self, window=None):
        import os as _os2
        _wp = int(_os2.environ.get("K_WPE", "256"))
        _we = int(_os2.environ.get("K_WEL", "64"))
        window = window or {"pe": _wp, "act": _we, "dve": _we, "pool": _we, "sp": 1}
        seg = {}
        pre = {}
        for q, lst in self.queues.items():
            cut = 0
            for i, ins in enumerate(lst):
                if ins.fn is None:
                    cut = i + 1
            pre[q] = lst[:cut]
            seg[q] = lst[cut:]
        inseg = set()
        for q in seg:
            for ins in seg[q]:
                inseg.add(id(ins))
        nun = {}
        users = {}
        for q in seg:
            for ins in seg[q]:
                c = 0
                for d in ins.deps + ins.odeps:
                    if id(d) in inseg:
                        c += 1
                        users.setdefault(id(d), []).append(ins)
                nun[id(ins)] = c
        done = {}
        rt = {}
        for q in seg:
            for ins in seg[q]:
                if nun[id(ins)] == 0:
                    rt[id(ins)] = 0.0
        pending = {q: list(seg[q]) for q in seg}
        tfree = {q: 0.0 for q in seg}
        out = {q: [] for q in seg}
        cur_t = [None]
        bus = [0.0]
        import os as _os
        LAT_X = float(_os.environ.get('K_LATX', '1.0'))
        LAT_S = 0.2
        EVAC_BONUS = float(_os.environ.get('K_EVAC', '1.5'))
        cand = {}
        self.idle_causes = {}
        dirty = set(seg.keys())
        total = sum(len(v) for v in seg.values())
        nsched = 0
        while nsched < total:
            for q in list(dirty):
                best = None
                lst = pending[q]
                W = window[q]
                tf = tfree[q]
                for j in range(min(W, len(lst))):
                    ins = lst[j]
                    r = rt.get(id(ins))
                    if r is None:
                        continue
                    st = r if r > tf else tf
                    key = st
                    if q == "act" and ins.tset is not None and ins.tset != cur_t[0]:
                        key = st + 1.3
                    if ins.evac:
                        key -= EVAC_BONUS
                    if best is None or key < best[0] - 1e-9:
                        best = (key, st, j, ins)
                    if key <= tf - EVAC_BONUS + 1e-9:
                        break
                cand[q] = best
            dirty.clear()
            bq = None
            for q, b in cand.items():
                if b is not None and (bq is None or b[0] < cand[bq][0]):
                    bq = q
            key, st, j, ins = cand[bq]
            tf0 = tfree[bq]
            lst = pending[bq]
            if lst[j] is not ins:
                j = lst.index(ins)
            del lst[j]
            cost = ins.cost
            if bq == "act" and ins.tset is not None and ins.tset != cur_t[0]:
                cost += 1.3
                cur_t[0] = ins.tset
            if ins.dma:
                tfree[bq] = st + 0.08
                s0 = max(bus[0], st)
                bus[0] = s0 + ins.nbytes / 180e3
                fin = bus[0] + 2.0
            else:
                fin = st + cost
                tfree[bq] = fin
            done[id(ins)] = fin
            if bq == "pe" and st - tf0 > 0.3:
                bd = None
                for d in ins.deps + ins.odeps:
                    if id(d) in done and (bd is None or done[id(d)] > done[id(bd)]):
                        bd = d
                kind = "RAW" if (bd is not None and id(bd) in self.rawids.get(id(ins), ())) else "WAR"
                k = (ins.tag, ((bd.q + "|" + bd.tag + "|" + kind) if bd is not None else "-"))
                self.idle_causes[k] = self.idle_causes.get(k, 0.0) + (st - tf0)
            out[bq].append(ins)
            nsched += 1
            dirty.add(bq)
            for u in users.get(id(ins), ()):
                nun[id(u)] -= 1
                lat = LAT_S if (u.q == ins.q and not ins.dma) else LAT_X
                v = fin + lat
                if rt.get(("p", id(u)), 0.0) < v:
                    rt[("p", id(u))] = v
                if nun[id(u)] == 0:
                    rt[id(u)] = rt.get(("p", id(u)), 0.0)
                    dirty.add(u.q)
        for q in seg:
            self.queues[q] = pre[q] + out[q]
        self.sim_makespan = max(tfree.values())

    def barrier(self):
        lasts = []
        for q in ("pe", "act", "dve", "pool"):
            lst = [i for i in self.queues[q] if i.fn is not None]
            if lst:
                lst[-1].needed = True
                lasts.append(lst[-1])
        for q in self.queues:
            ins = Ins()
            ins.evac = False
            ins.tag = "barrier"
            ins.q = q
            ins.fn = None
            ins.needed = False
            ins.dma = False
            ins.sem = None
            ins.semname = q
            ins.val = None
            ins.inc = 0
            ins.deps = list(lasts)
            self.queues[q].append(ins)

    def finalize(self, nc, engsems, block):
        dcnt = {}
        allsem = {}
        snap = None
        order = ["sp"] + [q for q in self.queues if q != "sp"]
        for q in order:
            lst = self.queues[q]
            c = 0
            for ins in lst:
                if ins.fn is None:
                    if q == "sp":
                        snap = dict(dcnt)
                        self._snaps = getattr(self, "_snaps", []) + [snap]
                    continue
                if ins.dma:
                    dcnt[ins.semname] = dcnt.get(ins.semname, 0) + ins.inc
                    ins.val = dcnt[ins.semname]
                    allsem[ins.semname] = ins.sem
                elif ins.needed:
                    c += 1
                    ins.val = c
                    ins.sem = engsems[q]

        snaps = getattr(self, "_snaps", [])

        def run(e, lst, final=False):
            waited = {}
            bi = 0
            for ins in lst:
                for d in ins.deps:
                    if waited.get(d.semname, 0) < d.val:
                        e.wait_ge(d.sem, d.val)
                        waited[d.semname] = d.val
                if ins.fn is None:
                    for name, tot in snaps[bi].items():
                        if waited.get(name, 0) < tot:
                            e.wait_ge(allsem[name], tot)
                            waited[name] = tot
                    bi += 1
                    continue
                r = ins.fn(e)
                if ins.dma:
                    r.then_inc(ins.sem, ins.inc)
                elif ins.needed:
                    r.then_inc(ins.sem, 1)
            if final:
                for name, tot in dcnt.items():
                    if waited.get(name, 0) < tot:
                        e.wait_ge(allsem[name], tot)

        qs = self.queues

        @block.sync
        def _(e):
            run(e, qs["sp"], final=True)

        @block.tensor
        def _(e):
            run(e, qs["pe"])

        @block.scalar
        def _(e):
            run(e, qs["act"])

        @block.vector
        def _(e):
            run(e, qs["dve"])

        @block.gpsimd
        def _(e):
            run(e, qs["pool"])


class BufPool:
    def __init__(self, items):
        self.free_list = list(items)
        self.total = len(items)

    def alloc(self):
        if not self.free_list:
            raise RuntimeError("pool exhausted")
        return self.free_list.pop(0)

    def free(self, b):
        self.free_list.append(b)


def weight_groups():
    g = []

    def win(name, c0, n):
        g.append((name, 8, n, [("w_in", 0, c0, n, 0)], "norm1"))

    win("rgate0", 2048, 512)
    win("rgate1", 2560, 512)
    win("rq", 0, 512)
    win("rk", 512, 512)
    win("rv0", 1024, 512)
    win("rv1", 1536, 512)
    win("hgate0", 6144, 512)
    win("hgate1", 6656, 512)
    win("hi0", 5120, 512)
    win("hi1", 5632, 512)
    win("hq0", 3072, 512)
    win("hf0", 4096, 512)
    win("hq1", 3584, 512)
    win("hf1", 4608, 512)
    win("ry0", 8448, 512)
    win("ry1", 8960, 512)
    win("ry2", 9472, 256)
    g.append(("rgw", 2, 2560, None, None))
    win("ru0", 7168, 512)
    win("ru1", 7680, 512)
    win("ru2", 8192, 256)
    brow = [0, 1024, 2048]
    bkc = [8, 8, 10]
    for b in range(3):
        for jh in range(2):
            win(f"mg{b}{jh}", 9728 + b * 1024 + jh * 512, 512)
            g.append((f"wb{b}{jh}", bkc[b], 512, [("w_branch", brow[b], jh * 512, 512, 0)], "hgn" if b == 1 else None))
    g.append(("wo0", 8, 512, [("w_out", 0, 0, 512, 0)], None))
    g.append(("wo1", 8, 512, [("w_out", 0, 512, 512, 0)], None))
    for i in range(11):
        g.append((f"wu{i}", 8, 512, [("w_up", 0, 256 * i, 256, 0), ("w_up", 0, DFF + 256 * i, 256, 256)], "norm2"))
    for half in range(2):
        for kg, (k0, kn) in enumerate([(0, 8), (8, 8), (16, 6)]):
            g.append((f"wd{half}{kg}", kn, 512, [("w_down", k0 * 128, half * 512, 512, 0)], None))
    return g


def smalls_layout():
    off = {}
    c = 0

    def add(name, n):
        nonlocal c
        off[name] = (c, n)
        c += n

    add("lb0", 8)
    add("lb1", 8)
    for l in range(DEPTH):
        add(f"rcw{l}", 40)
        add(f"rcb{l}", 10)
        add(f"rbr{l}", 10)
        add(f"rbi{l}", 10)
        add(f"rlam{l}", 10)
        add(f"fcw{l}", 66)
        add(f"fcb{l}", 22)
        add(f"n1{l}", 8)
        add(f"n2{l}", 8)
        add(f"hgn{l}", 8)
        add(f"h0{l}", 10)
        add(f"u0{l}", 30)
        add(f"a0{l}", 44)
    return off, c


SM_OFF, SM_N = smalls_layout()
OUT_W = 84


def build_program(SEQ):
    NTILE = SEQ // TT
    nc = bass.Bass("TRN2", target_bir_lowering=False)
    S = Sched()
    es = ExitStack()

    def din(name, shape, dt=F32):
        return nc.dram_tensor(name, list(shape), dt, kind="ExternalInput").ap()

    def dout(name, shape, dt=F32):
        return nc.dram_tensor(name, list(shape), dt, kind="ExternalOutput").ap()

    x_p = din("x_p", [SEQ, D])
    x_s = din("x_s", [DSEQ, D])
    st_ret = din("st_ret", [DEPTH, 4, 128, 256])
    st_hg = din("st_hg", [DEPTH, 8, 128, 128])
    smalls_d = din("smalls", [128, SM_N])
    wfin_d = din("wfin", [128, D])
    W = {
        "w_in": din("w_in", [DEPTH, D, INW]),
        "w_branch": din("w_branch", [DEPTH, 3328, D]),
        "w_out": din("w_out", [DEPTH, D, D]),
        "w_up": din("w_up", [DEPTH, D, 2 * DFF]),
        "w_down": din("w_down", [DEPTH, DFF, D]),
    }
    rg_w_r = din("rg_w_r", [DEPTH, 5, 256, 256])
    rg_w_i = din("rg_w_i", [DEPTH, 5, 256, 256])
    ident_d = din("ident", [128, 128])
    perm_d = din("permT", [128, 128])
    rotc_p = din("rotc_p", [128, SEQ])
    rots_p = din("rots_p", [128, SEQ])
    rotc_s = din("rotc_s", [128, DSEQ])
    rots_s = din("rots_s", [128, DSEQ])
    retmask_d = din("retmask", [2, 128, 4, 128])
    qd_d = din("qd", [2, 128, 4, 128])
    kd_d = din("kd", [2, 128, 4, 128])
    hgmask_d = din("hgmask", [2, 128, 128])
    scanmask_d = din("scanmask", [2, 128, TT])

    y_p = dout("y_p", [SEQ, D])
    y_s = dout("y_s", [DSEQ, D])
    ret_o = dout("ret_o", [2, DEPTH, 4, 128, 256])
    hg_o = dout("hg_o", [2, DEPTH, 8, 128, 128])
    small_o = dout("small_o", [128, 2 * DEPTH * OUT_W])

    groups = weight_groups()
    NG = len(groups)
    wscr = nc.dram_tensor("wscr", [DEPTH * NG, 128, SLOT], BF16, kind="Internal").ap()

    def sem(name):
        return (name, es.enter_context(nc.semaphore(name)))

    engsems = {q: sem("e_" + q)[1] for q in ("pe", "act", "dve", "pool")}
    nsem = {"i": 0}

    def newsem():
        nsem["i"] += 1
        return sem(f"d_m{nsem['i']}")

    sem_x = [sem(f"d_x{b}") for b in range(4)]
    sem_y = [sem(f"d_y{b}") for b in range(4)]
    sem_slot = [sem(f"d_w{i}") for i in range(NSLOT)]
    sem_stg = [sem(f"d_stg{i}") for i in range(2)]
    sem_scr = [sem(f"d_scr{i}") for i in range(2)]
    sem_rot = sem("d_rot")
    sem_out = sem("d_out")
    sem_gc = [sem(f"d_gc{i}") for i in range(5)]
    sem_sti = [sem(f"d_sti{i}") for i in range(4)]
    sem_sto = [sem(f"d_sto{i}") for i in range(4)]

    def sb(name, shape, dt=F32):
        return es.enter_context(nc.sbuf_tensor(name, list(shape), dt))

    smalls = sb("smalls_sb", [128, SM_N])
    K_SM = ("smalls",)

    def smc(name, a=0, n=None):
        o, w = SM_OFF[name]
        if n is None:
            n = w - a
        return smalls[:, o + a:o + a + n]

    def isps(*aps):
        for a_ in aps:
            try:
                if a_.space == mybir.MemoryType.PSUM:
                    return True
            except Exception:
                pass
        return False

    def fsz(ap):
        n = 1
        for d in ap.shape[1:]:
            n *= int(d)
        return n

    TSET = {AF.Silu: "silu", AF.Sigmoid: "sig", AF.Exp: "exp", AF.Ln: "ln", AF.Sqrt: "sqrt", AF.Gelu_apprx_tanh: "gelu"}

    def ACT(out, in_, func, reads, writes, bias=None, scale=None, accum=None):
        kw = {}
        if bias is not None:
            kw["bias"] = bias
        if scale is not None:
            kw["scale"] = scale
        if accum is not None:
            kw["accum_out"] = accum
        c = 0.22 + fsz(out) / 1400.0 + (0.15 if accum is not None else 0.0)
        return S.op("act", lambda e: e.activation(out=out, in_=in_, func=func, **kw), reads, writes, cost=c, tset=TSET.get(func), evac=isps(in_))

    def ecost(q, n, mul=1.0):
        if q == "dve":
            return 0.12 + mul * n / 960.0
        return 0.3 + mul * n / 450.0

    def TT_(q, out, in0, in1, op, reads, writes):
        return S.op(q, lambda e: e.tensor_tensor(out=out, in0=in0, in1=in1, op=op), reads, writes, cost=ecost(q, fsz(out)), evac=isps(in0, in1))

    def TS(q, out, in0, s1, s2, op0, op1, reads, writes):
        if s2 is None:
            return S.op(q, lambda e: e.tensor_scalar(out=out, in0=in0, scalar1=s1, scalar2=None, op0=op0), reads, writes,
                        cost=ecost(q, fsz(out)), evac=isps(in0))
        return S.op(q, lambda e: e.tensor_scalar(out=out, in0=in0, scalar1=s1, scalar2=s2, op0=op0, op1=op1), reads, writes,
                    cost=ecost(q, fsz(out)), evac=isps(in0))

    def STT(out, in0, scalar, in1, op0, op1, reads, writes):
        return S.op("dve", lambda e: e.scalar_tensor_tensor(out=out, in0=in0, scalar=scalar, in1=in1, op0=op0, op1=op1), reads, writes,
                    cost=ecost("dve", fsz(out)), evac=isps(in0, in1))

    def CP(q, out, in_, reads, writes):
        return S.op(q, lambda e: e.tensor_copy(out=out, in_=in_), reads, writes, cost=ecost(q, fsz(out)), evac=isps(in_))

    def MM(out, lhsT, rhs, start, stop, reads, writes):
        n = max(64, fsz(rhs))
        c = 0.03 + n / 1950.0
        if rhs.dtype == F32:
            c *= 4
        return S.op("pe", lambda e: e.matmul(out, lhsT=lhsT, rhs=rhs, start=start, stop=stop), reads, writes, cost=c)

    def TR(out, in_, idn, reads, writes):
        return S.op("pe", lambda e: e.transpose(out=out, in_=in_, identity=idn), reads, writes, cost=0.1)

    def DMA(out, in_, reads, writes, semt, nc_ok=False):
        nb = int(out.shape[0]) * fsz(out) * (2 if out.dtype == BF16 else 4)
        if nc_ok:
            return S.op("sp", lambda e: e.dma_start(out=out, in_=in_, allow_slow_non_contiguous=True), reads, writes, dma_sem=semt, nbytes=nb)
        return S.op("sp", lambda e: e.dma_start(out=out, in_=in_), reads, writes, dma_sem=semt, nbytes=nb)

    def MEMSET(q, ap, val, writes):
        return S.op(q, lambda e: e.memset(ap, val), (), writes, cost=ecost(q, fsz(ap)))

    rr = {"i": 0}

    def anyq():
        rr["i"] += 1
        return ("dve", "pool")[rr["i"] % 2]

    DMA(smalls[:], smalls_d[:, :], (), [K_SM], newsem())

    S.tag = "phase0"
    es0 = ExitStack()
    stg = [es0.enter_context(nc.sbuf_tensor(f"stg{i}", [128, SLOT], F32)) for i in range(2)]
    cvt = [es0.enter_context(nc.sbuf_tensor(f"cvt{i}", [128, SLOT], BF16)) for i in range(2)]
    allg = [(l, gidx) for l in range(DEPTH) for gidx in range(NG)]

    def p0_load(gi):
        l, gidx = allg[gi]
        name, kcn, ncols, pieces, scale = groups[gidx]
        i = gi % 2
        n = kcn * ncols
        sv = stg[i][:, 0:n].rearrange("p (k c) -> p k c", c=ncols)
        if name == "rgw":
            for gt, wsrc in enumerate((rg_w_r, rg_w_i)):
                dst = stg[i][:, gt * 2560:(gt + 1) * 2560].rearrange("p (n k e) -> p n k e", n=5, k=2)
                for nb in range(5):
                    DMA(dst[:, nb, :, :], wsrc[l, nb].rearrange("(k p) e -> p k e", p=128), (), [("stg", i)], sem_stg[i])
        else:
            for (tn, r0, c0, npc, dc) in pieces:
                src = W[tn][l, r0:r0 + kcn * 128, c0:c0 + npc].rearrange("(k p) c -> p k c", p=128)
                DMA(sv[:, :, dc:dc + npc], src, (), [("stg", i)], sem_stg[i])

    def p0_cvt(gi):
        l, gidx = allg[gi]
        name, kcn, ncols, pieces, scale = groups[gidx]
        i = gi % 2
        n = kcn * ncols
        sv = stg[i][:, 0:n].rearrange("p (k c) -> p k c", c=ncols)
        cvv = cvt[i][:, 0:n].rearrange("p (k c) -> p k c", c=ncols)
        if name == "rgw":
            CP("pool", cvt[i][:, 0:5120], stg[i][:, 0:5120], [("stg", i)], [("cvt", i)])
            n = 5120
        elif scale is None:
            ACT(cvv, sv, AF.Copy, [("stg", i)], [("cvt", i)])
        else:
            scn = {"norm1": f"n1{l}", "norm2": f"n2{l}", "hgn": f"hgn{l}"}[scale]
            sc = smc(scn).unsqueeze(2).broadcast_to([128, kcn, ncols])
            TT_(("dve", "pool")[(gi // 2) % 2], cvv, sv, sc, ALU.mult, [("stg", i), K_SM], [("cvt", i)])
        DMA(wscr[l * NG + gidx, :, 0:n], cvt[i][:, 0:n], [("cvt", i)], [("wscr", l, gidx)], sem_scr[i])

    p0_load(0)
    for gi in range(len(allg)):
        if gi + 1 < len(allg):
            p0_load(gi + 1)
        p0_cvt(gi)

    S.barrier()
    es0.close()

    xt = sb("xt", [128, 4, D])
    xnb = [sb(f"xnb{i}", [128, D], BF16) for i in range(2)]
    junk = sb("junk", [128, D], BF16)
    hnT = sb("hnT", [128, 8, TT], BF16)
    Vt = sb("Vt", [128, 4, D], BF16)
    big = sb("big", [128, 26, TT], BF16)
    NF = 13
    NBF = 16
    fbufs = [sb(f"f{i}", [128, TT]) for i in range(NF)]
    bbufs = [sb(f"b{i}", [128, TT], BF16) for i in range(NBF)]
    sbb = [sb(f"sbb{i}", [128, 1024], BF16) for i in range(2)]
    sall = [sb(f"sall{i}", [128, 1024]) for i in range(2)]
    ubuf = [sb(f"ubuf{i}", [128, 3 + TT]) for i in range(2)]
    abuf = [sb(f"abuf{i}", [128, 2 + TT]) for i in range(2)]
    wslot = [sb(f"wslot{i}", [128, SLOT], BF16) for i in range(NSLOT)]
    rotc = sb("rotc", [128, TT])
    rots = sb("rots", [128, TT])
    wfin = sb("wfin_sb", [128, D])
    ident = sb("ident_sb", [128, 128], BF16)
    identf = sb("identf_sb", [128, 128])
    ones = sb("ones_sb", [128, 128], BF16)
    permT = sb("perm_sb", [128, 128])
    retmask = sb("retmask_sb", [128, 4, 128])
    qd = sb("qd_sb", [128, 4, 128])
    kd = sb("kd_sb", [128, 4, 128])
    hgmask = sb("hgmask_sb", [128, 128])
    scanmask = sb("scanmask_sb", [128, TT])
    S_ret = [sb(f"S_ret{l}", [128, 4, 256]) for l in range(DEPTH)]
    S_hg = [sb(f"S_hg{l}", [128, 8, 128]) for l in range(DEPTH)]
    hstate = [sb(f"hstate{l}", [128, 10]) for l in range(DEPTH)]
    uhalo = [sb(f"uhalo{l}", [128, 10, 3]) for l in range(DEPTH)]
    ahalo = [sb(f"ahalo{l}", [128, 22, 2]) for l in range(DEPTH)]
    outst = sb("outst", [128, 2 * DEPTH * OUT_W])
    stat = sb("stat", [128, 16])
    lbt = sb("lbt", [128, 2, 8])
    omlt = sb("omlt", [128, 2, 8])
    rgc = sb("rgc", [128, 2, 10])
    rgc2 = sb("rgc2", [128, 2, 10])
    ebc = [sb(f"ebc{i}", [128, 8]) for i in range(4)]
    epsb = sb("epsb", [128, 1])

    psum = [es.enter_context(nc.psum_tensor(f"ps{i}", [128, TT], F32)) for i in range(8)]

    FP = BufPool([(fbufs[i], ("f", i)) for i in range(NF)])
    BP = BufPool([(bbufs[i], ("b", i)) for i in range(NBF)])
    PP = BufPool([(psum[i], ("ps", i)) for i in range(8)])
    WP = BufPool(list(range(NSLOT)))

    DMA(wfin[:], wfin_d[:, :], (), [("wfin",)], newsem())
    DMA(identf[:], ident_d[:, :], (), [("identf",)], newsem())
    DMA(permT[:], perm_d[:, :], (), [("permT",)], newsem())
    CP("pool", ident[:], identf[:], [("identf",)], [("ident",)])
    MEMSET("pool", ones[:], 1.0, [("ones",)])
    MEMSET("pool", epsb[:], EPS, [("epsb",)])
    MEMSET("pool", lbt[:, 0, :], 0.0, [("lbt",)])
    TT_("dve", lbt[:, 1, :], smc("lb1"), smc("lb0"), ALU.subtract, [K_SM], [("lbt",)])
    ACT(lbt[:, 1, :], lbt[:, 1, :], AF.Sigmoid, [("lbt",)], [("lbt",)])
    TS("dve", omlt[:], lbt[:], -1.0, 1.0, ALU.mult, ALU.add, [("lbt",)], [("omlt",)])
    for l in range(DEPTH):
        ACT(rgc[:, l, :], smc(f"rlam{l}"), AF.Exp, [K_SM], [("rgc",)], scale=-1.0)
        TS("dve", rgc[:, l, :], rgc[:, l, :], 1.0, None, ALU.add, None, [("rgc",)], [("rgc",)])
        ACT(rgc[:, l, :], rgc[:, l, :], AF.Ln, [("rgc",)], [("rgc",)])
    TS("dve", rgc[:], rgc[:], -8.0, None, ALU.mult, None, [("rgc",)], [("rgc",)])
    TS("dve", rgc2[:], rgc[:], 2.0, None, ALU.mult, None, [("rgc",)], [("rgc2",)])

    wseq = []
    wstate = {"next": 0, "cur": -1}
    wloaded = {}

    def wpump():
        while wstate["next"] < len(wseq) and wstate["next"] <= wstate["cur"] + NSLOT and WP.free_list:
            l, gidx = wseq[wstate["next"]]
            name, kcn, ncols, pieces, scale = groups[gidx]
            n = 5120 if name == "rgw" else kcn * ncols
            s = WP.alloc()
            DMA(wslot[s][:, 0:n], wscr[l * NG + gidx, :, 0:n], [("wscr", l, gidx)], [("wslot", s)], sem_slot[s])
            wloaded[wstate["next"]] = s
            wstate["next"] += 1

    def wget(l, name):
        wstate["cur"] += 1
        k = wstate["cur"]
        ll, gidx = wseq[k]
        assert ll == l and groups[gidx][0] == name, (ll, l, groups[gidx][0], name)
        wpump()
        assert k in wloaded
        s = wloaded.pop(k)
        _, kcn, ncols, _, _ = groups[gidx]
        view = wslot[s][:, 0:kcn * ncols].rearrange("p (k c) -> p k c", c=ncols)
        return s, view

    def wfree(s):
        WP.free(s)
        wpump()

    def emit_tile(grp, ti, T, xsrc, ydst):
        tb = min(128, T)
        NB = T // tb
        first = (ti == 0)

        for b in range(NB):
            DMA(xt[:tb, b, :], xsrc[b * tb:(b + 1) * tb, :], (), [("x", b)], sem_x[b])

        def norm_to_hnT(l):
            for b in range(NB):
                ACT(junk[:tb, :], xt[:tb, b, :], AF.Square, [("x", b)], [("junk",), ("stat", b)], accum=stat[:tb, b:b + 1])
                ACT(stat[:tb, 4 + b:5 + b], stat[:tb, b:b + 1], AF.Sqrt, [("stat", b), ("epsb",)], [("stat2", b)],
                    bias=epsb[:tb, 0:1], scale=1.0 / D)
                S.op("dve", lambda e, b=b: e.reciprocal(out=stat[:tb, 8 + b:9 + b], in_=stat[:tb, 4 + b:5 + b]),
                     [("stat2", b)], [("stat3", b)], cost=0.15)
                xb_ = xnb[b % 2]
                TS("dve", xb_[:tb, :], xt[:tb, b, :], stat[:tb, 8 + b:9 + b], None, ALU.mult, None,
                   [("x", b), ("stat3", b)], [("xnb", b % 2)])
                ps, pk = PP.alloc()
                pv = ps[:].bitcast(BF16)
                for kc in range(8):
                    TR(pv[:, kc * 128:kc * 128 + tb], xb_[:tb, kc * 128:(kc + 1) * 128], ident[:tb, :tb],
                       [("xnb", b % 2), ("ident",)], [pk])
                src = pv.rearrange("p (k t) -> p k t", t=128)[:, :, 0:tb]
                if b % 2 == 0:
                    ACT(hnT[:, :, b * tb:(b + 1) * tb], src, AF.Copy, [pk], [("hnT", b)])
                else:
                    CP("dve", hnT[:, :, b * tb:(b + 1) * tb], src, [pk], [("hnT", b)])
                PP.free((ps, pk))

        HN_ALL = [("hnT", b) for b in range(4)]

        def proj_fm(wv, cc, reads_extra=()):
            ps, pk = PP.alloc()
            for kc in range(8):
                MM(ps[:, 0:T], wv[:, kc, cc * 128:(cc + 1) * 128], hnT[:, kc, 0:T], kc == 0, kc == 7,
                   HN_ALL[:NB] + list(reads_extra), [pk])
            return ps, pk

        def proj_tm_to_V(wv, ws, coff):
            for b in range(NB):
                ps, pk = PP.alloc()
                for kc in range(8):
                    MM(ps[:tb, 0:512], hnT[:, kc, b * tb:(b + 1) * tb], wv[:, kc, 0:512], kc == 0, kc == 7,
                       [("hnT", b), ("wslot", ws)], [pk])
                ACT(Vt[:tb, b, coff:coff + 512], ps[:tb, 0:512], AF.Copy, [pk], [("V", b)])
                PP.free((ps, pk))

        def la_phase1(QT, qk, KT, kk, KE, kek, ke_scale, vcol, dv, maskap, maskkey, seg, S32, skey, decs, dec_reads, sbi):
            nseg_b = tb // seg
            nseg = NB * nseg_b
            S.sub = "la_tr"
            ps, pk = PP.alloc()
            pv = ps[:].bitcast(BF16)
            for b in range(NB):
                TR(pv[:tb, b * 128:(b + 1) * 128], KE[:, b * tb:(b + 1) * tb], ident[:, :], [kek, ("ident",)], [pk])
            ketm, ketk = BP.alloc()
            ACT(ketm[:tb, 0:NB * 128], pv[:tb, 0:NB * 128], AF.Copy, [pk], [ketk], scale=ke_scale)
            PP.free((ps, pk))
            S.sub = "la_U"
            per_bank = 512 // dv
            nbank = max(nseg_b, (nseg + per_bank - 1) // per_bank)
            ubanks = [PP.alloc() for _ in range(nbank)]

            def uloc(si):
                if nseg_b > 1:
                    return si % nseg_b, (si // nseg_b) * dv
                return si // per_bank, (si % per_bank) * dv

            for si in range(nseg):
                bi_, o0 = uloc(si)
                ps, pk = ubanks[bi_]
                b = si // nseg_b
                p0 = (si % nseg_b) * seg
                MM(ps[:, o0:o0 + dv], ketm[p0:p0 + seg, b * 128:(b + 1) * 128], Vt[p0:p0 + seg, b, vcol:vcol + dv], True, True,
                   [ketk, ("V", b)], [pk])
            BP.free((ketm, ketk))
            S.sub = "la_sc"
            sps, spk = PP.alloc()
            for b in range(NB):
                MM(sps[:tb, b * 128:b * 128 + tb], KT[:, b * tb:(b + 1) * tb], QT[:, b * tb:(b + 1) * tb], True, True,
                   [kk, qk], [spk])
            S.sub = "la_chain"
            sbt = sbb[sbi]
            sbk = ("sbb", sbi)
            sbv = sbt[:, 0:nseg * dv].rearrange("p (s v) -> p s v", v=dv)
            sal = sall[sbi]
            salk = ("sall", sbi)
            sav = sal[:, 0:nseg * dv].rearrange("p (s v) -> p s v", v=dv)
            ACT(sbv[:, 0, :], S32, AF.Copy, [skey], [sbk])
            prev, prevk = S32, skey
            for si in range(nseg):
                bi_, o0 = uloc(si)
                ps, pk = ubanks[bi_]
                if si == nseg - 1:
                    out, outk = S32, skey
                else:
                    out, outk = sav[:, si, :], salk
                STT(out, prev, decs(si), ps[:, o0:o0 + dv], ALU.mult, ALU.add, [prevk, pk] + list(dec_reads), [outk])
                prev, prevk = out, outk
            for u in ubanks:
                PP.free(u)
            if nseg > 1:
                ACT(sbv[:, 1:nseg, :], sav[:, 0:nseg - 1, :], AF.Copy, [salk], [sbk])
            S.sub = "la_mask"
            scm, sck = BP.alloc()
            psv = sps[:tb, 0:NB * 128].rearrange("p (b t) -> p b t", t=128)[:, :, 0:tb]
            scv = scm[:tb, 0:NB * 128].rearrange("p (b t) -> p b t", t=128)[:, :, 0:tb]
            TT_("dve", scv, psv, maskap.unsqueeze(1).broadcast_to([tb, NB, tb]), ALU.mult, [spk, maskkey], [sck])
            PP.free((sps, spk))
            S.sub = ""
            return dict(QT=QT, qk=qk, scm=scm, sck=sck, sbv=sbv, sbk=sbk, vcol=vcol, dv=dv, seg=seg)

        def la_phase2(c):
            QT, qk, scm, sck, sbv, sbk, vcol, dv, seg = (c[k] for k in ("QT", "qk", "scm", "sck", "sbv", "sbk", "vcol", "dv", "seg"))
            nd = dv // 128
            nseg_b = tb // seg
            outs = []
            S.sub = "la_o"
            for d in range(nd):
                ps, pk = PP.alloc()
                for b in range(NB):
                    MM(ps[:, b * tb:(b + 1) * tb], Vt[:tb, b, vcol + d * 128:vcol + (d + 1) * 128],
                       scm[:tb, b * 128:b * 128 + tb], True, False, [("V", b), sck], [pk])
                    for sj in range(nseg_b):
                        si = b * nseg_b + sj
                        t0 = b * tb + sj * seg
                        MM(ps[:, t0:t0 + seg], sbv[:, si, d * 128:(d + 1) * 128], QT[:, t0:t0 + seg], False, sj == nseg_b - 1,
                           [sbk, qk], [pk])
                outs.append((ps, pk))
            BP.free((scm, sck))
            S.sub = "la_sq"
            sqs = []
            for (ps, pk) in outs:
                sq, sqk = BP.alloc()
                ACT(sq[:, 0:T], ps[:, 0:T], AF.Square, [pk], [sqk])
                sqs.append((sq, sqk))
            S.sub = ""
            return outs, sqs

        def postproc(outs, sqs, kc0, dvtot):
            S.sub = "post"
            ss, ssk = PP.alloc()
            for i, (sq, sqk) in enumerate(sqs):
                MM(ss[:, 0:T], ones[:, :], sq[:, 0:T], i == 0, i == len(sqs) - 1, [("ones",), sqk], [ssk])
            for b_ in sqs:
                BP.free(b_)
            rs, rsk = FP.alloc()
            ACT(rs[:, 0:T], ss[:, 0:T], AF.Sqrt, [ssk, ("epsb",)], [rsk], bias=epsb[:, 0:1], scale=1.0 / dvtot)
            PP.free((ss, ssk))
            S.op("dve", lambda e: e.reciprocal(out=rs[:, 0:T], in_=rs[:, 0:T]), [rsk], [rsk], cost=0.12 + T / 960.0)
            for d, (ps, pk) in enumerate(outs):
                tmp, tk = FP.alloc()
                TT_("dve", tmp[:, 0:T], ps[:, 0:T], rs[:, 0:T], ALU.mult, [pk, rsk], [tk])
                PP.free((ps, pk))
                TT_("pool", big[:, kc0 + d, 0:T], tmp[:, 0:T], big[:, kc0 + d, 0:T], ALU.mult, [tk, ("big", kc0 + d)], [("big", kc0 + d)])
                FP.free((tmp, tk))
            FP.free((rs, rsk))
            S.sub = ""

        def pipeline(n, stages):
            for step in range(n + len(stages) - 1):
                for k, f in enumerate(stages):
                    i = step - k
                    if 0 <= i < n:
                        f(i)

        if grp == 0:
            DMA(rotc[:, 0:T], rotc_p[:, ti * TT:ti * TT + T], (), [("rot",)], sem_rot)
            DMA(rots[:, 0:T], rots_p[:, ti * TT:ti * TT + T], (), [("rot",)], sem_rot)
        else:
            DMA(rotc[:, 0:T], rotc_s[:, 0:T], (), [("rot",)], sem_rot)
            DMA(rots[:, 0:T], rots_s[:, 0:T], (), [("rot",)], sem_rot)

        lg = [float(np.log1p(-2.0 ** (-5.0 - h))) for h in range(4)]
        Lblk = tb

        for l in range(DEPTH):
            S.tag = "norm1"
            norm_to_hnT(l)

            S.tag = "ret"
            for gname, kcb in (("rgate0", 0), ("rgate1", 4)):
                ws, wv = wget(l, gname)
                for cc in range(4):
                    ps, pk = proj_fm(wv, cc, [("wslot", ws)])
                    ACT(big[:, kcb + cc, 0:T], ps[:, 0:T], AF.Silu, [pk], [("big", kcb + cc)])
                    PP.free((ps, pk))
                wfree(ws)
            QK = {}
            rws = {}
            rctx = {}

            def rot_a(i, l=l):
                gname = ("rq", "rk")[i // 4]
                h = i % 4
                if h == 0:
                    rws[gname] = wget(l, gname)
                ws, wv = rws[gname]
                ps, pk = proj_fm(wv, h, [("wslot", ws)])
                q32, q32k = FP.alloc()
                ACT(q32[:, 0:T], ps[:, 0:T], AF.Copy, [pk], [q32k])
                PP.free((ps, pk))
                rctx[i] = (q32, q32k)
                if h == 3:
                    wfree(ws)

            def rot_b(i):
                gname = ("rq", "rk")[i // 4]
                tab = (qd, kd)[i // 4]
                h = i % 4
                q32, q32k = rctx.pop(i)
                pq, pqk = PP.alloc()
                S.tag = "ret_perm"
                MM(pq[:, 0:T], permT[:, :], q32[:, 0:T], True, True, [("permT",), q32k], [pqk])
                S.tag = "ret"
                t1, t1k = FP.alloc()
                TT_("pool", t1[:, 0:T], q32[:, 0:T], rotc[:, 0:T], ALU.mult, [q32k, ("rot",)], [t1k])
                t2, t2k = FP.alloc()
                TT_("dve", t2[:, 0:T], pq[:, 0:T], rots[:, 0:T], ALU.mult, [pqk, ("rot",)], [t2k])
                PP.free((pq, pqk))
                FP.free((q32, q32k))
                TT_("pool", t1[:, 0:T], t1[:, 0:T], t2[:, 0:T], ALU.add, [t1k, t2k], [t1k])
                FP.free((t2, t2k))
                o, ok = BP.alloc()
                ov = o[:, 0:T].rearrange("p (b t) -> p b t", t=tb)
                tv = t1[:, 0:T].rearrange("p (b t) -> p b t", t=tb)
                TT_("dve", ov, tv, tab[:, h, 0:tb].unsqueeze(1).broadcast_to([128, NB, tb]), ALU.mult,
                    [t1k, ("rtabq",), ("rtabk",)], [ok])
                FP.free((t1, t1k))
                QK[(gname, h)] = (o, ok)

            pipeline(8, [rot_a, rot_b])
            for gname, coff in (("rv0", 0), ("rv1", 512)):
                ws, wv = wget(l, gname)
                proj_tm_to_V(wv, ws, coff)
                wfree(ws)
            lctx = {}

            def ret_b(h, l=l):
                QT, qk_ = QK[("rq", h)]
                KT, kk_ = QK[("rk", h)]
                gL = float(np.exp(lg[h] * Lblk))
                lctx[h] = la_phase1(QT, qk_, KT, kk_, KT, kk_, gL, h * 256, 256,
                                    retmask[:tb, h, 0:tb], ("rmask",), Lblk, S_ret[l][:, h, :], ("S_ret", l, h),
                                    lambda si, gL=gL: gL, (), h % 2)
                BP.free((KT, kk_))

            def ret_c(h):
                c = lctx[h]
                c["outs"], c["sqs"] = la_phase2(c)
                BP.free((c["QT"], c["qk"]))

            def ret_d(h):
                c = lctx.pop(h)
                postproc(c["outs"], c["sqs"], 2 * h, 256)

            pipeline(4, [ret_b, ret_c, ret_d])

            S.tag = "hgrn"
            for gname, kcb in (("hgate0", 8), ("hgate1", 12)):
                ws, wv = wget(l, gname)
                for cc in range(4):
                    ps, pk = proj_fm(wv, cc, [("wslot", ws)])
                    ACT(big[:, kcb + cc, 0:T], ps[:, 0:T], AF.Silu, [pk], [("big", kcb + cc)])
                    PP.free((ps, pk))
                wfree(ws)
            for gname, coff in (("hi0", 0), ("hi1", 512)):
                ws, wv = wget(l, gname)
                proj_tm_to_V(wv, ws, coff)
                wfree(ws)
            hq = {}
            hws = {}
            hctx = {}
            seg = min(64, T)
            nsg = T // seg

            def hg_a(hp, l=l):
                H2 = []
                for h in (2 * hp, 2 * hp + 1):
                    hh, cc = h // 4, h % 4
                    if cc == 0:
                        ws, wv = wget(l, f"hq{hh}")
                        for c2 in range(4):
                            ps, pk = proj_fm(wv, c2, [("wslot", ws)])
                            q, qk_ = FP.alloc()
                            ACT(q[:, 0:T], ps[:, 0:T], AF.Silu, [pk], [qk_])
                            PP.free((ps, pk))
                            hq[hh * 4 + c2] = (q, qk_)
                        wfree(ws)
                        hws[hh] = wget(l, f"hf{hh}")
                    ws, wv = hws[hh]
                    ps, pk = proj_fm(wv, cc, [("wslot", ws)])
                    if cc == 3:
                        wfree(ws)
                    f, fk = FP.alloc()
                    k1, k1k = FP.alloc()
                    B, Bk = FP.alloc()
                    H2.append(dict(h=h, ps=ps, pk=pk, f=f, fk=fk, k1=k1, k1k=k1k, B=B, Bk=Bk, eb=ebc[h % 4]))
                for c in H2:
                    ACT(c["f"][:, 0:T], c["ps"][:, 0:T], AF.Sigmoid, [c["pk"]], [c["fk"]])
                    PP.free((c["ps"], c["pk"]))
                for c in H2:
                    h = c["h"]
                    TS("dve", c["f"][:, 0:T], c["f"][:, 0:T], omlt[:, l, h:h + 1], lbt[:, l, h:h + 1], ALU.mult, ALU.add,
                       [c["fk"], ("omlt",), ("lbt",)], [c["fk"]])
                for c in H2:
                    ACT(c["k1"][:, 0:T], c["f"][:, 0:T], AF.Identity, [c["fk"]], [c["k1k"]], scale=-1.0, bias=1.0)
                for c in H2:
                    TS("dve", c["f"][:, 0:T], c["f"][:, 0:T], 1e-6, None, ALU.max, None, [c["fk"]], [c["fk"]])
                for c in H2:
                    ACT(c["f"][:, 0:T], c["f"][:, 0:T], AF.Ln, [c["fk"]], [c["fk"]])
                for c in H2:
                    B, f = c["B"], c["f"]
                    S.op("dve", lambda e, B=B, f=f: e.tensor_tensor_scan(out=B[:, 0:T], data0=scanmask[:, 0:T], data1=f[:, 0:T],
                                                                      initial=0.0, op0=ALU.mult, op1=ALU.add),
                         [c["fk"], ("scanmask",)], [c["Bk"]], cost=0.12 + 2 * T / 960.0)
                for c in H2:
                    ACT(c["f"][:, 0:T], c["B"][:, 0:T], AF.Exp, [c["Bk"]], [c["fk"]])
                for c in H2:
                    ACT(c["B"][:, 0:T], c["B"][:, 0:T], AF.Exp, [c["Bk"]], [c["Bk"]], scale=-1.0)
                for c in H2:
                    h = c["h"]
                    Ev = c["f"][:, 0:T].rearrange("p (c j) -> p c j", j=seg)
                    CP("pool", c["eb"][:, 0:nsg], Ev[:, :, seg - 1], [c["fk"]], [("ebc", h % 4)])
                    q, qk_ = hq.pop(h)
                    QT, QTk = BP.alloc()
                    TT_("pool", QT[:, 0:T], q[:, 0:T], c["f"][:, 0:T], ALU.mult, [qk_, c["fk"]], [QTk])
                    FP.free((q, qk_))
                    c["QT"], c["QTk"] = QT, QTk
                for c in H2:
                    KT, KTk = BP.alloc()
                    TT_("dve", KT[:, 0:T], c["k1"][:, 0:T], c["B"][:, 0:T], ALU.mult, [c["k1k"], c["Bk"]], [KTk])
                    FP.free((c["k1"], c["k1k"]))
                    FP.free((c["B"], c["Bk"]))
                    c["KT"], c["KTk"] = KT, KTk
                for c in H2:
                    Ev = c["f"][:, 0:T].rearrange("p (c j) -> p c j", j=seg)
                    KE, KEk = BP.alloc()
                    TT_("pool", KE[:, 0:T].rearrange("p (c j) -> p c j", j=seg), c["KT"][:, 0:T].rearrange("p (c j) -> p c j", j=seg),
                        Ev[:, :, seg - 1:seg].broadcast_to([128, nsg, seg]), ALU.mult, [c["KTk"], c["fk"]], [KEk])
                    FP.free((c["f"], c["fk"]))
                    hctx[c["h"]] = (c["QT"], c["QTk"], c["KT"], c["KTk"], KE, KEk, c["eb"])

            def hg_b(h, l=l):
                QT, QTk, KT, KTk, KE, KEk, eb = hctx[h]
                hctx[h] = la_phase1(QT, QTk, KT, KTk, KE, KEk, 1.0, h * 128, 128,
                                    hgmask[:tb, 0:tb], ("hgmask",), seg, S_hg[l][:, h, :], ("S_hg", l, h),
                                    lambda si, eb=eb: eb[:, si:si + 1], [("ebc", h % 4)], h % 2)
                BP.free((KT, KTk))
                BP.free((KE, KEk))

            def hg_c(h):
                c = hctx[h]
                c["outs"], c["sqs"] = la_phase2(c)
                BP.free((c["QT"], c["qk"]))

            def hg_d(h):
                c = hctx.pop(h)
                postproc(c["outs"], c["sqs"], 8 + h, 128)

            def hg_a1(h):
                if h % 2 == 0:
                    hg_a(h // 2)

            pipeline(8, [hg_a1, hg_b, hg_c, hg_d])

            S.tag = "rglru"
            cidx = 0
            for gname, ncc in (("ry0", 4), ("ry1", 4), ("ry2", 2)):
                ws, wv = wget(l, gname)
                for cc in range(ncc):
                    ps, pk = proj_fm(wv, cc, [("wslot", ws)])
                    ACT(big[:, 16 + cidx, 0:T], ps[:, 0:T], AF.Gelu_apprx_tanh, [pk], [("big", 16 + cidx)])
                    PP.free((ps, pk))
                    cidx += 1
                wfree(ws)
            gws, _ = wget(l, "rgw")
            gwv = wslot[gws][:, 0:5120].rearrange("p (g n k e) -> p g n k e", g=2, n=5, k=2)
            ruw = {}
            rgx = {}

            def rg_a(n, l=l):
                for c in (2 * n, 2 * n + 1):
                    gi_, cc = c // 4, c % 4
                    if cc == 0:
                        ruw[gi_] = wget(l, f"ru{gi_}")
                    ws, wv = ruw[gi_]
                    ps, pk = proj_fm(wv, cc, [("wslot", ws)])
                    if c == 9 or cc == 3:
                        wfree(ws)
                    ub = ubuf[c % 2]
                    ubk = ("ubuf", c % 2)
                    CP("pool", ub[:, 0:3], uhalo[l][:, c, :], [("uhalo", l, c)], [ubk])
                    ACT(ub[:, 3:3 + T], ps[:, 0:T], AF.Copy, [pk], [ubk])
                    PP.free((ps, pk))
                    CP("pool", uhalo[l][:, c, :], ub[:, T:T + 3], [ubk], [("uhalo", l, c)])
                    cw = smc(f"rcw{l}", 4 * c, 4)
                    t, tk = FP.alloc()
                    ACT(t[:, 0:T], ub[:, 0:T], AF.Identity, [ubk, K_SM], [tk], scale=cw[:, 0:1], bias=smc(f"rcb{l}", c, 1))
                    STT(t[:, 0:T], ub[:, 1:1 + T], cw[:, 1:2], t[:, 0:T], ALU.mult, ALU.add, [ubk, tk, K_SM], [tk])
                    STT(t[:, 0:T], ub[:, 2:2 + T], cw[:, 2:3], t[:, 0:T], ALU.mult, ALU.add, [ubk, tk, K_SM], [tk])
                    STT(t[:, 0:T], ub[:, 3:3 + T], cw[:, 3:4], t[:, 0:T], ALU.mult, ALU.add, [ubk, tk, K_SM], [tk])
                    xb_, xbk = BP.alloc()
                    ACT(xb_[:, 0:T], t[:, 0:T], AF.Copy, [tk], [xbk])
                    rgx[c] = (t, tk, xb_, xbk)

            def rg_b(n, l=l):
                pend = [rgx.pop(2 * n), rgx.pop(2 * n + 1)]
                gps = []
                for ei in range(2):
                    rps, rpk = PP.alloc()
                    ips, ipk = PP.alloc()
                    for kc in range(2):
                        MM(rps[:, 0:T], gwv[:, 0, n, kc, ei * 128:(ei + 1) * 128], pend[kc][2][:, 0:T], kc == 0, kc == 1,
                           [("wslot", gws), pend[kc][3]], [rpk])
                    for kc in range(2):
                        MM(ips[:, 0:T], gwv[:, 1, n, kc, ei * 128:(ei + 1) * 128], pend[kc][2][:, 0:T], kc == 0, kc == 1,
                           [("wslot", gws), pend[kc][3]], [ipk])
                    gps.append((rps, rpk, ips, ipk))
                E2 = []
                for ei in range(2):
                    e_ = 2 * n + ei
                    xc, xck = pend[ei][0], pend[ei][1]
                    rps, rpk, ips, ipk = gps[ei]
                    r, rk_ = FP.alloc()
                    ig, igk = FP.alloc()
                    a, ak = FP.alloc()
                    E2.append(dict(e_=e_, xc=xc, xck=xck, rps=rps, rpk=rpk, ips=ips, ipk=ipk, r=r, rk=rk_, ig=ig, igk=igk, a=a, ak=ak))
                for c in E2:
                    ACT(c["r"][:, 0:T], c["rps"][:, 0:T], AF.Sigmoid, [c["rpk"], K_SM], [c["rk"]], bias=smc(f"rbr{l}", c["e_"], 1))
                    PP.free((c["rps"], c["rpk"]))
                for c in E2:
                    ACT(c["ig"][:, 0:T], c["ips"][:, 0:T], AF.Sigmoid, [c["ipk"], K_SM], [c["igk"]], bias=smc(f"rbi{l}", c["e_"], 1))
                    PP.free((c["ips"], c["ipk"]))
                for c in E2:
                    e_ = c["e_"]
                    ACT(c["a"][:, 0:T], c["r"][:, 0:T], AF.Exp, [c["rk"], ("rgc",)], [c["ak"]], scale=rgc[:, l, e_:e_ + 1])
                for c in E2:
                    STT(c["r"][:, 0:T], c["a"][:, 0:T], -1.0, c["a"][:, 0:T], ALU.mult, ALU.mult, [c["ak"]], [c["rk"]])
                for c in E2:
                    TT_("pool", c["ig"][:, 0:T], c["ig"][:, 0:T], c["xc"][:, 0:T], ALU.mult, [c["igk"], c["xck"]], [c["igk"]])
                for c in E2:
                    TS("dve", c["r"][:, 0:T], c["r"][:, 0:T], -1.0, None, ALU.max, None, [c["rk"]], [c["rk"]])
                for c in E2:
                    ACT(c["r"][:, 0:T], c["r"][:, 0:T], AF.Sqrt, [c["rk"]], [c["rk"]], bias=1.0, scale=1.0)
                    if grp == 0 and first:
                        MEMSET("pool", c["r"][:, 0:1], 1.0, [c["rk"]])
                for c in E2:
                    TT_("dve", c["ig"][:, 0:T], c["ig"][:, 0:T], c["r"][:, 0:T], ALU.mult, [c["igk"], c["rk"]], [c["igk"]])
                    FP.free((c["r"], c["rk"]))
                for c in E2:
                    e_ = c["e_"]
                    hk = ("hstate", l, e_)
                    xc, a, ig = c["xc"], c["a"], c["ig"]
                    S.op("dve", lambda e, xc=xc, a=a, ig=ig, e_=e_, l=l: e.tensor_tensor_scan(
                        out=xc[:, 0:T], data0=a[:, 0:T], data1=ig[:, 0:T], initial=hstate[l][:, e_:e_ + 1],
                        op0=ALU.mult, op1=ALU.add), [c["ak"], c["igk"], hk], [c["xck"]], cost=0.12 + 2 * T / 960.0)
                    FP.free((c["a"], c["ak"]))
                    FP.free((c["ig"], c["igk"]))
                for c in E2:
                    e_ = c["e_"]
                    hk = ("hstate", l, e_)
                    CP("pool", hstate[l][:, e_:e_ + 1], c["xc"][:, T - 1:T], [c["xck"]], [hk])
                    TT_("pool", big[:, 16 + e_, 0:T], c["xc"][:, 0:T], big[:, 16 + e_, 0:T], ALU.mult,
                        [c["xck"], ("big", 16 + e_)], [("big", 16 + e_)])
                for (xc, xck, xb2, xbk2) in pend:
                    FP.free((xc, xck))
                    BP.free((xb2, xbk2))

            pipeline(5, [rg_a, rg_b])
            wfree(gws)

            S.tag = "merge"
            V_ALL = [("V", b) for b in range(4)]
            bkc = [8, 8, 10]
            bk0 = [0, 8, 16]
            acc = [FP.alloc() for _ in range(8)]
            for b in range(3):
                for jh in range(2):
                    gs, gv = wget(l, f"mg{b}{jh}")
                    bs, bv = wget(l, f"wb{b}{jh}")
                    for cc in range(4):
                        j = jh * 4 + cc
                        gps, gpk = proj_fm(gv, cc, [("wslot", gs)])
                        sg, sgk = FP.alloc()
                        ACT(sg[:, 0:T], gps[:, 0:T], AF.Sigmoid, [gpk], [sgk])
                        PP.free((gps, gpk))
                        pps, ppk = PP.alloc()
                        for kc in range(bkc[b]):
                            MM(pps[:, 0:T], bv[:, kc, cc * 128:(cc + 1) * 128], big[:, bk0[b] + kc, 0:T], kc == 0, kc == bkc[b] - 1,
                               [("wslot", bs), ("big", bk0[b] + kc)], [ppk])
                        am, amk = acc[j]
                        if b == 0:
                            TT_("dve", am[:, 0:T], pps[:, 0:T], sg[:, 0:T], ALU.mult, [ppk, sgk], [amk])
                        else:
                            TT_("dve", sg[:, 0:T], pps[:, 0:T], sg[:, 0:T], ALU.mult, [ppk, sgk], [sgk])
                            if b == 1:
                                TT_("pool", am[:, 0:T], am[:, 0:T], sg[:, 0:T], ALU.add, [amk, sgk], [amk])
                            else:
                                mxv = Vt[:].rearrange("p b c -> p (b c)").rearrange("p (k t) -> p k t", t=TT)
                                TT_("pool", mxv[:, j, 0:T], am[:, 0:T], sg[:, 0:T], ALU.add, [amk, sgk] + V_ALL, V_ALL + [("mx", j)])
                        PP.free((pps, ppk))
                        FP.free((sg, sgk))
                    wfree(gs)
                    wfree(bs)
            for a_ in acc:
                FP.free(a_)
            mxv = Vt[:].rearrange("p b c -> p (b c)").rearrange("p (k t) -> p k t", t=TT)

            S.tag = "wout"
            for half in range(2):
                ws, wv = wget(l, f"wo{half}")
                for b in range(NB):
                    ps, pk = PP.alloc()
                    for kc in range(8):
                        MM(ps[:tb, 0:512], mxv[:, kc, b * tb:(b + 1) * tb], wv[:, kc, 0:512], kc == 0, kc == 7,
                           V_ALL + [("wslot", ws)], [pk])
                    TT_("dve", xt[:tb, b, half * 512:(half + 1) * 512], ps[:tb, 0:512], xt[:tb, b, half * 512:(half + 1) * 512],
                        ALU.add, [pk, ("x", b)], [("x", b)])
                    PP.free((ps, pk))
                wfree(ws)

            S.tag = "norm2"
            norm_to_hnT(l)
            S.tag = "ffn_up"
            for i in range(11):
                ws, wv = wget(l, f"wu{i}")
                for cc in range(2):
                    c = 2 * i + cc
                    aps, apk = proj_fm(wv, cc, [("wslot", ws)])
                    gps, gpk = proj_fm(wv, 2 + cc, [("wslot", ws)])
                    ab = abuf[c % 2]
                    abk = ("abuf", c % 2)
                    CP("pool", ab[:, 0:2], ahalo[l][:, c, :], [("ahalo", l, c)], [abk])
                    ACT(ab[:, 2:2 + T], aps[:, 0:T], AF.Copy, [apk], [abk])
                    CP("pool", ahalo[l][:, c, :], ab[:, T:T + 2], [abk], [("ahalo", l, c)])
                    cw = smc(f"fcw{l}", 3 * c, 3)
                    t, tk = FP.alloc()
                    ACT(t[:, 0:T], ab[:, 0:T], AF.Identity, [abk, K_SM], [tk], scale=cw[:, 0:1])
                    STT(t[:, 0:T], ab[:, 1:1 + T], cw[:, 1:2], t[:, 0:T], ALU.mult, ALU.add, [abk, tk, K_SM], [tk])
                    STT(t[:, 0:T], aps[:, 0:T], cw[:, 2:3], t[:, 0:T], ALU.mult, ALU.add, [apk, tk, K_SM], [tk])
                    PP.free((aps, apk))
                    ACT(t[:, 0:T], t[:, 0:T], AF.Gelu_apprx_tanh, [tk, K_SM], [tk], bias=smc(f"fcb{l}", c, 1))
                    TT_("dve", big[:, c, 0:T], gps[:, 0:T], t[:, 0:T], ALU.mult, [gpk, tk], [("big", c)])
                    PP.free((gps, gpk))
                    FP.free((t, tk))
                wfree(ws)
            S.tag = "ffn_down"
            kgs = [(0, 8), (8, 8), (16, 6)]
            accs = [PP.alloc() for _ in range(NB)]
            for kg, (k0, kn) in enumerate(kgs):
                ws, wv = wget(l, f"wd0{kg}")
                for b in range(NB):
                    ps, pk = accs[b]
                    for kc in range(kn):
                        MM(ps[:tb, 0:512], big[:, k0 + kc, b * tb:(b + 1) * tb], wv[:, kc, 0:512],
                           (kg == 0 and kc == 0), (kg == 2 and kc == kn - 1), [("big", k0 + kc), ("wslot", ws)], [pk])
                wfree(ws)
            for b in range(NB):
                ps, pk = accs[b]
                TT_("dve", xt[:tb, b, 0:512], ps[:tb, 0:512], xt[:tb, b, 0:512], ALU.add, [pk, ("x", b)], [("x", b)])
                PP.free((ps, pk))
            wds = [wget(l, f"wd1{kg}") for kg in range(3)]
            for b in range(NB):
                ps, pk = PP.alloc()
                for kg, (k0, kn) in enumerate(kgs):
                    ws, wv = wds[kg]
                    for kc in range(kn):
                        MM(ps[:tb, 0:512], big[:, k0 + kc, b * tb:(b + 1) * tb], wv[:, kc, 0:512],
                           (kg == 0 and kc == 0), (kg == 2 and kc == kn - 1), [("big", k0 + kc), ("wslot", ws)], [pk])
                TT_("dve", xt[:tb, b, 512:1024], ps[:tb, 0:512], xt[:tb, b, 512:1024], ALU.add, [pk, ("x", b)], [("x", b)])
                PP.free((ps, pk))
            for ws, wv in wds:
                wfree(ws)

        S.tag = "final"
        for b in range(NB):
            ACT(junk[:tb, :], xt[:tb, b, :], AF.Square, [("x", b)], [("junk",), ("stat", b)], accum=stat[:tb, b:b + 1])
            ACT(stat[:tb, 4 + b:5 + b], stat[:tb, b:b + 1], AF.Sqrt, [("stat", b), ("epsb",)], [("stat2", b)],
                bias=epsb[:tb, 0:1], scale=1.0 / D)
            S.op("dve", lambda e, b=b: e.reciprocal(out=stat[:tb, 8 + b:9 + b], in_=stat[:tb, 4 + b:5 + b]),
                 [("stat2", b)], [("stat3", b)], cost=0.15)
            STT(xt[:tb, b, :], xt[:tb, b, :], stat[:tb, 8 + b:9 + b], wfin[:tb, :], ALU.mult, ALU.mult,
                [("x", b), ("stat3", b), ("wfin",)], [("x", b)])
            DMA(ydst[b * tb:(b + 1) * tb, :], xt[:tb, b, :], [("x", b)], [], sem_y[b])

    ntiles_total = NTILE + 1
    for _ in range(ntiles_total):
        for l in range(DEPTH):
            for gidx in range(NG):
                wseq.append((l, gidx))

    def load_group_consts(grp):
        DMA(retmask[:], retmask_d[grp], (), [("rmask",)], sem_gc[0])
        DMA(qd[:], qd_d[grp], (), [("rtabq",)], sem_gc[1])
        DMA(kd[:], kd_d[grp], (), [("rtabk",)], sem_gc[2])
        DMA(hgmask[:], hgmask_d[grp], (), [("hgmask",)], sem_gc[3])
        DMA(scanmask[:], scanmask_d[grp], (), [("scanmask",)], sem_gc[4])

    def state_keys(l):
        return [("S_ret", l, h) for h in range(4)], [("S_hg", l, h) for h in range(8)], \
               [("hstate", l, e) for e in range(10)], [("uhalo", l, c) for c in range(10)], [("ahalo", l, c) for c in range(22)]

    def store_states(grp):
        for l in range(DEPTH):
            kr, kh, ks, ku, ka = state_keys(l)
            DMA(ret_o[grp, l].rearrange("h k v -> k h v"), S_ret[l][:], kr, [], sem_sto[2 * l])
            DMA(hg_o[grp, l].rearrange("h k v -> k h v"), S_hg[l][:], kh, [], sem_sto[2 * l + 1])
            o0 = (grp * DEPTH + l) * OUT_W
            CP("pool", outst[:, o0:o0 + 10], hstate[l][:], ks, [("outst",)])
            CP("pool", outst[:, o0 + 10:o0 + 40], uhalo[l][:].rearrange("p c j -> p (c j)"), ku, [("outst",)])
            CP("pool", outst[:, o0 + 40:o0 + 84], ahalo[l][:].rearrange("p c j -> p (c j)"), ka, [("outst",)])

    load_group_consts(0)
    for l in range(DEPTH):
        kr, kh, ks, ku, ka = state_keys(l)
        MEMSET("pool", S_ret[l][:], 0.0, kr)
        MEMSET("pool", S_hg[l][:], 0.0, kh)
        MEMSET("pool", hstate[l][:], 0.0, ks)
        MEMSET("pool", uhalo[l][:], 0.0, ku)
        MEMSET("pool", ahalo[l][:], 0.0, ka)
    for ti in range(NTILE):
        emit_tile(0, ti, TT, x_p[ti * TT:(ti + 1) * TT, :], y_p[ti * TT:(ti + 1) * TT, :])
    store_states(0)
    load_group_consts(1)
    for l in range(DEPTH):
        kr, kh, ks, ku, ka = state_keys(l)
        DMA(S_ret[l][:], st_ret[l].rearrange("h k v -> k h v"), (), kr, sem_sti[2 * l])
        DMA(S_hg[l][:], st_hg[l].rearrange("h k v -> k h v"), (), kh, sem_sti[2 * l + 1])
        CP("pool", hstate[l][:], smc(f"h0{l}"), [K_SM], ks)
        CP("pool", uhalo[l][:].rearrange("p c j -> p (c j)"), smc(f"u0{l}"), [K_SM], ku)
        CP("pool", ahalo[l][:].rearrange("p c j -> p (c j)"), smc(f"a0{l}"), [K_SM], ka)
    emit_tile(1, 0, DSEQ, x_s, y_s)
    store_states(1)
    DMA(small_o[:, :], outst[:], [("outst",)], [], sem_out)

    build_program.last_sched = S
    S.reorder()
    block = es.enter_context(nc.Block())
    S.finalize(nc, engsems, block)
    es.close()
    return nc


def _fm(v, nch):
    return np.ascontiguousarray(np.asarray(v, np.float32).reshape(nch, 128).T)


def _consts(SEQ):
    c = {}
    c["ident"] = np.eye(128, dtype=np.float32)
    P = np.zeros((128, 128), np.float32)
    for p in range(64):
        P[p + 64, p] = -1.0
    for p in range(64, 128):
        P[p - 64, p] = 1.0
    c["permT"] = P
    half = 64
    inv = np.power(np.float32(10000.0), -np.arange(half, dtype=np.float32) / np.float32(half)).astype(np.float32)
    inv2 = np.concatenate([inv, inv])

    def rot(pos0, n):
        pos = (np.arange(n, dtype=np.float32) + np.float32(pos0)).astype(np.float32)
        ang = (inv2[:, None] * pos[None, :]).astype(np.float32)
        return np.cos(ang).astype(np.float32), np.sin(ang).astype(np.float32)

    c["rotc_p"], c["rots_p"] = rot(0, SEQ)
    c["rotc_s"], c["rots_s"] = rot(PAST, DSEQ)
    lg = np.log1p(-np.power(2.0, -5.0 - np.arange(4, dtype=np.float64)))
    retmask = np.zeros((2, 128, 4, 128), np.float32)
    qd = np.zeros((2, 128, 4, 128), np.float32)
    kd = np.zeros((2, 128, 4, 128), np.float32)
    for grp, L in ((0, 128), (1, DSEQ)):
        n = np.arange(L)
        for h in range(4):
            qd[grp, :, h, :L] = np.exp(lg[h] * (n + 1.0))[None, :]
            kd[grp, :, h, :L] = (np.exp(-lg[h] * (n + 1.0)) * (128 ** -0.5))[None, :]
            m = n[:, None]
            t = n[None, :]
            samechunk = (m // 64) == (t // 64)
            Dm = np.where(samechunk, np.where(t >= m, 1.0, np.exp(lg[h] * 2.0 * (m - t))), np.where(t > m, 1.0, 0.0))
            retmask[grp, :L, h, :L] = Dm
    c["retmask"], c["qd"], c["kd"] = retmask, qd, kd
    hgmask = np.zeros((2, 128, 128), np.float32)
    n = np.arange(128)
    hgmask[0] = (((n[:, None] // 64) == (n[None, :] // 64)) & (n[:, None] <= n[None, :])).astype(np.float32)
    hgmask[1, :DSEQ, :DSEQ] = (n[:DSEQ, None] <= n[None, :DSEQ]).astype(np.float32)
    c["hgmask"] = hgmask
    sm = np.ones((2, 128, TT), np.float32)
    sm[0, :, 0::64] = 0.0
    sm[1, :, 0] = 0.0
    c["scanmask"] = sm
    return c


_CACHE = {}


def kernel(x_prompt, x_sample, state_ret, state_hgrn, state_rglru, cache_rg_conv, cache_ffn_conv,
           norm1_w, w_in, w_branch, w_out, rg_conv_w, rg_conv_b, rg_w_r, rg_b_r, rg_w_i, rg_b_i,
           rg_lambda, hg_lb, hg_norm_w, norm2_w, w_up, ffn_conv_w, ffn_conv_b, w_down, final_norm_w):
    f32 = np.float32
    x_prompt = np.asarray(x_prompt, f32)
    B, SEQ, _ = x_prompt.shape
    ncore = 8
    assert B == ncore
    if SEQ not in _CACHE:
        _CACHE[SEQ] = (build_program(SEQ), _consts(SEQ))
    nc, consts = _CACHE[SEQ]

    shared = {
        "w_in": np.ascontiguousarray(w_in, f32), "w_branch": np.ascontiguousarray(w_branch, f32),
        "w_out": np.ascontiguousarray(w_out, f32), "w_up": np.ascontiguousarray(w_up, f32),
        "w_down": np.ascontiguousarray(w_down, f32), "rg_w_r": np.ascontiguousarray(rg_w_r, f32),
        "rg_w_i": np.ascontiguousarray(rg_w_i, f32),
        "wfin": np.ascontiguousarray(np.broadcast_to(np.asarray(final_norm_w, f32)[None, :], (128, D))),
    }
    shared.update(consts)
    in_maps = []
    for b in range(ncore):
        sm = np.zeros((128, SM_N), f32)

        def put(name, arr):
            o, w = SM_OFF[name]
            sm[:, o:o + w] = np.asarray(arr, f32).reshape(128, w)

        put("lb0", _fm(hg_lb[0], 8))
        put("lb1", _fm(hg_lb[1], 8))
        for l in range(DEPTH):
            put(f"rcw{l}", np.asarray(rg_conv_w[l], f32).reshape(4, 10, 128).transpose(2, 1, 0))
            put(f"rcb{l}", _fm(rg_conv_b[l], 10))
            put(f"rbr{l}", _fm(rg_b_r[l], 10))
            put(f"rbi{l}", _fm(rg_b_i[l], 10))
            put(f"rlam{l}", _fm(rg_lambda[l], 10))
            put(f"fcw{l}", np.asarray(ffn_conv_w[l], f32).reshape(3, 22, 128).transpose(2, 1, 0))
            put(f"fcb{l}", _fm(ffn_conv_b[l], 22))
            put(f"n1{l}", _fm(norm1_w[l], 8))
            put(f"n2{l}", _fm(norm2_w[l], 8))
            put(f"hgn{l}", _fm(hg_norm_w[l], 8))
            put(f"h0{l}", _fm(state_rglru[l, b], 10))
            put(f"u0{l}", np.asarray(cache_rg_conv[l, b], f32).reshape(3, 10, 128).transpose(2, 1, 0))
            put(f"a0{l}", np.asarray(cache_ffn_conv[l, b], f32).reshape(2, 22, 128).transpose(2, 1, 0))
        m = dict(shared)
        m["x_p"] = np.ascontiguousarray(x_prompt[b])
        m["x_s"] = np.ascontiguousarray(x_sample[b], f32)
        m["st_ret"] = np.ascontiguousarray(state_ret[:, b], f32)
        m["st_hg"] = np.ascontiguousarray(state_hgrn[:, b], f32)
        m["smalls"] = sm
        in_maps.append(m)

    res = run_bass_kernel_spmd(nc, in_maps, core_ids=list(range(ncore)))
    R = res.results
    y_p = np.stack([R[b]["y_p"] for b in range(ncore)], 0)
    y_s = np.stack([R[b]["y_s"] for b in range(ncore)], 0)
    ret = np.stack([R[b]["ret_o"] for b in range(ncore)], 0)
    hg = np.stack([R[b]["hg_o"] for b in range(ncore)], 0)
    so = np.stack([R[b]["small_o"] for b in range(ncore)], 0)
    so = so.reshape(ncore, 128, 2, DEPTH, OUT_W)
    outs = [y_p.astype(f32), y_s.astype(f32)]
    outs.append(np.ascontiguousarray(ret[:, 0].transpose(1, 0, 2, 3, 4)))
    outs.append(np.ascontiguousarray(ret[:, 1].transpose(1, 0, 2, 3, 4)))
    outs.append(np.ascontiguousarray(hg[:, 0].transpose(1, 0, 2, 3, 4)))
    outs.append(np.ascontiguousarray(hg[:, 1].transpose(1, 0, 2, 3, 4)))
    for grp in range(2):
        pass
    hs = so[..., 0:10]
    uh = so[..., 10:40].reshape(ncore, 128, 2, DEPTH, 10, 3)
    ah = so[..., 40:84].reshape(ncore, 128, 2, DEPTH, 22, 2)
    for grp in range(2):
        pass
    rgl = [np.ascontiguousarray(hs[:, :, g].transpose(2, 0, 3, 1).reshape(DEPTH, ncore, RGW)) for g in range(2)]
    rgc_ = [np.ascontiguousarray(uh[:, :, g].transpose(2, 0, 4, 3, 1).reshape(DEPTH, ncore, 3, RGW)) for g in range(2)]
    ffc = [np.ascontiguousarray(ah[:, :, g].transpose(2, 0, 4, 3, 1).reshape(DEPTH, ncore, 2, DFF)) for g in range(2)]
    outs += [rgl[0], rgl[1], rgc_[0], rgc_[1], ffc[0], ffc[1]]
    return tuple(o.astype(f32) for o in outs)
```

```python
import numpy as np
from contextlib import ExitStack
import concourse.bass as bass
import concourse.mybir as mybir
from concourse.bass_utils import run_bass_kernel_spmd

F32 = mybir.dt.float32
BF16 = mybir.dt.bfloat16
AF = mybir.ActivationFunctionType
ALU = mybir.AluOpType

D = 1024
DEPTH = 2
SEQ_FULL = 8192
DSEQ = 16
PAST = 2048
TT = 512
INW = 12800
DFF = 2816
RGW = 1280
EPS = 1e-6
SLOT = 5120
NSLOT = 4


class Ins:
    __slots__ = ("q", "fn", "deps", "needed", "sem", "semname", "val", "dma", "inc", "tag", "cost", "odeps", "idx", "tset", "nbytes", "evac")


class Sched:
    def __init__(self):
        self.queues = {k: [] for k in ("pe", "act", "dve", "pool", "sp")}
        self.lw = {}
        self.rd = {}
        self.n = 0
        self.tag = "setup"
        self.sub = ""
        self.rawids = {}

    def op(self, q, fn, reads=(), writes=(), dma_sem=None, inc=16, cost=0.5, tset=None, nbytes=0, evac=False):
        ins = Ins()
        ins.evac = evac
        ins.cost = cost
        ins.tset = tset
        ins.nbytes = nbytes
        ins.odeps = []
        ins.idx = self.n
        ins.tag = self.tag + ((":" + self.sub) if self.sub else "")
        ins.q = q
        ins.fn = fn
        ins.needed = False
        ins.dma = dma_sem is not None
        ins.sem = dma_sem[1] if dma_sem else None
        ins.semname = dma_sem[0] if dma_sem else q
        ins.val = None
        ins.inc = inc
        deps = []
        for k in reads:
            w = self.lw.get(k)
            if w is not None:
                deps.append(w)
        nraw = len(deps)
        self._nraw = nraw
        for k in writes:
            w = self.lw.get(k)
            if w is not None:
                deps.append(w)
            r = self.rd.get(k)
            if r:
                deps.extend(r)
        dd = []
        seen = set()
        rawids = set(id(d) for d in deps[:nraw])
        self.rawids[id(ins)] = rawids
        for d in deps:
            if id(d) in seen or d is ins:
                continue
            seen.add(id(d))
            if q == "pe" and d.q == "pe" and not d.dma:
                ins.odeps.append(d)
                continue
            d.needed = True
            dd.append(d)
        ins.deps = dd
        for k in writes:
            self.lw[k] = ins
            self.rd[k] = []
        self.n += 1
        for k in reads:
            self.rd.setdefault(k, []).append(ins)
        self.queues[q].append(ins)
        return ins

    def reorder(self, window=None):
        import os as _os2
        _wp = int(_os2.environ.get("K_WPE", "256"))
        _we = int(_os2.environ.get("K_WEL", "64"))
        window = window or {"pe": _wp, "act": _we, "dve": _we, "pool": _we, "sp": 1}
        seg = {}
        pre = {}
        for q, lst in self.queues.items():
            cut = 0
            for i, ins in enumerate(lst):
                if ins.fn is None:
                    cut = i + 1
            pre[q] = lst[:cut]
            seg[q] = lst[cut:]
        inseg = set()
        for q in seg:
            for ins in seg[q]:
                inseg.add(id(ins))
        nun = {}
        users = {}
        for q in seg:
            for ins in seg[q]:
                c = 0
                for d in ins.deps + ins.odeps:
                    if id(d) in inseg:
                        c += 1
                        users.setdefault(id(d), []).append(ins)
                nun[id(ins)] = c
        done = {}
        rt = {}
        for q in seg:
            for ins in seg[q]:
                if nun[id(ins)] == 0:
                    rt[id(ins)] = 0.0
        pending = {q: list(seg[q]) for q in seg}
        tfree = {q: 0.0 for q in seg}
        out = {q: [] for q in seg}
        cur_t = [None]
        bus = [0.0]
        import os as _os
        LAT_X = float(_os.environ.get('K_LATX', '1.0'))
        LAT_S = 0.2
        EVAC_BONUS = float(_os.environ.get('K_EVAC', '1.5'))
        cand = {}
        self.idle_causes = {}
        dirty = set(seg.keys())
        total = sum(len(v) for v in seg.values())
        nsched = 0
        while nsched < total:
            for q in list(dirty):
                best = None
                lst = pending[q]
                W = window[q]
                tf = tfree[q]
                for j in range(min(W, len(lst))):
                    ins = lst[j]
                    r = rt.get(id(ins))
                    if r is None:
                        continue
                    st = r if r > tf else tf
                    key = st
                    if q == "act" and ins.tset is not None and ins.tset != cur_t[0]:
                        key = st + 1.3
                    if ins.evac:
                        key -= EVAC_BONUS
                    if best is None or key < best[0] - 1e-9:
                        best = (key, st, j, ins)
                    if key <= tf - EVAC_BONUS + 1e-9:
                        break
                cand[q] = best
            dirty.clear()
            bq = None
            for q, b in cand.items():
                if b is not None and (bq is None or b[0] < cand[bq][0]):
                    bq = q
            key, st, j, ins = cand[bq]
            tf0 = tfree[bq]
            lst = pending[bq]
            if lst[j] is not ins:
                j = lst.index(ins)
            del lst[j]
            cost = ins.cost
            if bq == "act" and ins.tset is not None and ins.tset != cur_t[0]:
                cost += 1.3
                cur_t[0] = ins.tset
            if ins.dma:
                tfree[bq] = st + 0.08
                s0 = max(bus[0], st)
                bus[0] = s0 + ins.nbytes / 180e3
                fin = bus[0] + 2.0
            else:
                fin = st + cost
                tfree[bq] = fin
            done[id(ins)] = fin
            if bq == "pe" and st - tf0 > 0.3:
                bd = None
                for d in ins.deps + ins.odeps:
                    if id(d) in done and (bd is None or done[id(d)] > done[id(bd)]):
                        bd = d
                kind = "RAW" if (bd is not None and id(bd) in self.rawids.get(id(ins), ())) else "WAR"
                k = (ins.tag, ((bd.q + "|" + bd.tag + "|" + kind) if bd is not None else "-"))
                self.idle_causes[k] = self.idle_causes.get(k, 0.0) + (st - tf0)
            out[bq].append(ins)
            nsched += 1
            dirty.add(bq)
            for u in users.get(id(ins), ()):
                nun[id(u)] -= 1
                lat = LAT_S if (u.q == ins.q and not ins.dma) else LAT_X
                v = fin + lat
                if rt.get(("p", id(u)), 0.0) < v:
                    rt[("p", id(u))] = v
                if nun[id(u)] == 0:
                    rt[id(u)] = rt.get(("p", id(u)), 0.0)
                    dirty.add(u.q)
        for q in seg:
            self.queues[q] = pre[q] + out[q]
        self.sim_makespan = max(tfree.values())

    def barrier(self):
        lasts = []
        for q in ("pe", "act", "dve", "pool"):
            lst = [i for i in self.queues[q] if i.fn is not None]
            if lst:
                lst[-1].needed = True
                lasts.append(lst[-1])
        for q in self.queues:
            ins = Ins()
            ins.evac = False
            ins.tag = "barrier"
            ins.q = q
            ins.fn = None
            ins.needed = False
            ins.dma = False
            ins.sem = None
            ins.semname = q
            ins.val = None
            ins.inc = 0
            ins.deps = list(lasts)
            self.queues[q].append(ins)

    def finalize(self, nc, engsems, block):
        dcnt = {}
        allsem = {}
        snap = None
        order = ["sp"] + [q for q in self.queues if q != "sp"]
        for q in order:
            lst = self.queues[q]
            c = 0
            for ins in lst:
                if ins.fn is None:
                    if q == "sp":
                        snap = dict(dcnt)
                        self._snaps = getattr(self, "_snaps", []) + [snap]
                    continue
                if ins.dma:
                    dcnt[ins.semname] = dcnt.get(ins.semname, 0) + ins.inc
                    ins.val = dcnt[ins.semname]
                    allsem[ins.semname] = ins.sem
                elif ins.needed:
                    c += 1
                    ins.val = c
                    ins.sem = engsems[q]

        snaps = getattr(self, "_snaps", [])

        def run(e, lst, final=False):
            waited = {}
            bi = 0
            for ins in lst:
                for d in ins.deps:
                    if waited.get(d.semname, 0) < d.val:
                        e.wait_ge(d.sem, d.val)
                        waited[d.semname] = d.val
                if ins.fn is None:
                    for name, tot in snaps[bi].items():
                        if waited.get(name, 0) < tot:
                            e.wait_ge(allsem[name], tot)
                            waited[name] = tot
                    bi += 1
                    continue
                r = ins.fn(e)
                if ins.dma:
                    r.then_inc(ins.sem, ins.inc)
                elif ins.needed:
                    r.then_inc(ins.sem, 1)
            if final:
                for name, tot in dcnt.items():
                    if waited.get(name, 0) < tot:
                        e.wait_ge(allsem[name], tot)

        qs = self.queues

        @block.sync
        def _(e):
            run(e, qs["sp"], final=True)

        @block.tensor
        def _(e):
            run(e, qs["pe"])

        @block.scalar
        def _(e):
            run(e, qs["act"])

        @block.vector
        def _(e):
            run(e, qs["dve"])

        @block.gpsimd
        def _(e):
            run(e, qs["pool"])


class BufPool:
    def __init__(self, items):
        self.free_list = list(items)
        self.total = len(items)

    def alloc(self):
        if not self.free_list:
            raise RuntimeError("pool exhausted")
        return self.free_list.pop(0)

    def free(self, b):
        self.free_list.append(b)


def weight_groups():
    g = []

    def win(name, c0, n):
        g.append((name, 8, n, [("w_in", 0, c0, n, 0)], "norm1"))

    win("rgate0", 2048, 512)
    win("rgate1", 2560, 512)
    win("rq", 0, 512)
    win("rk", 512, 512)
    win("rv0", 1024, 512)
    win("rv1", 1536, 512)
    win("hgate0", 6144, 512)
    win("hgate1", 6656, 512)
    win("hi0", 5120, 512)
    win("hi1", 5632, 512)
    win("hq0", 3072, 512)
    win("hf0", 4096, 512)
    win("hq1", 3584, 512)
    win("hf1", 4608, 512)
    win("ry0", 8448, 512)
    win("ry1", 8960, 512)
    win("ry2", 9472, 256)
    g.append(("rgw", 2, 2560, None, None))
    win("ru0", 7168, 512)
    win("ru1", 7680, 512)
    win("ru2", 8192, 256)
    brow = [0, 1024, 2048]
    bkc = [8, 8, 10]
    for b in range(3):
        for jh in range(2):
            win(f"mg{b}{jh}", 9728 + b * 1024 + jh * 512, 512)
            g.append((f"wb{b}{jh}", bkc[b], 512, [("w_branch", brow[b], jh * 512, 512, 0)], "hgn" if b == 1 else None))
    g.append(("wo0", 8, 512, [("w_out", 0, 0, 512, 0)], None))
    g.append(("wo1", 8, 512, [("w_out", 0, 512, 512, 0)], None))
    for i in range(11):
        g.append((f"wu{i}", 8, 512, [("w_up", 0, 256 * i, 256, 0), ("w_up", 0, DFF + 256 * i, 256, 256)], "norm2"))
    for half in range(2):
        for kg, (k0, kn) in enumerate([(0, 8), (8, 8), (16, 6)]):
            g.append((f"wd{half}{kg}", kn, 512, [("w_down", k0 * 128, half * 512, 512, 0)], None))
    return g


def smalls_layout():
    off = {}
    c = 0

    def add(name, n):
        nonlocal c
        off[name] = (c, n)
        c += n

    add("lb0", 8)
    add("lb1", 8)
    for l in range(DEPTH):
        add(f"rcw{l}", 40)
        add(f"rcb{l}", 10)
        add(f"rbr{l}", 10)
        add(f"rbi{l}", 10)
        add(f"rlam{l}", 10)
        add(f"fcw{l}", 66)
        add(f"fcb{l}", 22)
        add(f"n1{l}", 8)
        add(f"n2{l}", 8)
        add(f"hgn{l}", 8)
        add(f"h0{l}", 10)
        add(f"u0{l}", 30)
        add(f"a0{l}", 44)
    return off, c


SM_OFF, SM_N = smalls_layout()
OUT_W = 84


def build_program(SEQ):
    NTILE = SEQ // TT
    nc = bass.Bass("TRN2", target_bir_lowering=False)
    S = Sched()
    es = ExitStack()

    def din(name, shape, dt=F32):
        return nc.dram_tensor(name, list(shape), dt, kind="ExternalInput").ap()

    def dout(name, shape, dt=F32):
        return nc.dram_tensor(name, list(shape), dt, kind="ExternalOutput").ap()

    x_p = din("x_p", [SEQ, D])
    x_s = din("x_s", [DSEQ, D])
    st_ret = din("st_ret", [DEPTH, 4, 128, 256])
    st_hg = din("st_hg", [DEPTH, 8, 128, 128])
    smalls_d = din("smalls", [128, SM_N])
    wfin_d = din("wfin", [128, D])
    W = {
        "w_in": din("w_in", [DEPTH, D, INW]),
        "w_branch": din("w_branch", [DEPTH, 3328, D]),
        "w_out": din("w_out", [DEPTH, D, D]),
        "w_up": din("w_up", [DEPTH, D, 2 * DFF]),
        "w_down": din("w_down", [DEPTH, DFF, D]),
    }
    rg_w_r = din("rg_w_r", [DEPTH, 5, 256, 256])
    rg_w_i = din("rg_w_i", [DEPTH, 5, 256, 256])
    ident_d = din("ident", [128, 128])
    perm_d = din("permT", [128, 128])
    rotc_p = din("rotc_p", [128, SEQ])
    rots_p = din("rots_p", [128, SEQ])
    rotc_s = din("rotc_s", [128, DSEQ])
    rots_s = din("rots_s", [128, DSEQ])
    retmask_d = din("retmask", [2, 128, 4, 128])
    qd_d = din("qd", [2, 128, 4, 128])
    kd_d = din("kd", [2, 128, 4, 128])
    hgmask_d = din("hgmask", [2, 128, 128])
    scanmask_d = din("scanmask", [2, 128, TT])

    y_p = dout("y_p", [SEQ, D])
    y_s = dout("y_s", [DSEQ, D])
    ret_o = dout("ret_o", [2, DEPTH, 4, 128, 256])
    hg_o = dout("hg_o", [2, DEPTH, 8, 128, 128])
    small_o = dout("small_o", [128, 2 * DEPTH * OUT_W])

    groups = weight_groups()
    NG = len(groups)
    wscr = nc.dram_tensor("wscr", [DEPTH * NG, 128, SLOT], BF16, kind="Internal").ap()

    def sem(name):
        return (name, es.enter_context(nc.semaphore(name)))

    engsems = {q: sem("e_" + q)[1] for q in ("pe", "act", "dve", "pool")}
    nsem = {"i": 0}

    def newsem():
        nsem["i"] += 1
        return sem(f"d_m{nsem['i']}")

    sem_x = [sem(f"d_x{b}") for b in range(4)]
    sem_y = [sem(f"d_y{b}") for b in range(4)]
    sem_slot = [sem(f"d_w{i}") for i in range(NSLOT)]
    sem_stg = [sem(f"d_stg{i}") for i in range(2)]
    sem_scr = [sem(f"d_scr{i}") for i in range(2)]
    sem_rot = sem("d_rot")
    sem_out = sem("d_out")
    sem_gc = [sem(f"d_gc{i}") for i in range(5)]
    sem_sti = [sem(f"d_sti{i}") for i in range(4)]
    sem_sto = [sem(f"d_sto{i}") for i in range(4)]

    def sb(name, shape, dt=F32):
        return es.enter_context(nc.sbuf_tensor(name, list(shape), dt))

    smalls = sb("smalls_sb", [128, SM_N])
    K_SM = ("smalls",)

    def smc(name, a=0, n=None):
        o, w = SM_OFF[name]
        if n is None:
            n = w - a
        return smalls[:, o + a:o + a + n]

    def isps(*aps):
        for a_ in aps:
            try:
                if a_.space == mybir.MemoryType.PSUM:
                    return True
            except Exception:
                pass
        return False

    def fsz(ap):
        n = 1
        for d in ap.shape[1:]:
            n *= int(d)
        return n

    TSET = {AF.Silu: "silu", AF.Sigmoid: "sig", AF.Exp: "exp", AF.Ln: "ln", AF.Sqrt: "sqrt", AF.Gelu_apprx_tanh: "gelu"}

    def ACT(out, in_, func, reads, writes, bias=None, scale=None, accum=None):
        kw = {}
        if bias is not None:
            kw["bias"] = bias
        if scale is not None:
            kw["scale"] = scale
        if accum is not None:
            kw["accum_out"] = accum
        c = 0.22 + fsz(out) / 1400.0 + (0.15 if accum is not None else 0.0)
        return S.op("act", lambda e: e.activation(out=out, in_=in_, func=func, **kw), reads, writes, cost=c, tset=TSET.get(func), evac=isps(in_))

    def ecost(q, n, mul=1.0):
        if q == "dve":
            return 0.12 + mul * n / 960.0
        return 0.25 + mul * n / 700.0

    def TT_(q, out, in0, in1, op, reads, writes):
        return S.op(q, lambda e: e.tensor_tensor(out=out, in0=in0, in1=in1, op=op), reads, writes, cost=ecost(q, fsz(out)), evac=isps(in0, in1))

    def TS(q, out, in0, s1, s2, op0, op1, reads, writes):
        if s2 is None:
            return S.op(q, lambda e: e.tensor_scalar(out=out, in0=in0, scalar1=s1, scalar2=None, op0=op0), reads, writes,
                        cost=ecost(q, fsz(out)), evac=isps(in0))
        return S.op(q, lambda e: e.tensor_scalar(out=out, in0=in0, scalar1=s1, scalar2=s2, op0=op0, op1=op1), reads, writes,
                    cost=ecost(q, fsz(out)), evac=isps(in0))

    def STT(out, in0, scalar, in1, op0, op1, reads, writes):
        return S.op("dve", lambda e: e.scalar_tensor_tensor(out=out, in0=in0, scalar=scalar, in1=in1, op0=op0, op1=op1), reads, writes,
                    cost=ecost("dve", fsz(out)), evac=isps(in0, in1))

    def CP(q, out, in_, reads, writes):
        return S.op(q, lambda e: e.tensor_copy(out=out, in_=in_), reads, writes, cost=ecost(q, fsz(out)), evac=isps(in_))

    def MM(out, lhsT, rhs, start, stop, reads, writes):
        n = max(64, fsz(rhs))
        c = 0.03 + n / 1950.0
        if rhs.dtype == F32:
            c *= 4
        return S.op("pe", lambda e: e.matmul(out, lhsT=lhsT, rhs=rhs, start=start, stop=stop), reads, writes, cost=c)

    def TR(out, in_, idn, reads, writes):
        return S.op("pe", lambda e: e.transpose(out=out, in_=in_, identity=idn), reads, writes, cost=0.1)

    def DMA(out, in_, reads, writes, semt, nc_ok=False):
        nb = int(out.shape[0]) * fsz(out) * (2 if out.dtype == BF16 else 4)
        if nc_ok:
            return S.op("sp", lambda e: e.dma_start(out=out, in_=in_, allow_slow_non_contiguous=True), reads, writes, dma_sem=semt, nbytes=nb)
        return S.op("sp", lambda e: e.dma_start(out=out, in_=in_), reads, writes, dma_sem=semt, nbytes=nb)

    def MEMSET(q, ap, val, writes):
        return S.op(q, lambda e: e.memset(ap, val), (), writes, cost=ecost(q, fsz(ap)))

    rr = {"i": 0}

    def anyq():
        rr["i"] += 1
        return ("dve", "pool")[rr["i"] % 2]

    DMA(smalls[:], smalls_d[:, :], (), [K_SM], newsem())

    S.tag = "phase0"
    es0 = ExitStack()
    stg = [es0.enter_context(nc.sbuf_tensor(f"stg{i}", [128, SLOT], F32)) for i in range(2)]
    cvt = [es0.enter_context(nc.sbuf_tensor(f"cvt{i}", [128, SLOT], BF16)) for i in range(2)]
    allg = [(l, gidx) for l in range(DEPTH) for gidx in range(NG)]

    def p0_load(gi):
        l, gidx = allg[gi]
        name, kcn, ncols, pieces, scale = groups[gidx]
        i = gi % 2
        n = kcn * ncols
        sv = stg[i][:, 0:n].rearrange("p (k c) -> p k c", c=ncols)
        if name == "rgw":
            for gt, wsrc in enumerate((rg_w_r, rg_w_i)):
                dst = stg[i][:, gt * 2560:(gt + 1) * 2560].rearrange("p (n k e) -> p n k e", n=5, k=2)
                for nb in range(5):
                    DMA(dst[:, nb, :, :], wsrc[l, nb].rearrange("(k p) e -> p k e", p=128), (), [("stg", i)], sem_stg[i])
        else:
            for (tn, r0, c0, npc, dc) in pieces:
                src = W[tn][l, r0:r0 + kcn * 128, c0:c0 + npc].rearrange("(k p) c -> p k c", p=128)
                DMA(sv[:, :, dc:dc + npc], src, (), [("stg", i)], sem_stg[i])

    def p0_cvt(gi):
        l, gidx = allg[gi]
        name, kcn, ncols, pieces, scale = groups[gidx]
        i = gi % 2
        n = kcn * ncols
        sv = stg[i][:, 0:n].rearrange("p (k c) -> p k c", c=ncols)
        cvv = cvt[i][:, 0:n].rearrange("p (k c) -> p k c", c=ncols)
        if name == "rgw":
            CP("pool", cvt[i][:, 0:5120], stg[i][:, 0:5120], [("stg", i)], [("cvt", i)])
            n = 5120
        elif scale is None:
            ACT(cvv, sv, AF.Copy, [("stg", i)], [("cvt", i)])
        else:
            scn = {"norm1": f"n1{l}", "norm2": f"n2{l}", "hgn": f"hgn{l}"}[scale]
            sc = smc(scn).unsqueeze(2).broadcast_to([128, kcn, ncols])
            TT_(("dve", "pool")[(gi // 2) % 2], cvv, sv, sc, ALU.mult, [("stg", i), K_SM], [("cvt", i)])
        DMA(wscr[l * NG + gidx, :, 0:n], cvt[i][:, 0:n], [("cvt", i)], [("wscr", l, gidx)], sem_scr[i])

    p0_load(0)
    for gi in range(len(allg)):
        if gi + 1 < len(allg):
            p0_load(gi + 1)
        p0_cvt(gi)

    S.barrier()
    es0.close()

    xt = sb("xt", [128, 4, D])
    xnb = [sb(f"xnb{i}", [128, D], BF16) for i in range(2)]
    junk = sb("junk", [128, D], BF16)
    hnT = sb("hnT", [128, 8, TT], BF16)
    Vt = sb("Vt", [128, 4, D], BF16)
    big = sb("big", [128, 26, TT], BF16)
    NF = 13
    NBF = 16
    fbufs = [sb(f"f{i}", [128, TT]) for i in range(NF)]
    bbufs = [sb(f"b{i}", [128, TT], BF16) for i in range(NBF)]
    sbb = [sb(f"sbb{i}", [128, 1024], BF16) for i in range(2)]
    sall = [sb(f"sall{i}", [128, 1024]) for i in range(2)]
    ubuf = [sb(f"ubuf{i}", [128, 3 + TT]) for i in range(2)]
    abuf = [sb(f"abuf{i}", [128, 2 + TT]) for i in range(2)]
    wslot = [sb(f"wslot{i}", [128, SLOT], BF16) for i in range(NSLOT)]
    rotc = sb("rotc", [128, TT])
    rots = sb("rots", [128, TT])
    wfin = sb("wfin_sb", [128, D])
    ident = sb("ident_sb", [128, 128], BF16)
    identf = sb("identf_sb", [128, 128])
    ones = sb("ones_sb", [128, 128], BF16)
    permT = sb("perm_sb", [128, 128])
    retmask = sb("retmask_sb", [128, 4, 128])
    qd = sb("qd_sb", [128, 4, 128])
    kd = sb("kd_sb", [128, 4, 128])
    hgmask = sb("hgmask_sb", [128, 128])
    scanmask = sb("scanmask_sb", [128, TT])
    S_ret = [sb(f"S_ret{l}", [128, 4, 256]) for l in range(DEPTH)]
    S_hg = [sb(f"S_hg{l}", [128, 8, 128]) for l in range(DEPTH)]
    hstate = [sb(f"hstate{l}", [128, 10]) for l in range(DEPTH)]
    uhalo = [sb(f"uhalo{l}", [128, 10, 3]) for l in range(DEPTH)]
    ahalo = [sb(f"ahalo{l}", [128, 22, 2]) for l in range(DEPTH)]
    outst = sb("outst", [128, 2 * DEPTH * OUT_W])
    stat = sb("stat", [128, 16])
    lbt = sb("lbt", [128, 2, 8])
    omlt = sb("omlt", [128, 2, 8])
    rgc = sb("rgc", [128, 2, 10])
    rgc2 = sb("rgc2", [128, 2, 10])
    ebc = [sb(f"ebc{i}", [128, 8]) for i in range(4)]
    epsb = sb("epsb", [128, 1])

    psum = [es.enter_context(nc.psum_tensor(f"ps{i}", [128, TT], F32)) for i in range(8)]

    FP = BufPool([(fbufs[i], ("f", i)) for i in range(NF)])
    BP = BufPool([(bbufs[i], ("b", i)) for i in range(NBF)])
    PP = BufPool([(psum[i], ("ps", i)) for i in range(8)])
    WP = BufPool(list(range(NSLOT)))

    DMA(wfin[:], wfin_d[:, :], (), [("wfin",)], newsem())
    DMA(identf[:], ident_d[:, :], (), [("identf",)], newsem())
    DMA(permT[:], perm_d[:, :], (), [("permT",)], newsem())
    CP("pool", ident[:], identf[:], [("identf",)], [("ident",)])
    MEMSET("pool", ones[:], 1.0, [("ones",)])
    MEMSET("pool", epsb[:], EPS, [("epsb",)])
    MEMSET("pool", lbt[:, 0, :], 0.0, [("lbt",)])
    TT_("dve", lbt[:, 1, :], smc("lb1"), smc("lb0"), ALU.subtract, [K_SM], [("lbt",)])
    ACT(lbt[:, 1, :], lbt[:, 1, :], AF.Sigmoid, [("lbt",)], [("lbt",)])
    TS("dve", omlt[:], lbt[:], -1.0, 1.0, ALU.mult, ALU.add, [("lbt",)], [("omlt",)])
    for l in range(DEPTH):
        ACT(rgc[:, l, :], smc(f"rlam{l}"), AF.Exp, [K_SM], [("rgc",)], scale=-1.0)
        TS("dve", rgc[:, l, :], rgc[:, l, :], 1.0, None, ALU.add, None, [("rgc",)], [("rgc",)])
        ACT(rgc[:, l, :], rgc[:, l, :], AF.Ln, [("rgc",)], [("rgc",)])
    TS("dve", rgc[:], rgc[:], -8.0, None, ALU.mult, None, [("rgc",)], [("rgc",)])
    TS("dve", rgc2[:], rgc[:], 2.0, None, ALU.mult, None, [("rgc",)], [("rgc2",)])

    wseq = []
    wstate = {"next": 0, "cur": -1}
    wloaded = {}

    def wpump():
        while wstate["next"] < len(wseq) and wstate["next"] <= wstate["cur"] + NSLOT and WP.free_list:
            l, gidx = wseq[wstate["next"]]
            name, kcn, ncols, pieces, scale = groups[gidx]
            n = 5120 if name == "rgw" else kcn * ncols
            s = WP.alloc()
            DMA(wslot[s][:, 0:n], wscr[l * NG + gidx, :, 0:n], [("wscr", l, gidx)], [("wslot", s)], sem_slot[s])
            wloaded[wstate["next"]] = s
            wstate["next"] += 1

    def wget(l, name):
        wstate["cur"] += 1
        k = wstate["cur"]
        ll, gidx = wseq[k]
        assert ll == l and groups[gidx][0] == name, (ll, l, groups[gidx][0], name)
        wpump()
        assert k in wloaded
        s = wloaded.pop(k)
        _, kcn, ncols, _, _ = groups[gidx]
        view = wslot[s][:, 0:kcn * ncols].rearrange("p (k c) -> p k c", c=ncols)
        return s, view

    def wfree(s):
        WP.free(s)
        wpump()

    def emit_tile(grp, ti, T, xsrc, ydst):
        tb = min(128, T)
        NB = T // tb
        first = (ti == 0)

        for b in range(NB):
            DMA(xt[:tb, b, :], xsrc[b * tb:(b + 1) * tb, :], (), [("x", b)], sem_x[b])

        def norm_to_hnT(l):
            for b in range(NB):
                ACT(junk[:tb, :], xt[:tb, b, :], AF.Square, [("x", b)], [("junk",), ("stat", b)], accum=stat[:tb, b:b + 1])
                ACT(stat[:tb, 4 + b:5 + b], stat[:tb, b:b + 1], AF.Sqrt, [("stat", b), ("epsb",)], [("stat2", b)],
                    bias=epsb[:tb, 0:1], scale=1.0 / D)
                S.op("dve", lambda e, b=b: e.reciprocal(out=stat[:tb, 8 + b:9 + b], in_=stat[:tb, 4 + b:5 + b]),
                     [("stat2", b)], [("stat3", b)], cost=0.15)
                xb_ = xnb[b % 2]
                TS("dve", xb_[:tb, :], xt[:tb, b, :], stat[:tb, 8 + b:9 + b], None, ALU.mult, None,
                   [("x", b), ("stat3", b)], [("xnb", b % 2)])
                ps, pk = PP.alloc()
                pv = ps[:].bitcast(BF16)
                for kc in range(8):
                    TR(pv[:, kc * 128:kc * 128 + tb], xb_[:tb, kc * 128:(kc + 1) * 128], ident[:tb, :tb],
                       [("xnb", b % 2), ("ident",)], [pk])
                src = pv.rearrange("p (k t) -> p k t", t=128)[:, :, 0:tb]
                if b % 2 == 0:
                    ACT(hnT[:, :, b * tb:(b + 1) * tb], src, AF.Copy, [pk], [("hnT", b)])
                else:
                    CP("dve", hnT[:, :, b * tb:(b + 1) * tb], src, [pk], [("hnT", b)])
                PP.free((ps, pk))

        HN_ALL = [("hnT", b) for b in range(4)]

        def proj_fm(wv, cc, reads_extra=()):
            ps, pk = PP.alloc()
            for kc in range(8):
                MM(ps[:, 0:T], wv[:, kc, cc * 128:(cc + 1) * 128], hnT[:, kc, 0:T], kc == 0, kc == 7,
                   HN_ALL[:NB] + list(reads_extra), [pk])
            return ps, pk

        def proj_tm_to_V(wv, ws, coff):
            for b in range(NB):
                ps, pk = PP.alloc()
                for kc in range(8):
                    MM(ps[:tb, 0:512], hnT[:, kc, b * tb:(b + 1) * tb], wv[:, kc, 0:512], kc == 0, kc == 7,
                       [("hnT", b), ("wslot", ws)], [pk])
                if b % 2 == 0:
                    ACT(Vt[:tb, b, coff:coff + 512], ps[:tb, 0:512], AF.Copy, [pk], [("V", b)])
                else:
                    CP("dve", Vt[:tb, b, coff:coff + 512], ps[:tb, 0:512], [pk], [("V", b)])
                PP.free((ps, pk))

        def la_phase1(QT, qk, KT, kk, KE, kek, ke_scale, vcol, dv, maskap, maskkey, seg, S32, skey, decs, dec_reads, sbi):
            nseg_b = tb // seg
            nseg = NB * nseg_b
            S.sub = "la_tr"
            ps, pk = PP.alloc()
            pv = ps[:].bitcast(BF16)
            for b in range(NB):
                TR(pv[:tb, b * 128:(b + 1) * 128], KE[:, b * tb:(b + 1) * tb], ident[:, :], [kek, ("ident",)], [pk])
            ketm, ketk = BP.alloc()
            ACT(ketm[:tb, 0:NB * 128], pv[:tb, 0:NB * 128], AF.Copy, [pk], [ketk], scale=ke_scale)
            PP.free((ps, pk))
            S.sub = "la_U"
            per_bank = 512 // dv
            nbank = max(nseg_b, (nseg + per_bank - 1) // per_bank)
            ubanks = [PP.alloc() for _ in range(nbank)]

            def uloc(si):
                if nseg_b > 1:
                    return si % nseg_b, (si // nseg_b) * dv
                return si // per_bank, (si % per_bank) * dv

            for si in range(nseg):
                bi_, o0 = uloc(si)
                ps, pk = ubanks[bi_]
                b = si // nseg_b
                p0 = (si % nseg_b) * seg
                MM(ps[:, o0:o0 + dv], ketm[p0:p0 + seg, b * 128:(b + 1) * 128], Vt[p0:p0 + seg, b, vcol:vcol + dv], True, True,
                   [ketk, ("V", b)], [pk])
            BP.free((ketm, ketk))
            S.sub = "la_sc"
            sps, spk = PP.alloc()
            for b in range(NB):
                MM(sps[:tb, b * 128:b * 128 + tb], KT[:, b * tb:(b + 1) * tb], QT[:, b * tb:(b + 1) * tb], True, True,
                   [kk, qk], [spk])
            S.sub = "la_chain"
            sbt = sbb[sbi]
            sbk = ("sbb", sbi)
            sbv = sbt[:, 0:nseg * dv].rearrange("p (s v) -> p s v", v=dv)
            sal = sall[sbi]
            salk = ("sall", sbi)
            sav = sal[:, 0:nseg * dv].rearrange("p (s v) -> p s v", v=dv)
            CP("pool", sbv[:, 0, :], S32, [skey], [sbk])
            prev, prevk = S32, skey
            for si in range(nseg):
                bi_, o0 = uloc(si)
                ps, pk = ubanks[bi_]
                if si == nseg - 1:
                    out, outk = S32, skey
                else:
                    out, outk = sav[:, si, :], salk
                STT(out, prev, decs(si), ps[:, o0:o0 + dv], ALU.mult, ALU.add, [prevk, pk] + list(dec_reads), [outk])
                prev, prevk = out, outk
            for u in ubanks:
                PP.free(u)
            if nseg > 1:
                CP("pool", sbv[:, 1:nseg, :], sav[:, 0:nseg - 1, :], [salk], [sbk])
            S.sub = "la_mask"
            scm, sck = BP.alloc()
            psv = sps[:tb, 0:NB * 128].rearrange("p (b t) -> p b t", t=128)[:, :, 0:tb]
            scv = scm[:tb, 0:NB * 128].rearrange("p (b t) -> p b t", t=128)[:, :, 0:tb]
            TT_("dve", scv, psv, maskap.unsqueeze(1).broadcast_to([tb, NB, tb]), ALU.mult, [spk, maskkey], [sck])
            PP.free((sps, spk))
            S.sub = ""
            return dict(QT=QT, qk=qk, scm=scm, sck=sck, sbv=sbv, sbk=sbk, vcol=vcol, dv=dv, seg=seg)

        def la_phase2(c):
            QT, qk, scm, sck, sbv, sbk, vcol, dv, seg = (c[k] for k in ("QT", "qk", "scm", "sck", "sbv", "sbk", "vcol", "dv", "seg"))
            nd = dv // 128
            nseg_b = tb // seg
            outs = []
            S.sub = "la_o"
            for d in range(nd):
                ps, pk = PP.alloc()
                for b in range(NB):
                    MM(ps[:, b * tb:(b + 1) * tb], Vt[:tb, b, vcol + d * 128:vcol + (d + 1) * 128],
                       scm[:tb, b * 128:b * 128 + tb], True, False, [("V", b), sck], [pk])
                    for sj in range(nseg_b):
                        si = b * nseg_b + sj
                        t0 = b * tb + sj * seg
                        MM(ps[:, t0:t0 + seg], sbv[:, si, d * 128:(d + 1) * 128], QT[:, t0:t0 + seg], False, sj == nseg_b - 1,
                           [sbk, qk], [pk])
                outs.append((ps, pk))
            BP.free((scm, sck))
            S.sub = "la_sq"
            sqs = []
            for (ps, pk) in outs:
                sq, sqk = BP.alloc()
                ACT(sq[:, 0:T], ps[:, 0:T], AF.Square, [pk], [sqk])
                sqs.append((sq, sqk))
            S.sub = ""
            return outs, sqs

        def postproc(outs, sqs, kc0, dvtot):
            S.sub = "post"
            ss, ssk = PP.alloc()
            for i, (sq, sqk) in enumerate(sqs):
                MM(ss[:, 0:T], ones[:, :], sq[:, 0:T], i == 0, i == len(sqs) - 1, [("ones",), sqk], [ssk])
            for b_ in sqs:
                BP.free(b_)
            rs, rsk = FP.alloc()
            ACT(rs[:, 0:T], ss[:, 0:T], AF.Sqrt, [ssk, ("epsb",)], [rsk], bias=epsb[:, 0:1], scale=1.0 / dvtot)
            PP.free((ss, ssk))
            S.op("dve", lambda e: e.reciprocal(out=rs[:, 0:T], in_=rs[:, 0:T]), [rsk], [rsk], cost=0.12 + T / 960.0)
            for d, (ps, pk) in enumerate(outs):
                tmp, tk = FP.alloc()
                TT_("dve", tmp[:, 0:T], ps[:, 0:T], rs[:, 0:T], ALU.mult, [pk, rsk], [tk])
                PP.free((ps, pk))
                TT_("pool", big[:, kc0 + d, 0:T], tmp[:, 0:T], big[:, kc0 + d, 0:T], ALU.mult, [tk, ("big", kc0 + d)], [("big", kc0 + d)])
                FP.free((tmp, tk))
            FP.free((rs, rsk))
            S.sub = ""

        def pipeline(n, stages):
            for step in range(n + len(stages) - 1):
                for k, f in enumerate(stages):
                    i = step - k
                    if f is not None and 0 <= i < n:
                        f(i)

        if grp == 0:
            DMA(rotc[:, 0:T], rotc_p[:, ti * TT:ti * TT + T], (), [("rot",)], sem_rot)
            DMA(rots[:, 0:T], rots_p[:, ti * TT:ti * TT + T], (), [("rot",)], sem_rot)
        else:
            DMA(rotc[:, 0:T], rotc_s[:, 0:T], (), [("rot",)], sem_rot)
            DMA(rots[:, 0:T], rots_s[:, 0:T], (), [("rot",)], sem_rot)

        lg = [float(np.log1p(-2.0 ** (-5.0 - h))) for h in range(4)]
        Lblk = tb

        for l in range(DEPTH):
            S.tag = "norm1"
            norm_to_hnT(l)

            S.tag = "ret"
            for gname, kcb in (("rgate0", 0), ("rgate1", 4)):
                ws, wv = wget(l, gname)
                for cc in range(4):
                    ps, pk = proj_fm(wv, cc, [("wslot", ws)])
                    ACT(big[:, kcb + cc, 0:T], ps[:, 0:T], AF.Silu, [pk], [("big", kcb + cc)])
                    PP.free((ps, pk))
                wfree(ws)
            QK = {}
            rws = {}
            rctx = {}

            def rot_a(i, l=l):
                gname = ("rq", "rk")[i // 4]
                h = i % 4
                if h == 0:
                    rws[gname] = wget(l, gname)
                ws, wv = rws[gname]
                ps, pk = proj_fm(wv, h, [("wslot", ws)])
                q32, q32k = FP.alloc()
                ACT(q32[:, 0:T], ps[:, 0:T], AF.Copy, [pk], [q32k])
                PP.free((ps, pk))
                rctx[i] = (q32, q32k)
                if h == 3:
                    wfree(ws)

            def rot_b(i):
                gname = ("rq", "rk")[i // 4]
                tab = (qd, kd)[i // 4]
                h = i % 4
                q32, q32k = rctx.pop(i)
                pq, pqk = PP.alloc()
                S.tag = "ret_perm"
                MM(pq[:, 0:T], permT[:, :], q32[:, 0:T], True, True, [("permT",), q32k], [pqk])
                S.tag = "ret"
                t1, t1k = FP.alloc()
                TT_("pool", t1[:, 0:T], q32[:, 0:T], rotc[:, 0:T], ALU.mult, [q32k, ("rot",)], [t1k])
                t2, t2k = FP.alloc()
                TT_("dve", t2[:, 0:T], pq[:, 0:T], rots[:, 0:T], ALU.mult, [pqk, ("rot",)], [t2k])
                PP.free((pq, pqk))
                FP.free((q32, q32k))
                TT_("pool", t1[:, 0:T], t1[:, 0:T], t2[:, 0:T], ALU.add, [t1k, t2k], [t1k])
                FP.free((t2, t2k))
                o, ok = BP.alloc()
                ov = o[:, 0:T].rearrange("p (b t) -> p b t", t=tb)
                tv = t1[:, 0:T].rearrange("p (b t) -> p b t", t=tb)
                TT_("dve", ov, tv, tab[:, h, 0:tb].unsqueeze(1).broadcast_to([128, NB, tb]), ALU.mult,
                    [t1k, ("rtabq",), ("rtabk",)], [ok])
                FP.free((t1, t1k))
                QK[(gname, h)] = (o, ok)

            pipeline(8, [rot_a, rot_b])
            for gname, coff in (("rv0", 0), ("rv1", 512)):
                ws, wv = wget(l, gname)
                proj_tm_to_V(wv, ws, coff)
                wfree(ws)
            lctx = {}

            def ret_b(h, l=l):
                QT, qk_ = QK[("rq", h)]
                KT, kk_ = QK[("rk", h)]
                gL = float(np.exp(lg[h] * Lblk))
                lctx[h] = la_phase1(QT, qk_, KT, kk_, KT, kk_, gL, h * 256, 256,
                                    retmask[:tb, h, 0:tb], ("rmask",), Lblk, S_ret[l][:, h, :], ("S_ret", l, h),
                                    lambda si, gL=gL: gL, (), h % 2)
                BP.free((KT, kk_))

            def ret_c(h):
                c = lctx[h]
                c["outs"], c["sqs"] = la_phase2(c)
                BP.free((c["QT"], c["qk"]))

            def ret_d(h):
                c = lctx.pop(h)
                postproc(c["outs"], c["sqs"], 2 * h, 256)

            pipeline(4, [ret_b, ret_c, ret_d])

            S.tag = "hgrn"
            for gname, kcb in (("hgate0", 8), ("hgate1", 12)):
                ws, wv = wget(l, gname)
                for cc in range(4):
                    ps, pk = proj_fm(wv, cc, [("wslot", ws)])
                    ACT(big[:, kcb + cc, 0:T], ps[:, 0:T], AF.Silu, [pk], [("big", kcb + cc)])
                    PP.free((ps, pk))
                wfree(ws)
            for gname, coff in (("hi0", 0), ("hi1", 512)):
                ws, wv = wget(l, gname)
                proj_tm_to_V(wv, ws, coff)
                wfree(ws)
            hq = {}
            hws = {}
            hctx = {}
            seg = min(64, T)
            nsg = T // seg

            def hg_a(hp, l=l):
                H2 = []
                for h in (2 * hp, 2 * hp + 1):
                    hh, cc = h // 4, h % 4
                    if cc == 0:
                        ws, wv = wget(l, f"hq{hh}")
                        for c2 in range(4):
                            ps, pk = proj_fm(wv, c2, [("wslot", ws)])
                            q, qk_ = FP.alloc()
                            ACT(q[:, 0:T], ps[:, 0:T], AF.Silu, [pk], [qk_])
                            PP.free((ps, pk))
                            hq[hh * 4 + c2] = (q, qk_)
                        wfree(ws)
                        hws[hh] = wget(l, f"hf{hh}")
                    ws, wv = hws[hh]
                    ps, pk = proj_fm(wv, cc, [("wslot", ws)])
                    if cc == 3:
                        wfree(ws)
                    f, fk = FP.alloc()
                    k1, k1k = FP.alloc()
                    B, Bk = FP.alloc()
                    H2.append(dict(h=h, ps=ps, pk=pk, f=f, fk=fk, k1=k1, k1k=k1k, B=B, Bk=Bk, eb=ebc[h % 4]))
                for c in H2:
                    ACT(c["f"][:, 0:T], c["ps"][:, 0:T], AF.Sigmoid, [c["pk"]], [c["fk"]])
                    PP.free((c["ps"], c["pk"]))
                for c in H2:
                    h = c["h"]
                    TS("dve", c["f"][:, 0:T], c["f"][:, 0:T], omlt[:, l, h:h + 1], lbt[:, l, h:h + 1], ALU.mult, ALU.add,
                       [c["fk"], ("omlt",), ("lbt",)], [c["fk"]])
                for c in H2:
                    TS("pool", c["k1"][:, 0:T], c["f"][:, 0:T], -1.0, 1.0, ALU.mult, ALU.add, [c["fk"]], [c["k1k"]])
                for c in H2:
                    TS("dve", c["f"][:, 0:T], c["f"][:, 0:T], 1e-6, None, ALU.max, None, [c["fk"]], [c["fk"]])
                for c in H2:
                    ACT(c["f"][:, 0:T], c["f"][:, 0:T], AF.Ln, [c["fk"]], [c["fk"]])
                for c in H2:
                    B, f = c["B"], c["f"]
                    S.op("dve", lambda e, B=B, f=f: e.tensor_tensor_scan(out=B[:, 0:T], data0=scanmask[:, 0:T], data1=f[:, 0:T],
                                                                      initial=0.0, op0=ALU.mult, op1=ALU.add),
                         [c["fk"], ("scanmask",)], [c["Bk"]], cost=0.12 + 2 * T / 960.0)
                for c in H2:
                    ACT(c["f"][:, 0:T], c["B"][:, 0:T], AF.Exp, [c["Bk"]], [c["fk"]])
                for c in H2:
                    ACT(c["B"][:, 0:T], c["B"][:, 0:T], AF.Exp, [c["Bk"]], [c["Bk"]], scale=-1.0)
                for c in H2:
                    h = c["h"]
                    Ev = c["f"][:, 0:T].rearrange("p (c j) -> p c j", j=seg)
                    CP("pool", c["eb"][:, 0:nsg], Ev[:, :, seg - 1], [c["fk"]], [("ebc", h % 4)])
                    q, qk_ = hq.pop(h)
                    QT, QTk = BP.alloc()
                    TT_("pool", QT[:, 0:T], q[:, 0:T], c["f"][:, 0:T], ALU.mult, [qk_, c["fk"]], [QTk])
                    FP.free((q, qk_))
                    c["QT"], c["QTk"] = QT, QTk
                for c in H2:
                    KT, KTk = BP.alloc()
                    TT_("dve", KT[:, 0:T], c["k1"][:, 0:T], c["B"][:, 0:T], ALU.mult, [c["k1k"], c["Bk"]], [KTk])
                    FP.free((c["k1"], c["k1k"]))
                    FP.free((c["B"], c["Bk"]))
                    c["KT"], c["KTk"] = KT, KTk
                for c in H2:
                    Ev = c["f"][:, 0:T].rearrange("p (c j) -> p c j", j=seg)
                    KE, KEk = BP.alloc()
                    TT_("pool", KE[:, 0:T].rearrange("p (c j) -> p c j", j=seg), c["KT"][:, 0:T].rearrange("p (c j) -> p c j", j=seg),
                        Ev[:, :, seg - 1:seg].broadcast_to([128, nsg, seg]), ALU.mult, [c["KTk"], c["fk"]], [KEk])
                    FP.free((c["f"], c["fk"]))
                    hctx[c["h"]] = (c["QT"], c["QTk"], c["KT"], c["KTk"], KE, KEk, c["eb"])

            def hg_b(h, l=l):
                QT, QTk, KT, KTk, KE, KEk, eb = hctx[h]
                hctx[h] = la_phase1(QT, QTk, KT, KTk, KE, KEk, 1.0, h * 128, 128,
                                    hgmask[:tb, 0:tb], ("hgmask",), seg, S_hg[l][:, h, :], ("S_hg", l, h),
                                    lambda si, eb=eb: eb[:, si:si + 1], [("ebc", h % 4)], h % 2)
                BP.free((KT, KTk))
                BP.free((KE, KEk))

            def hg_c(h):
                c = hctx[h]
                c["outs"], c["sqs"] = la_phase2(c)
                BP.free((c["QT"], c["qk"]))

            def hg_d(h):
                c = hctx.pop(h)
                postproc(c["outs"], c["sqs"], 8 + h, 128)

            def hg_a1(h):
                if h % 2 == 0:
                    hg_a(h // 2)

            pipeline(8, [hg_a1, hg_b, hg_c, hg_d])

            S.tag = "rglru"
            cidx = 0
            for gname, ncc in (("ry0", 4), ("ry1", 4), ("ry2", 2)):
                ws, wv = wget(l, gname)
                for cc in range(ncc):
                    ps, pk = proj_fm(wv, cc, [("wslot", ws)])
                    ACT(big[:, 16 + cidx, 0:T], ps[:, 0:T], AF.Gelu_apprx_tanh, [pk], [("big", 16 + cidx)])
                    PP.free((ps, pk))
                    cidx += 1
                wfree(ws)
            gws, _ = wget(l, "rgw")
            gwv = wslot[gws][:, 0:5120].rearrange("p (g n k e) -> p g n k e", g=2, n=5, k=2)
            ruw = {}
            rgx = {}

            def rg_a(n, l=l):
                for c in (2 * n, 2 * n + 1):
                    gi_, cc = c // 4, c % 4
                    if cc == 0:
                        ruw[gi_] = wget(l, f"ru{gi_}")
                    ws, wv = ruw[gi_]
                    ps, pk = proj_fm(wv, cc, [("wslot", ws)])
                    if c == 9 or cc == 3:
                        wfree(ws)
                    ub = ubuf[c % 2]
                    ubk = ("ubuf", c % 2)
                    CP("pool", ub[:, 0:3], uhalo[l][:, c, :], [("uhalo", l, c)], [ubk])
                    CP("dve", ub[:, 3:3 + T], ps[:, 0:T], [pk], [ubk])
                    PP.free((ps, pk))
                    CP("pool", uhalo[l][:, c, :], ub[:, T:T + 3], [ubk], [("uhalo", l, c)])
                    cw = smc(f"rcw{l}", 4 * c, 4)
                    t, tk = FP.alloc()
                    TS("dve", t[:, 0:T], ub[:, 0:T], cw[:, 0:1], smc(f"rcb{l}", c, 1), ALU.mult, ALU.add, [ubk, K_SM], [tk])
                    STT(t[:, 0:T], ub[:, 1:1 + T], cw[:, 1:2], t[:, 0:T], ALU.mult, ALU.add, [ubk, tk, K_SM], [tk])
                    STT(t[:, 0:T], ub[:, 2:2 + T], cw[:, 2:3], t[:, 0:T], ALU.mult, ALU.add, [ubk, tk, K_SM], [tk])
                    STT(t[:, 0:T], ub[:, 3:3 + T], cw[:, 3:4], t[:, 0:T], ALU.mult, ALU.add, [ubk, tk, K_SM], [tk])
                    xb_, xbk = BP.alloc()
                    CP("pool", xb_[:, 0:T], t[:, 0:T], [tk], [xbk])
                    rgx[c] = (t, tk, xb_, xbk)

            def rg_b(n, l=l):
                pend = [rgx.pop(2 * n), rgx.pop(2 * n + 1)]
                gps = []
                for ei in range(2):
                    rps, rpk = PP.alloc()
                    ips, ipk = PP.alloc()
                    for kc in range(2):
                        MM(rps[:, 0:T], gwv[:, 0, n, kc, ei * 128:(ei + 1) * 128], pend[kc][2][:, 0:T], kc == 0, kc == 1,
                           [("wslot", gws), pend[kc][3]], [rpk])
                    for kc in range(2):
                        MM(ips[:, 0:T], gwv[:, 1, n, kc, ei * 128:(ei + 1) * 128], pend[kc][2][:, 0:T], kc == 0, kc == 1,
                           [("wslot", gws), pend[kc][3]], [ipk])
                    gps.append((rps, rpk, ips, ipk))
                E2 = []
                for ei in range(2):
                    e_ = 2 * n + ei
                    xc, xck = pend[ei][0], pend[ei][1]
                    rps, rpk, ips, ipk = gps[ei]
                    r, rk_ = FP.alloc()
                    ig, igk = FP.alloc()
                    a, ak = FP.alloc()
                    E2.append(dict(e_=e_, xc=xc, xck=xck, rps=rps, rpk=rpk, ips=ips, ipk=ipk, r=r, rk=rk_, ig=ig, igk=igk, a=a, ak=ak))
                for c in E2:
                    ACT(c["r"][:, 0:T], c["rps"][:, 0:T], AF.Sigmoid, [c["rpk"], K_SM], [c["rk"]], bias=smc(f"rbr{l}", c["e_"], 1))
                    PP.free((c["rps"], c["rpk"]))
                for c in E2:
                    ACT(c["ig"][:, 0:T], c["ips"][:, 0:T], AF.Sigmoid, [c["ipk"], K_SM], [c["igk"]], bias=smc(f"rbi{l}", c["e_"], 1))
                    PP.free((c["ips"], c["ipk"]))
                for c in E2:
                    e_ = c["e_"]
                    ACT(c["a"][:, 0:T], c["r"][:, 0:T], AF.Exp, [c["rk"], ("rgc",)], [c["ak"]], scale=rgc[:, l, e_:e_ + 1])
                for c in E2:
                    STT(c["r"][:, 0:T], c["a"][:, 0:T], -1.0, c["a"][:, 0:T], ALU.mult, ALU.mult, [c["ak"]], [c["rk"]])
                for c in E2:
                    TT_("pool", c["ig"][:, 0:T], c["ig"][:, 0:T], c["xc"][:, 0:T], ALU.mult, [c["igk"], c["xck"]], [c["igk"]])
                for c in E2:
                    TS("dve", c["r"][:, 0:T], c["r"][:, 0:T], -1.0, None, ALU.max, None, [c["rk"]], [c["rk"]])
                for c in E2:
                    ACT(c["r"][:, 0:T], c["r"][:, 0:T], AF.Sqrt, [c["rk"]], [c["rk"]], bias=1.0, scale=1.0)
                    if grp == 0 and first:
                        MEMSET("pool", c["r"][:, 0:1], 1.0, [c["rk"]])
                for c in E2:
                    TT_("dve", c["ig"][:, 0:T], c["ig"][:, 0:T], c["r"][:, 0:T], ALU.mult, [c["igk"], c["rk"]], [c["igk"]])
                    FP.free((c["r"], c["rk"]))
                for c in E2:
                    e_ = c["e_"]
                    hk = ("hstate", l, e_)
                    xc, a, ig = c["xc"], c["a"], c["ig"]
                    S.op("dve", lambda e, xc=xc, a=a, ig=ig, e_=e_, l=l: e.tensor_tensor_scan(
                        out=xc[:, 0:T], data0=a[:, 0:T], data1=ig[:, 0:T], initial=hstate[l][:, e_:e_ + 1],
                        op0=ALU.mult, op1=ALU.add), [c["ak"], c["igk"], hk], [c["xck"]], cost=0.12 + 2 * T / 960.0)
                    FP.free((c["a"], c["ak"]))
                    FP.free((c["ig"], c["igk"]))
                for c in E2:
                    e_ = c["e_"]
                    hk = ("hstate", l, e_)
                    CP("pool", hstate[l][:, e_:e_ + 1], c["xc"][:, T - 1:T], [c["xck"]], [hk])
                    TT_("pool", big[:, 16 + e_, 0:T], c["xc"][:, 0:T], big[:, 16 + e_, 0:T], ALU.mult,
                        [c["xck"], ("big", 16 + e_)], [("big", 16 + e_)])
                for (xc, xck, xb2, xbk2) in pend:
                    FP.free((xc, xck))
                    BP.free((xb2, xbk2))

            pipeline(5, [rg_a, None, rg_b])
            wfree(gws)

            S.tag = "merge"
            V_ALL = [("V", b) for b in range(4)]
            bkc = [8, 8, 10]
            bk0 = [0, 8, 16]
            acc = [FP.alloc() for _ in range(8)]
            for b in range(3):
                for jh in range(2):
                    gs, gv = wget(l, f"mg{b}{jh}")
                    bs, bv = wget(l, f"wb{b}{jh}")
                    for cc in range(4):
                        j = jh * 4 + cc
                        gps, gpk = proj_fm(gv, cc, [("wslot", gs)])
                        sg, sgk = FP.alloc()
                        ACT(sg[:, 0:T], gps[:, 0:T], AF.Sigmoid, [gpk], [sgk])
                        PP.free((gps, gpk))
                        pps, ppk = PP.alloc()
                        for kc in range(bkc[b]):
                            MM(pps[:, 0:T], bv[:, kc, cc * 128:(cc + 1) * 128], big[:, bk0[b] + kc, 0:T], kc == 0, kc == bkc[b] - 1,
                               [("wslot", bs), ("big", bk0[b] + kc)], [ppk])
                        am, amk = acc[j]
                        if b == 0:
                            TT_("dve", am[:, 0:T], pps[:, 0:T], sg[:, 0:T], ALU.mult, [ppk, sgk], [amk])
                        else:
                            TT_("dve", sg[:, 0:T], pps[:, 0:T], sg[:, 0:T], ALU.mult, [ppk, sgk], [sgk])
                            if b == 1:
                                TT_("pool", am[:, 0:T], am[:, 0:T], sg[:, 0:T], ALU.add, [amk, sgk], [amk])
                            else:
                                mxv = Vt[:].rearrange("p b c -> p (b c)").rearrange("p (k t) -> p k t", t=TT)
                                TT_("pool", mxv[:, j, 0:T], am[:, 0:T], sg[:, 0:T], ALU.add, [amk, sgk] + V_ALL, V_ALL + [("mx", j)])
                        PP.free((pps, ppk))
                        FP.free((sg, sgk))
                    wfree(gs)
                    wfree(bs)
            for a_ in acc:
                FP.free(a_)
            mxv = Vt[:].rearrange("p b c -> p (b c)").rearrange("p (k t) -> p k t", t=TT)

            S.tag = "wout"
            for half in range(2):
                ws, wv = wget(l, f"wo{half}")
                for b in range(NB):
                    ps, pk = PP.alloc()
                    for kc in range(8):
                        MM(ps[:tb, 0:512], mxv[:, kc, b * tb:(b + 1) * tb], wv[:, kc, 0:512], kc == 0, kc == 7,
                           V_ALL + [("wslot", ws)], [pk])
                    TT_("dve", xt[:tb, b, half * 512:(half + 1) * 512], ps[:tb, 0:512], xt[:tb, b, half * 512:(half + 1) * 512],
                        ALU.add, [pk, ("x", b)], [("x", b)])
                    PP.free((ps, pk))
                wfree(ws)

            S.tag = "norm2"
            norm_to_hnT(l)
            S.tag = "ffn_up"
            for i in range(11):
                ws, wv = wget(l, f"wu{i}")
                for cc in range(2):
                    c = 2 * i + cc
                    aps, apk = proj_fm(wv, cc, [("wslot", ws)])
                    gps, gpk = proj_fm(wv, 2 + cc, [("wslot", ws)])
                    ab = abuf[c % 2]
                    abk = ("abuf", c % 2)
                    CP("pool", ab[:, 0:2], ahalo[l][:, c, :], [("ahalo", l, c)], [abk])
                    ACT(ab[:, 2:2 + T], aps[:, 0:T], AF.Copy, [apk], [abk])
                    CP("pool", ahalo[l][:, c, :], ab[:, T:T + 2], [abk], [("ahalo", l, c)])
                    cw = smc(f"fcw{l}", 3 * c, 3)
                    t, tk = FP.alloc()
                    ACT(t[:, 0:T], ab[:, 0:T], AF.Identity, [abk, K_SM], [tk], scale=cw[:, 0:1])
                    STT(t[:, 0:T], ab[:, 1:1 + T], cw[:, 1:2], t[:, 0:T], ALU.mult, ALU.add, [abk, tk, K_SM], [tk])
                    STT(t[:, 0:T], aps[:, 0:T], cw[:, 2:3], t[:, 0:T], ALU.mult, ALU.add, [apk, tk, K_SM], [tk])
                    PP.free((aps, apk))
                    ACT(t[:, 0:T], t[:, 0:T], AF.Gelu_apprx_tanh, [tk, K_SM], [tk], bias=smc(f"fcb{l}", c, 1))
                    TT_("dve", big[:, c, 0:T], gps[:, 0:T], t[:, 0:T], ALU.mult, [gpk, tk], [("big", c)])
                    PP.free((gps, gpk))
                    FP.free((t, tk))
                wfree(ws)
            S.tag = "ffn_down"
            kgs = [(0, 8), (8, 8), (16, 6)]
            accs = [PP.alloc() for _ in range(NB)]
            for kg, (k0, kn) in enumerate(kgs):
                ws, wv = wget(l, f"wd0{kg}")
                for b in range(NB):
                    ps, pk = accs[b]
                    for kc in range(kn):
                        MM(ps[:tb, 0:512], big[:, k0 + kc, b * tb:(b + 1) * tb], wv[:, kc, 0:512],
                           (kg == 0 and kc == 0), (kg == 2 and kc == kn - 1), [("big", k0 + kc), ("wslot", ws)], [pk])
                wfree(ws)
            for b in range(NB):
                ps, pk = accs[b]
                TT_("dve", xt[:tb, b, 0:512], ps[:tb, 0:512], xt[:tb, b, 0:512], ALU.add, [pk, ("x", b)], [("x", b)])
                PP.free((ps, pk))
            wds = [wget(l, f"wd1{kg}") for kg in range(3)]
            for b in range(NB):
                ps, pk = PP.alloc()
                for kg, (k0, kn) in enumerate(kgs):
                    ws, wv = wds[kg]
                    for kc in range(kn):
                        MM(ps[:tb, 0:512], big[:, k0 + kc, b * tb:(b + 1) * tb], wv[:, kc, 0:512],
                           (kg == 0 and kc == 0), (kg == 2 and kc == kn - 1), [("big", k0 + kc), ("wslot", ws)], [pk])
                TT_("dve", xt[:tb, b, 512:1024], ps[:tb, 0:512], xt[:tb, b, 512:1024], ALU.add, [pk, ("x", b)], [("x", b)])
                PP.free((ps, pk))
            for ws, wv in wds:
                wfree(ws)

        S.tag = "final"
        for b in range(NB):
            ACT(junk[:tb, :], xt[:tb, b, :], AF.Square, [("x", b)], [("junk",), ("stat", b)], accum=stat[:tb, b:b + 1])
            ACT(stat[:tb, 4 + b:5 + b], stat[:tb, b:b + 1], AF.Sqrt, [("stat", b), ("epsb",)], [("stat2", b)],
                bias=epsb[:tb, 0:1], scale=1.0 / D)
            S.op("dve", lambda e, b=b: e.reciprocal(out=stat[:tb, 8 + b:9 + b], in_=stat[:tb, 4 + b:5 + b]),
                 [("stat2", b)], [("stat3", b)], cost=0.15)
            STT(xt[:tb, b, :], xt[:tb, b, :], stat[:tb, 8 + b:9 + b], wfin[:tb, :], ALU.mult, ALU.mult,
                [("x", b), ("stat3", b), ("wfin",)], [("x", b)])
            DMA(ydst[b * tb:(b + 1) * tb, :], xt[:tb, b, :], [("x", b)], [], sem_y[b])

    ntiles_total = NTILE + 1
    for _ in range(ntiles_total):
        for l in range(DEPTH):
            for gidx in range(NG):
                wseq.append((l, gidx))

    def load_group_consts(grp):
        DMA(retmask[:], retmask_d[grp], (), [("rmask",)], sem_gc[0])
        DMA(qd[:], qd_d[grp], (), [("rtabq",)], sem_gc[1])
        DMA(kd[:], kd_d[grp], (), [("rtabk",)], sem_gc[2])
        DMA(hgmask[:], hgmask_d[grp], (), [("hgmask",)], sem_gc[3])
        DMA(scanmask[:], scanmask_d[grp], (), [("scanmask",)], sem_gc[4])

    def state_keys(l):
        return [("S_ret", l, h) for h in range(4)], [("S_hg", l, h) for h in range(8)], \
               [("hstate", l, e) for e in range(10)], [("uhalo", l, c) for c in range(10)], [("ahalo", l, c) for c in range(22)]

    def store_states(grp):
        for l in range(DEPTH):
            kr, kh, ks, ku, ka = state_keys(l)
            DMA(ret_o[grp, l].rearrange("h k v -> k h v"), S_ret[l][:], kr, [], sem_sto[2 * l])
            DMA(hg_o[grp, l].rearrange("h k v -> k h v"), S_hg[l][:], kh, [], sem_sto[2 * l + 1])
            o0 = (grp * DEPTH + l) * OUT_W
            CP("pool", outst[:, o0:o0 + 10], hstate[l][:], ks, [("outst",)])
            CP("pool", outst[:, o0 + 10:o0 + 40], uhalo[l][:].rearrange("p c j -> p (c j)"), ku, [("outst",)])
            CP("pool", outst[:, o0 + 40:o0 + 84], ahalo[l][:].rearrange("p c j -> p (c j)"), ka, [("outst",)])

    load_group_consts(0)
    for l in range(DEPTH):
        kr, kh, ks, ku, ka = state_keys(l)
        MEMSET("pool", S_ret[l][:], 0.0, kr)
        MEMSET("pool", S_hg[l][:], 0.0, kh)
        MEMSET("pool", hstate[l][:], 0.0, ks)
        MEMSET("pool", uhalo[l][:], 0.0, ku)
        MEMSET("pool", ahalo[l][:], 0.0, ka)
    for ti in range(NTILE):
        emit_tile(0, ti, TT, x_p[ti * TT:(ti + 1) * TT, :], y_p[ti * TT:(ti + 1) * TT, :])
    store_states(0)
    load_group_consts(1)
    for l in range(DEPTH):
        kr, kh, ks, ku, ka = state_keys(l)
        DMA(S_ret[l][:], st_ret[l].rearrange("h k v -> k h v"), (), kr, sem_sti[2 * l])
        DMA(S_hg[l][:], st_hg[l].rearrange("h k v -> k h v"), (), kh, sem_sti[2 * l + 1])
        CP("pool", hstate[l][:], smc(f"h0{l}"), [K_SM], ks)
        CP("pool", uhalo[l][:].rearrange("p c j -> p (c j)"), smc(f"u0{l}"), [K_SM], ku)
        CP("pool", ahalo[l][:].rearrange("p c j -> p (c j)"), smc(f"a0{l}"), [K_SM], ka)
    emit_tile(1, 0, DSEQ, x_s, y_s)
    store_states(1)
    DMA(small_o[:, :], outst[:], [("outst",)], [], sem_out)

    build_program.last_sched = S
    S.reorder()
    block = es.enter_context(nc.Block())
    S.finalize(nc, engsems, block)
    es.close()
    return nc


def _fm(v, nch):
    return np.ascontiguousarray(np.asarray(v, np.float32).reshape(nch, 128).T)


def _consts(SEQ):
    c = {}
    c["ident"] = np.eye(128, dtype=np.float32)
    P = np.zeros((128, 128), np.float32)
    for p in range(64):
        P[p + 64, p] = -1.0
    for p in range(64, 128):
        P[p - 64, p] = 1.0
    c["permT"] = P
    half = 64
    inv = np.power(np.float32(10000.0), -np.arange(half, dtype=np.float32) / np.float32(half)).astype(np.float32)
    inv2 = np.concatenate([inv, inv])

    def rot(pos0, n):
        pos = (np.arange(n, dtype=np.float32) + np.float32(pos0)).astype(np.float32)
        ang = (inv2[:, None] * pos[None, :]).astype(np.float32)
        return np.cos(ang).astype(np.float32), np.sin(ang).astype(np.float32)

    c["rotc_p"], c["rots_p"] = rot(0, SEQ)
    c["rotc_s"], c["rots_s"] = rot(PAST, DSEQ)
    lg = np.log1p(-np.power(2.0, -5.0 - np.arange(4, dtype=np.float64)))
    retmask = np.zeros((2, 128, 4, 128), np.float32)
    qd = np.zeros((2, 128, 4, 128), np.float32)
    kd = np.zeros((2, 128, 4, 128), np.float32)
    for grp, L in ((0, 128), (1, DSEQ)):
        n = np.arange(L)
        for h in range(4):
            qd[grp, :, h, :L] = np.exp(lg[h] * (n + 1.0))[None, :]
            kd[grp, :, h, :L] = (np.exp(-lg[h] * (n + 1.0)) * (128 ** -0.5))[None, :]
            m = n[:, None]
            t = n[None, :]
            samechunk = (m // 64) == (t // 64)
            Dm = np.where(samechunk, np.where(t >= m, 1.0, np.exp(lg[h] * 2.0 * (m - t))), np.where(t > m, 1.0, 0.0))
            retmask[grp, :L, h, :L] = Dm
    c["retmask"], c["qd"], c["kd"] = retmask, qd, kd
    hgmask = np.zeros((2, 128, 128), np.float32)
    n = np.arange(128)
    hgmask[0] = (((n[:, None] // 64) == (n[None, :] // 64)) & (n[:, None] <= n[None, :])).astype(np.float32)
    hgmask[1, :DSEQ, :DSEQ] = (n[:DSEQ, None] <= n[None, :DSEQ]).astype(np.float32)
    c["hgmask"] = hgmask
    sm = np.ones((2, 128, TT), np.float32)
    sm[0, :, 0::64] = 0.0
    sm[1, :, 0] = 0.0
    c["scanmask"] = sm
    return c


_CACHE = {}


def kernel(x_prompt, x_sample, state_ret, state_hgrn, state_rglru, cache_rg_conv, cache_ffn_conv,
           norm1_w, w_in, w_branch, w_out, rg_conv_w, rg_conv_b, rg_w_r, rg_b_r, rg_w_i, rg_b_i,
           rg_lambda, hg_lb, hg_norm_w, norm2_w, w_up, ffn_conv_w, ffn_conv_b, w_down, final_norm_w):
    f32 = np.float32
    x_prompt = np.asarray(x_prompt, f32)
    B, SEQ, _ = x_prompt.shape
    ncore = 8
    assert B == ncore
    if SEQ not in _CACHE:
        _CACHE[SEQ] = (build_program(SEQ), _consts(SEQ))
    nc, consts = _CACHE[SEQ]

    shared = {
        "w_in": np.ascontiguousarray(w_in, f32), "w_branch": np.ascontiguousarray(w_branch, f32),
        "w_out": np.ascontiguousarray(w_out, f32), "w_up": np.ascontiguousarray(w_up, f32),
        "w_down": np.ascontiguousarray(w_down, f32), "rg_w_r": np.ascontiguousarray(rg_w_r, f32),
        "rg_w_i": np.ascontiguousarray(rg_w_i, f32),
        "wfin": np.ascontiguousarray(np.broadcast_to(np.asarray(final_norm_w, f32)[None, :], (128, D))),
    }
    shared.update(consts)
    in_maps = []
    for b in range(ncore):
        sm = np.zeros((128, SM_N), f32)

        def put(name, arr):
            o, w = SM_OFF[name]
            sm[:, o:o + w] = np.asarray(arr, f32).reshape(128, w)

        put("lb0", _fm(hg_lb[0], 8))
        put("lb1", _fm(hg_lb[1], 8))
        for l in range(DEPTH):
            put(f"rcw{l}", np.asarray(rg_conv_w[l], f32).reshape(4, 10, 128).transpose(2, 1, 0))
            put(f"rcb{l}", _fm(rg_conv_b[l], 10))
            put(f"rbr{l}", _fm(rg_b_r[l], 10))
            put(f"rbi{l}", _fm(rg_b_i[l], 10))
            put(f"rlam{l}", _fm(rg_lambda[l], 10))
            put(f"fcw{l}", np.asarray(ffn_conv_w[l], f32).reshape(3, 22, 128).transpose(2, 1, 0))
            put(f"fcb{l}", _fm(ffn_conv_b[l], 22))
            put(f"n1{l}", _fm(norm1_w[l], 8))
            put(f"n2{l}", _fm(norm2_w[l], 8))
            put(f"hgn{l}", _fm(hg_norm_w[l], 8))
            put(f"h0{l}", _fm(state_rglru[l, b], 10))
            put(f"u0{l}", np.asarray(cache_rg_conv[l, b], f32).reshape(3, 10, 128).transpose(2, 1, 0))
            put(f"a0{l}", np.asarray(cache_ffn_conv[l, b], f32).reshape(2, 22, 128).transpose(2, 1, 0))
        m = dict(shared)
        m["x_p"] = np.ascontiguousarray(x_prompt[b])
        m["x_s"] = np.ascontiguousarray(x_sample[b], f32)
        m["st_ret"] = np.ascontiguousarray(state_ret[:, b], f32)
        m["st_hg"] = np.ascontiguousarray(state_hgrn[:, b], f32)
        m["smalls"] = sm
        in_maps.append(m)

    res = run_bass_kernel_spmd(nc, in_maps, core_ids=list(range(ncore)))
    R = res.results
    y_p = np.stack([R[b]["y_p"] for b in range(ncore)], 0)
    y_s = np.stack([R[b]["y_s"] for b in range(ncore)], 0)
    ret = np.stack([R[b]["ret_o"] for b in range(ncore)], 0)
    hg = np.stack([R[b]["hg_o"] for b in range(ncore)], 0)
    so = np.stack([R[b]["small_o"] for b in range(ncore)], 0)
    so = so.reshape(ncore, 128, 2, DEPTH, OUT_W)
    outs = [y_p.astype(f32), y_s.astype(f32)]
    outs.append(np.ascontiguousarray(ret[:, 0].transpose(1, 0, 2, 3, 4)))
    outs.append(np.ascontiguousarray(ret[:, 1].transpose(1, 0, 2, 3, 4)))
    outs.append(np.ascontiguousarray(hg[:, 0].transpose(1, 0, 2, 3, 4)))
    outs.append(np.ascontiguousarray(hg[:, 1].transpose(1, 0, 2, 3, 4)))
    for grp in range(2):
        pass
    hs = so[..., 0:10]
    uh = so[..., 10:40].reshape(ncore, 128, 2, DEPTH, 10, 3)
    ah = so[..., 40:84].reshape(ncore, 128, 2, DEPTH, 22, 2)
    for grp in range(2):
        pass
    rgl = [np.ascontiguousarray(hs[:, :, g].transpose(2, 0, 3, 1).reshape(DEPTH, ncore, RGW)) for g in range(2)]
    rgc_ = [np.ascontiguousarray(uh[:, :, g].transpose(2, 0, 4, 3, 1).reshape(DEPTH, ncore, 3, RGW)) for g in range(2)]
    ffc = [np.ascontiguousarray(ah[:, :, g].transpose(2, 0, 4, 3, 1).reshape(DEPTH, ncore, 2, DFF)) for g in range(2)]
    outs += [rgl[0], rgl[1], rgc_[0], rgc_[1], ffc[0], ffc[1]]
    return tuple(o.astype(f32) for o in outs)
```
